# Optimizing a Trainium2 kernel written in Bass

```python
import math
import jax, jax.numpy as jnp
from jax import lax
import numpy as np

D_MODEL = 2048
BATCH = 4
SEQ = 2048
DEPTH = 4

GRID_W = 64
CTX_LEN = 256
N_BRANCH = 3
ATT_HEADS = D_MODEL // 256
ATT_HEAD_DIM = 64
ATT_V_DIM = 2 * ATT_HEAD_DIM
ATT_WIDTH = ATT_HEADS * 2 * ATT_HEAD_DIM
ROPE_BASE = 10000.0
Q_BLOCK = 128
HY_WIDTH = D_MODEL // 4
HY_ORDER = 2
HY_BANDS = 16
HY_EMB_DIM = 1 + 2 * HY_BANDS
HY_FILTER_DIM = 64
HY_MIN_DECAY = math.log(1e-2) / 1.5
HY_MAX_DECAY = math.log(1e-2) / 0.3
POOL_WINDOWS = (2, 4, 8, 16)
POOL_GROUPS = len(POOL_WINDOWS)
POOL_WIDTH = D_MODEL // 4
POOL_GROUP_DIM = POOL_WIDTH // POOL_GROUPS
IN_SPLITS = (ATT_WIDTH, 2 * ATT_WIDTH, 3 * ATT_WIDTH, 3 * ATT_WIDTH + 3 * HY_WIDTH,
             3 * ATT_WIDTH + 3 * HY_WIDTH + POOL_WIDTH)
IN_WIDTH = 3 * ATT_WIDTH + 3 * HY_WIDTH + POOL_WIDTH + N_BRANCH * D_MODEL
FF_DIM = 5632
EPS = 1e-6

kernel_name = "hybrid_diffattn_hyena_pool_dit"

F32 = jnp.float32


def rmsnorm(x, g):
    xf = x.astype(F32)
    y = xf * lax.rsqrt(jnp.mean(xf * xf, axis=-1, keepdims=True) + EPS)
    return (y * g.astype(F32)).astype(x.dtype)


def dwconv3(x, w):
    xp = jnp.pad(x, ((0, 0), (1, 1), (0, 0)))
    return xp[:, :-2] * w[0] + xp[:, 1:-1] * w[1] + xp[:, 2:] * w[2]


def axial_rope(rows):
    row = jnp.repeat(jnp.arange(rows, dtype=F32), GRID_W)
    col = jnp.tile(jnp.arange(GRID_W, dtype=F32), rows)
    n_freq = ATT_HEAD_DIM // 4
    inv = ROPE_BASE ** (-jnp.arange(n_freq, dtype=F32) / n_freq)
    ang = jnp.stack([row[:, None] * inv, col[:, None] * inv], axis=1)
    return jnp.cos(ang), jnp.sin(ang)


def apply_rope(t, cos, sin):
    B, L, H, M, _ = t.shape
    tr = t.astype(F32).reshape(B, L, H, M, 2, 2, ATT_HEAD_DIM // 4)
    c = cos[None, :, None, None]
    s = sin[None, :, None, None]
    t1, t2 = tr[..., 0, :], tr[..., 1, :]
    out = jnp.stack([t1 * c - t2 * s, t2 * c + t1 * s], axis=-2)
    return out.reshape(t.shape).astype(t.dtype)


def heads_qk(t):
    B, L, _ = t.shape
    return t.reshape(B, L, ATT_HEADS, 2, ATT_HEAD_DIM)


def heads_v(t):
    B, L, _ = t.shape
    return t.reshape(B, L, ATT_HEADS, ATT_V_DIM)


def diff_lambda(lam_p, layer_idx):
    lam_init = 0.8 - 0.6 * math.exp(-0.3 * layer_idx)
    lp = lam_p.astype(F32)
    lam = jnp.exp(jnp.sum(lp[0] * lp[1])) - jnp.exp(jnp.sum(lp[2] * lp[3])) + lam_init
    return lam, lam_init


def diff_attend(q, k, v, lam):
    s = jnp.einsum('bqhmd,bkhmd->bhmqk', q, k).astype(F32) * (ATT_HEAD_DIM ** -0.5)
    p = jax.nn.softmax(s, axis=-1).astype(v.dtype)
    o = jnp.einsum('bhmqk,bkhe->bqhme', p, v)
    return o[:, :, :, 0] - lam.astype(o.dtype) * o[:, :, :, 1]


def blocked_diff_attention(q, k, v, lam):
    B, L = q.shape[0], q.shape[1]
    nb = L // Q_BLOCK
    qb = jnp.moveaxis(q.reshape(B, nb, Q_BLOCK, ATT_HEADS, 2, ATT_HEAD_DIM), 1, 0)
    ob = lax.map(lambda qq: diff_attend(qq, k, v, lam), qb)
    return jnp.moveaxis(ob, 0, 1).reshape(B, L, ATT_HEADS, ATT_V_DIM)


def hyena_filters(L, w1, b1, w2, b2, w3, freq):
    t = jnp.linspace(0.0, 1.0, L, dtype=F32)[:, None]
    w_ang = 2.0 * math.pi * jnp.arange(L, dtype=F32)[:, None] / L
    f = jnp.linspace(1e-4, HY_BANDS - 1, HY_BANDS, dtype=F32)[None, :]
    z = jnp.concatenate([t, jnp.cos(f * w_ang), -jnp.sin(f * w_ang)], axis=-1)
    fr = freq.astype(F32)
    h = jnp.sin(fr[0] * (z @ w1.astype(F32) + b1.astype(F32)))
    h = jnp.sin(fr[1] * (h @ w2.astype(F32) + b2.astype(F32)))
    h = (h @ w3.astype(F32)).reshape(L, 2, HY_ORDER, HY_WIDTH)
    deltas = jnp.abs(jnp.linspace(HY_MIN_DECAY, HY_MAX_DECAY, HY_WIDTH, dtype=F32))
    h = h * jnp.exp(-t * deltas[None, :])[:, None, None, :]
    fwd, bwd = h[:, 0], h[:, 1]
    k = jnp.concatenate([fwd, jnp.zeros((1, HY_ORDER, HY_WIDTH), F32), bwd[:0:-1]], axis=0)
    return k / (jnp.sum(jnp.abs(k), axis=0, keepdims=True) + EPS)


def fft_longconv(u, k, bias):
    L = u.shape[1]
    uf = jnp.fft.rfft(u.astype(F32), n=2 * L, axis=1)
    kf = jnp.fft.rfft(k, n=2 * L, axis=0)
    y = jnp.fft.irfft(uf * kf[None], n=2 * L, axis=1)[:, :L]
    return (y + u.astype(F32) * bias.astype(F32)).astype(u.dtype)


def hyena_mix(hy, short_w, filt, bias):
    hy = dwconv3(hy, short_w)
    v, x1, x2 = jnp.split(hy, 3, axis=-1)
    z = x1 * fft_longconv(v, filt[:, 0], bias[0])
    return x2 * fft_longconv(z, filt[:, 1], bias[1])


def pool_mix(p, pool_w, pool_scale):
    B, L, _ = p.shape
    pg = p.reshape(B, L, POOL_GROUPS, POOL_GROUP_DIM)
    t = jnp.arange(L)
    outs = []
    for g, win in enumerate(POOL_WINDOWS):
        xg = pg[:, :, g].astype(F32)
        cs = jnp.pad(jnp.cumsum(xg, axis=1), ((0, 0), (1, 0), (0, 0)))
        lo = jnp.clip(t - win // 2, 0, L)
        hi = jnp.clip(t + win - win // 2, 0, L)
        mean = (cs[:, hi] - cs[:, lo]) / (hi - lo).astype(F32)[None, :, None]
        outs.append((mean - xg).astype(p.dtype) @ pool_w[g])
    return jnp.concatenate(outs, axis=-1) * pool_scale


def mixer_branches(q, k_all, v_all, hy, pool, gates, lam, lam_init, filt, subln_g, hy_short_w, hy_bias,
                   pool_w, pool_scale, w_att_o, w_hy_o, w_pool_o, w_out):
    B, L = hy.shape[0], hy.shape[1]
    att = blocked_diff_attention(q, k_all, v_all, lam)
    att = (rmsnorm(att, subln_g) * (1.0 - lam_init)).reshape(B, L, ATT_WIDTH)
    hyo = hyena_mix(hy, hy_short_w, filt, hy_bias)
    poo = pool_mix(pool, pool_w, pool_scale)
    g_a, g_h, g_p = jnp.split(jax.nn.sigmoid(gates), N_BRANCH, axis=-1)
    merged = g_a * (att @ w_att_o) + g_h * (hyo @ w_hy_o) + g_p * (poo @ w_pool_o)
    return merged @ w_out


def conv_ffn(h, w_up, conv_w, w_down):
    gate, val = jnp.split(h @ w_up, 2, axis=-1)
    return (jax.nn.gelu(dwconv3(gate, conv_w)) * val) @ w_down


def setup_inputs(seed: int = 0) -> dict:
    key = jax.random.key(seed)
    ks = jax.random.split(key, 32)

    def nrm(k, shape, fan_in):
        return jax.random.normal(k, shape, F32) * (fan_in ** -0.5)

    def near_one(k, shape):
        return 1.0 + 0.02 * jax.random.normal(k, shape, F32)

    return {
        "x": jax.random.normal(ks[0], (BATCH, SEQ, D_MODEL), F32),
        "c": jax.random.normal(ks[1], (BATCH, D_MODEL), F32),
        "ctx": jax.random.normal(ks[2], (BATCH, CTX_LEN, D_MODEL), F32),
        "c_ctx": jax.random.normal(ks[3], (D_MODEL,), F32),
        "w_ada": nrm(ks[4], (DEPTH, D_MODEL, 6 * D_MODEL), D_MODEL),
        "b_ada": 0.02 * jax.random.normal(ks[5], (DEPTH, 6 * D_MODEL), F32),
        "norm_g": near_one(ks[6], (DEPTH, 4, D_MODEL)),
        "w_in": nrm(ks[7], (DEPTH, D_MODEL, IN_WIDTH), D_MODEL),
        "diff_lam": 0.1 * jax.random.normal(ks[8], (DEPTH, 4, ATT_HEAD_DIM), F32),
        "attn_subln_g": near_one(ks[9], (DEPTH, ATT_V_DIM)),
        "hy_short_w": nrm(ks[10], (DEPTH, 3, 3 * HY_WIDTH), 3),
        "hy_ffn_w1": nrm(ks[11], (DEPTH, HY_EMB_DIM, HY_FILTER_DIM), HY_EMB_DIM),
        "hy_ffn_b1": 0.1 * jax.random.normal(ks[12], (DEPTH, HY_FILTER_DIM), F32),
        "hy_ffn_w2": nrm(ks[13], (DEPTH, HY_FILTER_DIM, HY_FILTER_DIM), HY_FILTER_DIM),
        "hy_ffn_b2": 0.1 * jax.random.normal(ks[14], (DEPTH, HY_FILTER_DIM), F32),
        "hy_ffn_w3": nrm(ks[15], (DEPTH, HY_FILTER_DIM, 2 * HY_ORDER * HY_WIDTH), HY_FILTER_DIM),
        "hy_freq": near_one(ks[16], (DEPTH, 2, HY_FILTER_DIM)),
        "hy_bias": 0.5 * jax.random.normal(ks[17], (DEPTH, HY_ORDER, HY_WIDTH), F32),
        "pool_w": nrm(ks[18], (DEPTH, POOL_GROUPS, POOL_GROUP_DIM, POOL_GROUP_DIM), POOL_GROUP_DIM),
        "pool_scale": near_one(ks[19], (DEPTH, POOL_WIDTH)),
        "w_att_o": nrm(ks[20], (DEPTH, ATT_WIDTH, D_MODEL), ATT_WIDTH),
        "w_hy_o": nrm(ks[21], (DEPTH, HY_WIDTH, D_MODEL), HY_WIDTH),
        "w_pool_o": nrm(ks[22], (DEPTH, POOL_WIDTH, D_MODEL), POOL_WIDTH),
        "w_out": nrm(ks[23], (DEPTH, D_MODEL, D_MODEL), D_MODEL),
        "w_up": nrm(ks[24], (DEPTH, D_MODEL, 2 * FF_DIM), D_MODEL),
        "ff_conv_w": nrm(ks[25], (DEPTH, 3, FF_DIM), 3),
        "w_down": nrm(ks[26], (DEPTH, FF_DIM, D_MODEL), FF_DIM),
    }


def reference(x, c, ctx, c_ctx, w_ada, b_ada, norm_g, w_in, diff_lam, attn_subln_g, hy_short_w,
              hy_ffn_w1, hy_ffn_b1, hy_ffn_w2, hy_ffn_b2, hy_ffn_w3, hy_freq, hy_bias, pool_w, pool_scale,
              w_att_o, w_hy_o, w_pool_o, w_out, w_up, ff_conv_w, w_down):
    L = x.shape[1]
    LC = ctx.shape[1]
    ROWS = L // GRID_W
    cos, sin = axial_rope(ROWS)
    silu_c = jax.nn.silu(c)
    silu_cc = jax.nn.silu(c_ctx)
    for l in range(DEPTH):
        last = l == DEPTH - 1
        m_x = [m[:, None, :] for m in jnp.split(silu_c @ w_ada[l] + b_ada[l], 6, axis=-1)]
        m_c = jnp.split(silu_cc @ w_ada[l] + b_ada[l], 6, axis=-1)
        lam, lam_init = diff_lambda(diff_lam[l], l)
        filt_x = hyena_filters(L, hy_ffn_w1[l], hy_ffn_b1[l], hy_ffn_w2[l], hy_ffn_b2[l], hy_ffn_w3[l], hy_freq[l])

        hx = rmsnorm(x, norm_g[l, 0]) * (1.0 + m_x[1]) + m_x[0]
        hc = rmsnorm(ctx, norm_g[l, 0]) * (1.0 + m_c[1]) + m_c[0]
        if last:
            k_c, v_c = jnp.split(hc @ w_in[l][:, ATT_WIDTH:3 * ATT_WIDTH], 2, axis=-1)
        else:
            q_c, k_c, v_c, hy_c, pool_c, gate_c = jnp.split(hc @ w_in[l], IN_SPLITS, axis=-1)
        k_c = heads_qk(k_c)
        v_c = heads_v(v_c)

        q_x, k_x, v_x, hy_x, pool_x, gate_x = jnp.split(hx @ w_in[l], IN_SPLITS, axis=-1)
        q_x = apply_rope(heads_qk(q_x), cos, sin)
        k_x = apply_rope(heads_qk(k_x), cos, sin)
        k_all = jnp.concatenate([k_x, k_c], axis=1)
        v_all = jnp.concatenate([heads_v(v_x), v_c], axis=1)
        mix_x = mixer_branches(q_x, k_all, v_all, hy_x, pool_x, gate_x, lam, lam_init, filt_x,
                               attn_subln_g[l], hy_short_w[l], hy_bias[l], pool_w[l], pool_scale[l],
                               w_att_o[l], w_hy_o[l], w_pool_o[l], w_out[l])
        x = x + m_x[2] * rmsnorm(mix_x, norm_g[l, 1])

        h2 = rmsnorm(x, norm_g[l, 2]) * (1.0 + m_x[4]) + m_x[3]
        x = x + m_x[5] * rmsnorm(conv_ffn(h2, w_up[l], ff_conv_w[l], w_down[l]), norm_g[l, 3])

        if not last:
            filt_c = hyena_filters(LC, hy_ffn_w1[l], hy_ffn_b1[l], hy_ffn_w2[l], hy_ffn_b2[l], hy_ffn_w3[l], hy_freq[l])
            mix_c = mixer_branches(heads_qk(q_c), k_c, v_c, hy_c, pool_c, gate_c, lam, lam_init, filt_c,
                                   attn_subln_g[l], hy_short_w[l], hy_bias[l], pool_w[l], pool_scale[l],
                                   w_att_o[l], w_hy_o[l], w_pool_o[l], w_out[l])
            ctx = ctx + m_c[2] * rmsnorm(mix_c, norm_g[l, 1])
            hc2 = rmsnorm(ctx, norm_g[l, 2]) * (1.0 + m_c[4]) + m_c[3]
            ctx = ctx + m_c[5] * rmsnorm(conv_ffn(hc2, w_up[l], ff_conv_w[l], w_down[l]), norm_g[l, 3])
    return x
```

```python
import numpy as np
import concourse.bass as bass
import concourse.mybir as mybir

F32 = mybir.dt.float32
BF16 = mybir.dt.bfloat16
AF = mybir.ActivationFunctionType
ALU = mybir.AluOpType
AX = mybir.AxisListType

ENGS = ("sp", "act", "dve", "pool", "pe")
NSLOT = 12
SEM_CAP = 30000


class _Op:
    __slots__ = ("fn", "deps", "dma", "signal", "sigcount", "slot", "val", "prev", "semidx")

    def __init__(self, fn, deps, dma):
        self.fn = fn
        self.deps = deps
        self.dma = dma
        self.signal = False
        self.sigcount = 0
        self.slot = 0
        self.val = 0
        self.prev = None
        self.semidx = 0


def _key(x):
    if isinstance(x, tuple):
        return tuple(_key(y) for y in x)
    if isinstance(x, (str, int)):
        return x
    return ("id", id(x))


class Prog:
    def __init__(self, same_engine_sync=True):
        self.nc = bass.Bass("TRN2", target_bir_lowering=False)
        self.ops = {e: [] for e in ENGS}
        self.lastw = {}
        self.readers = {}
        self.same = same_engine_sync
        self.sb_off = 16384
        self.sb_hw = 0
        self.ndma = {e: 0 for e in ENGS}
        self.slot_last = {}
        self._n = 0

    def sb(self, shape, dt, name=None):
        self._n += 1
        name = name or f"sb{self._n}"
        esz = 4 if dt == F32 else 2
        if dt in (mybir.dt.int32, mybir.dt.uint32):
            esz = 4
        per_part = int(np.prod(shape[1:])) * esz
        per_part = (per_part + 63) // 64 * 64
        t = self.nc.alloc_sbuf_tensor_at(f"{name}_{self._n}", list(shape), dt, offset=self.sb_off)
        self.sb_off += per_part
        self.sb_hw = max(self.sb_hw, self.sb_off)
        assert self.sb_off <= 16384 + 212000, f"SBUF overflow {self.sb_off}"
        return t

    def sb_mark(self):
        return self.sb_off

    def sb_reset(self, mark):
        self.sb_off = mark

    def dram(self, name, shape, dt, kind="Internal"):
        return self.nc.dram_tensor(name, list(shape), dt, kind=kind)

    def op(self, eng, fn, reads=(), writes=(), dma=False):
        reads = [_key(r) for r in reads]
        writes = [_key(w) for w in writes]
        deps = set()
        for r in reads:
            lw = self.lastw.get(r)
            if lw is not None:
                deps.add(lw)
        for w in writes:
            lw = self.lastw.get(w)
            if lw is not None:
                deps.add(lw)
            for rd in self.readers.get(w, ()):
                deps.add(rd)
        idx = len(self.ops[eng])
        me = (eng, idx)
        o = _Op(fn, None, dma)
        fdeps = []
        for d in deps:
            if d == me:
                continue
            tgt = self.ops[d[0]][d[1]]
            if d[0] == eng and not tgt.dma:
                if eng == "pe" or not self.same:
                    continue
            fdeps.append(d)
        o.deps = fdeps
        if dma:
            j = self.ndma[eng]
            self.ndma[eng] += 1
            o.slot = j % NSLOT
            o.val = 16 * (j // NSLOT + 1)
            o.prev = self.slot_last.get((eng, o.slot))
            self.slot_last[(eng, o.slot)] = me
        self.ops[eng].append(o)
        for w in writes:
            self.lastw[w] = me
            self.readers[w] = []
        for r in reads:
            self.readers.setdefault(r, []).append(me)
        return me

    def barrier(self):
        lasts = []
        for e in ENGS:
            if self.ops[e]:
                for i in range(len(self.ops[e]) - 1, -1, -1):
                    if not self.ops[e][i].dma and self.ops[e][i].fn is not None:
                        lasts.append((e, i))
                        break
            for s in range(NSLOT):
                l = self.slot_last.get((e, s))
                if l is not None:
                    lasts.append(l)
        self._barrier_deps = lasts
        for e in ENGS:
            o = _Op(None, [d for d in lasts if not (d[0] == e and not self.ops[d[0]][d[1]].dma and e == "pe")], False)
            self.ops[e].append(o)
        self.lastw = {}
        self.readers = {}

    def emit(self):
        nc = self.nc
        for e in ENGS:
            for o in self.ops[e]:
                for (e2, i2) in o.deps:
                    t = self.ops[e2][i2]
                    if not t.dma:
                        t.signal = True
        nsem = {}
        for e in ENGS:
            c = 0
            for o in self.ops[e]:
                if o.dma or o.fn is None:
                    continue
                if o.signal:
                    c += 1
                    o.semidx = (c - 1) // SEM_CAP
                    o.sigcount = c - o.semidx * SEM_CAP
            nsem[e] = max(1, (c + SEM_CAP - 1) // SEM_CAP)
        from contextlib import ExitStack
        with ExitStack() as es:
            csem = {e: [es.enter_context(nc.semaphore(f"c_{e}_{k}")) for k in range(nsem[e])] for e in ENGS}
            dsem = {e: [es.enter_context(nc.semaphore(f"d_{e}_{s}")) for s in range(NSLOT)]
                    for e in ENGS if self.ndma[e] > 0}
            block = es.enter_context(nc.Block())
            ops = self.ops

            def replay(e, eng):
                waited = {}

                def wait_for(d):
                    t = ops[d[0]][d[1]]
                    if t.dma:
                        key = ("d", d[0], t.slot)
                        sem = dsem[d[0]][t.slot]
                        val = t.val
                    else:
                        if t.fn is None:
                            return
                        key = ("c", d[0], t.semidx)
                        sem = csem[d[0]][t.semidx]
                        val = t.sigcount
                    if waited.get(key, 0) < val:
                        eng.wait_ge(sem, val)
                        waited[key] = val

                for o in ops[e]:
                    for d in o.deps:
                        wait_for(d)
                    if o.fn is None:
                        continue
                    if o.dma:
                        if o.prev is not None:
                            wait_for(o.prev)
                        inst = o.fn(eng)
                        inst.then_inc(dsem[e][o.slot], 16)
                    else:
                        inst = o.fn(eng)
                        if o.signal:
                            inst.then_inc(csem[e][o.semidx], 1)

            @block.sync
            def _(eng):
                replay("sp", eng)

            @block.scalar
            def _(eng):
                replay("act", eng)

            @block.vector
            def _(eng):
                replay("dve", eng)

            @block.gpsimd
            def _(eng):
                replay("pool", eng)

            @block.tensor
            def _(eng):
                replay("pe", eng)
        return nc


import math
import ml_dtypes
from concourse.bass_utils import run_bass_kernel_spmd

D = 2048
KC = 16
LX = 2048
LCX = 256
T = LX + LCX
NT = T // 128
TB = [(0, 512), (512, 512), (1024, 512), (1536, 512), (2048, 256)]
PW = T + 4
IN_W = 11264
FF = 5632
FJ = FF // 128
DEPTH = 4
EPS = 1e-6
NCORE = 4
TWO_PI = 2.0 * math.pi


def pidx(t):
    return t + 1 if t < LX else t + 3


class B:
    def __init__(self, dump=()):
        self.P = Prog()
        self.nc = self.P.nc
        self.dump = set(dump)
        nc = self.nc
        self.ps = [nc.alloc_psum_tensor(f"psb{i}", [128, 512], F32) for i in range(8)]
        self.rot = 0
        self.ident = None

    def din(self, name, shape, dt=F32):
        return self.P.dram(name, shape, dt, kind="ExternalInput").ap()

    def dsc(self, name, shape, dt=F32):
        kind = "ExternalOutput" if name in self.dump else "Internal"
        return self.P.dram(name, shape, dt, kind=kind).ap()

    def gbank(self):
        b = self.ps[self.rot % 4]
        self.rot += 1
        return b

    def mm(self, out, lhsT, rhs, start, stop, r, w, skip=False):
        if skip:
            self.P.op("pe", lambda e: e.matmul(out, lhsT=lhsT, rhs=rhs, start=start, stop=stop,
                                               skip_group_check=True), r, w)
        else:
            self.P.op("pe", lambda e: e.matmul(out, lhsT=lhsT, rhs=rhs, start=start, stop=stop), r, w)

    def tr(self, out, in_, r, w, ident=None):
        idn = self.ident if ident is None else ident
        self.P.op("pe", lambda e: e.transpose(out=out, in_=in_, identity=idn), r, w)

    def act(self, out, in_, func, r, w, bias=0.0, scale=1.0, accum=None):
        if accum is None:
            self.P.op("act", lambda e: e.activation(out=out, in_=in_, func=func, bias=bias, scale=scale), r, w)
        else:
            self.P.op("act", lambda e: e.activation(out=out, in_=in_, func=func, bias=bias, scale=scale,
                                                    accum_out=accum), r, w)

    def tt(self, eng, out, in0, in1, op, r, w):
        self.P.op(eng, lambda e: e.tensor_tensor(out=out, in0=in0, in1=in1, op=op), r, w)

    def ts(self, eng, out, in0, s1, s2, op0, op1, r, w):
        if s2 is None:
            self.P.op(eng, lambda e: e.tensor_scalar(out=out, in0=in0, scalar1=s1, scalar2=None, op0=op0), r, w)
        else:
            self.P.op(eng, lambda e: e.tensor_scalar(out=out, in0=in0, scalar1=s1, scalar2=s2, op0=op0, op1=op1), r, w)

    def stt(self, out, in0, scalar, in1, op0, op1, r, w):
        self.P.op("dve", lambda e: e.scalar_tensor_tensor(out=out, in0=in0, scalar=scalar, in1=in1, op0=op0, op1=op1),
                  r, w)

    def cp(self, eng, out, in_, r, w):
        if eng == "act":
            self.P.op("act", lambda e: e.copy(out=out, in_=in_), r, w)
        else:
            self.P.op(eng, lambda e: e.tensor_copy(out=out, in_=in_), r, w)

    def ms(self, eng, ap, val, w):
        self.P.op(eng, lambda e: e.memset(ap, val), (), w)

    def recip(self, out, in_, r, w):
        self.P.op("dve", lambda e: e.reciprocal(out=out, in_=in_), r, w)

    def dma(self, q, out, in_, r, w):
        self.P.op(q, lambda e: e.dma_start(out=out, in_=in_), r, w, dma=True)


def blk(ap3, c0, c1, t0, n):
    return ap3[c0:c1, :, t0:t0 + n].rearrange("c p t -> p c t")


def wslab(w2d, k0, kc, c0, ncol):
    return w2d[k0 * 128:(k0 + kc) * 128, c0:c0 + ncol].rearrange("(kc p) n -> p kc n", p=128)


def build(nlayers=DEPTH, dump=(), stop_after=None):
    bb = B(dump)
    P = bb.P
    nc = bb.nc
    ps = bb.ps
    mm, tr, act, tt, ts, stt, cp, ms, recip, dma = bb.mm, bb.tr, bb.act, bb.tt, bb.ts, bb.stt, bb.cp, bb.ms, bb.recip, bb.dma
    din, dsc = bb.din, bb.dsc

    xT_in = din("xT", [KC, 128, T])
    ccT = din("ccT", [128, KC, 2])
    w_ada = din("w_ada", [DEPTH, D, 6 * D])
    b_ada = din("b_ada", [DEPTH, 6 * D])
    gT = din("gT", [DEPTH, 128, 4, KC])
    w_in = din("w_in", [DEPTH, D, IN_W])
    diff_lam = din("diff_lam", [DEPTH, 4 * 64])
    subln = din("attn_subln_g", [DEPTH, 128])
    hsw = din("hsw", [DEPTH, 128, 12, 3])
    hw1 = din("hy_ffn_w1", [DEPTH, 33, 64])
    hw2 = din("hy_ffn_w2", [DEPTH, 64, 64])
    hw3 = din("hy_ffn_w3", [DEPTH, 64, 2048])
    hpb = din("hpb", [DEPTH, 64, 4])
    hbias = din("hbias", [DEPTH, 128, 2, 4])
    pool_w = din("pool_w", [DEPTH, 4, 128, 128])
    pscale = din("pscale", [DEPTH, 128, 4])
    w_att_o = din("w_att_o", [DEPTH, 1024, D])
    w_hy_o = din("w_hy_o", [DEPTH, 512, D])
    w_pool_o = din("w_pool_o", [DEPTH, 512, D])
    w_out = din("w_out", [DEPTH, D, D])
    w_up = din("w_up", [DEPTH, D, 2 * FF])
    fcw = din("fcw", [DEPTH, 128, FJ, 3])
    w_down = din("w_down", [DEPTH, FF, D])
    c_ident = din("c_ident", [128, 128])
    c_cos = din("c_cos", [128, T])
    c_sin = din("c_sin", [128, T])
    c_corr = din("c_corr", [128, 4, 16])
    segs = []
    for nm, L in (("x", LX), ("c", LCX)):
        SC = L // 128
        TBI = 256
        segs.append(dict(
            nm=nm, L=L, SC=SC, t0=0 if nm == "x" else LX, TBI=TBI,
            FW=din(f"c_fw_{nm}", [2, SC, 128, SC, 128], BF16),
            GV=din(f"c_gv_{nm}", [L // TBI, 128, 2 * SC, TBI], BF16),
            zT=din(f"c_zT_{nm}", [33, L]),
            decay=din(f"c_decay_{nm}", [128, SC, 512]),
            ws=din(f"c_ws_{nm}", [128, 3, SC]),
            KTAB=dsc(f"KTAB_{nm}", [DEPTH, 2, SC, 128, 3, 512]),
        ))
    outT = P.dram("outT", [KC, 128, LX], F32, kind="ExternalOutput").ap()

    XT = dsc("XT", [KC, 128, T])
    MOD = dsc("MOD", [DEPTH, 2, 6 * D])
    QKT = dsc("QKT", [16, 128, T], BF16)
    VA = dsc("VA", [NT, 128, 8, 129], BF16)
    HYT = dsc("HYT", [12, 128, T])
    PLT = dsc("PLT", [4, 128, T])
    GT = dsc("GT", [48, 128, T], BF16)
    BRT = dsc("BRT", [16, 128, T], BF16)
    YT = dsc("YT", [KC, 128, T])
    AT = dsc("AT", [FJ, 128, T], BF16)

    identf = P.sb([128, 128], F32, "identf")
    ident = P.sb([128, 128], BF16, "ident")
    ones_b = P.sb([128, 128], BF16, "ones_b")
    ones_f = P.sb([128, 128], F32, "ones_f")
    modD = P.sb([128, DEPTH, 2, 6, KC], F32, "modD")
    bb.ident = ident[:]
    dma("sp", identf[:], c_ident, (), [identf])
    cp("dve", ident[:], identf[:], [identf], [ident])
    ms("dve", ones_b[:], 1.0, [ones_b])
    ms("dve", ones_f[:], 1.0, [ones_f])
    dma("sp", XT, xT_in, (), ["XT"])
    base_mark = P.sb_mark()

    def ada_phase():
        m0 = P.sb_mark()
        sc = P.sb([128, KC, 2], F32, "sc")
        NB_ = 4
        wa = [P.sb([128, KC, 512], F32, f"wa{i}") for i in range(NB_)]
        bt = [P.sb([2, 512], F32, f"bt{i}") for i in range(NB_)]
        mo = [P.sb([2, 512], F32, f"mo{i}") for i in range(NB_)]
        dma("sp", sc[:], ccT, (), [sc])
        act(sc[:], sc[:], AF.Silu, [sc], [sc])
        it = 0
        for l in range(nlayers):
            for ng in range(24):
                b = it % NB_
                it += 1
                dma("sp" if it % 2 == 0 else "act", wa[b][:], wslab(w_ada[l], 0, KC, ng * 512, 512), (), [wa[b]])
                dma("act", bt[b][:], b_ada[l:l + 1, ng * 512:(ng + 1) * 512].broadcast_to([2, 512]), (), [bt[b]])
                pb = ps[4 + b % 2]
                for kc in range(KC):
                    mm(pb[0:2, :], sc[:, kc, :], wa[b][:, kc, :], kc == 0, kc == KC - 1, [sc, wa[b]], [pb])
                tt("dve", mo[b][:], pb[0:2, :], bt[b][:], ALU.add, [pb, bt[b]], [mo[b]])
                dma("sp", MOD[l, :, ng * 512:(ng + 1) * 512], mo[b][:], [mo[b]], [("MOD", l)])
        P.barrier()
        P.sb_reset(m0)
        mr = [P.sb([96, 128], F32, f"mr{i}") for i in range(2)]
        mt = P.sb([128, 2, 6, KC], F32, "mt")
        g_sb = P.sb([128, 4, KC], F32, "g_sb")
        tmp = P.sb([128, KC], F32, "tmpm")
        it = 0
        for l in range(nlayers):
            dma("sp", g_sb[:], gT[l], (), [g_sb])
            for s in range(2):
                b = it % 2
                it += 1
                dma("sp", mr[b][:], MOD[l, s].rearrange("(r p) -> r p", p=128), (), [mr[b]])
                pb = ps[4 + b]
                tr(pb[:, 0:96], mr[b][:], [mr[b], identf], [pb], ident=identf[0:96, 0:96])
                cp("dve", mt[:, s].rearrange("p m j -> p (m j)"), pb[:, 0:96], [pb], [(mt, s)])
                ts("dve", tmp[:], mt[:, s, 1, :], 1.0, None, ALU.add, None, [(mt, s)], [tmp])
                tt("dve", modD[:, l, s, 0, :], tmp[:], g_sb[:, 0, :], ALU.mult, [tmp, g_sb], [modD])
                cp("dve", modD[:, l, s, 1, :], mt[:, s, 0, :], [(mt, s)], [modD])
                tt("dve", modD[:, l, s, 2, :], mt[:, s, 2, :], g_sb[:, 1, :], ALU.mult, [(mt, s), g_sb], [modD])
                ts("dve", tmp[:], mt[:, s, 4, :], 1.0, None, ALU.add, None, [(mt, s)], [tmp])
                tt("dve", modD[:, l, s, 3, :], tmp[:], g_sb[:, 2, :], ALU.mult, [tmp, g_sb], [modD])
                cp("dve", modD[:, l, s, 4, :], mt[:, s, 3, :], [(mt, s)], [modD])
                tt("dve", modD[:, l, s, 5, :], mt[:, s, 5, :], g_sb[:, 3, :], ALU.mult, [(mt, s), g_sb], [modD])
        P.barrier()
        P.sb_reset(m0)

    def stats_rstd(src, n, sq, rstd, pstat, rkeys, nchunks=KC, dim=D):
        act(sq[:, :, :n], src, AF.Square, rkeys, [sq])
        for j in range(nchunks):
            mm(pstat[:, :n], ones_b[:], sq[:, j, :n], j == 0, j == nchunks - 1, [sq, ones_b], [pstat])
        act(rstd[:, :n], pstat[:, :n], AF.Sqrt, [pstat], [rstd], bias=EPS, scale=1.0 / dim)
        recip(rstd[:, :n], rstd[:, :n], [rstd], [rstd])

    def seg_of(t0):
        return 0 if t0 < LX else 1

    def norm_pass(l, ia, ib, hxT):
        m0 = P.sb_mark()
        xb = [P.sb([128, KC, 512], F32, f"xb{i}") for i in range(2)]
        sq = P.sb([128, KC, 512], BF16, "sq")
        t1 = P.sb([128, KC, 512], F32, "t1")
        rstd = [P.sb([128, 512], F32, f"rstd{i}") for i in range(2)]
        for bi, (t0, n) in enumerate(TB):
            b = bi % 2
            s = seg_of(t0)
            dma("sp", xb[b][:, :, :n], blk(XT, 0, KC, t0, n), ["XT"], [xb[b]])
            stats_rstd(xb[b][:, :, :n], n, sq, rstd[b], ps[4 + b], [xb[b]])
            tt("dve", t1[:, :, :n], xb[b][:, :, :n], rstd[b][:, :n].unsqueeze(1).broadcast_to([128, KC, n]), ALU.mult,
               [xb[b], rstd[b]], [t1])
            tt("pool", t1[:, :, :n], t1[:, :, :n], modD[:, l, s, ia, :].unsqueeze(2).broadcast_to([128, KC, n]), ALU.mult,
               [t1, modD], [t1])
            tt("dve", hxT[:, :, t0:t0 + n], t1[:, :, :n], modD[:, l, s, ib, :].unsqueeze(2).broadcast_to([128, KC, n]),
               ALU.add, [t1, modD], [("hxT", bi)])
        P.barrier()
        P.sb_reset(m0)

    def resid_pass(l, ig):
        m0 = P.sb_mark()
        xb = [P.sb([128, KC, 512], F32, f"rxb{i}") for i in range(2)]
        yb = [P.sb([128, KC, 512], F32, f"ryb{i}") for i in range(2)]
        sq = P.sb([128, KC, 512], BF16, "rsq")
        rstd = [P.sb([128, 512], F32, f"rrstd{i}") for i in range(2)]
        for bi, (t0, n) in enumerate(TB):
            b = bi % 2
            s = seg_of(t0)
            dma("sp", xb[b][:, :, :n], blk(XT, 0, KC, t0, n), ["XT"], [xb[b]])
            dma("act", yb[b][:, :, :n], blk(YT, 0, KC, t0, n), ["YT"], [yb[b]])
            stats_rstd(yb[b][:, :, :n], n, sq, rstd[b], ps[4 + b], [yb[b]])
            tt("dve", yb[b][:, :, :n], yb[b][:, :, :n], rstd[b][:, :n].unsqueeze(1).broadcast_to([128, KC, n]), ALU.mult,
               [yb[b], rstd[b]], [yb[b]])
            tt("pool", yb[b][:, :, :n], yb[b][:, :, :n], modD[:, l, s, ig, :].unsqueeze(2).broadcast_to([128, KC, n]),
               ALU.mult, [yb[b], modD], [yb[b]])
            tt("dve", xb[b][:, :, :n], xb[b][:, :, :n], yb[b][:, :, :n], ALU.add, [xb[b], yb[b]], [xb[b]])
            dma("sp", blk(XT, 0, KC, t0, n), xb[b][:, :, :n], [xb[b]], ["XT"])
        P.barrier()
        P.sb_reset(m0)

    def resid_norm_pass(l, ig, l2, ia, ib, hxT):
        m0 = P.sb_mark()
        xb = [P.sb([128, KC, 512], F32, f"fxb{i}") for i in range(2)]
        yb = P.sb([128, KC, 512], F32, "fyb")
        sq = P.sb([128, KC, 512], BF16, "fsq")
        rstd = [P.sb([128, 512], F32, f"frstd{i}") for i in range(4)]
        for bi, (t0, n) in enumerate(TB):
            b = bi % 2
            s_ = seg_of(t0)
            dma("sp", xb[b][:, :, :n], blk(XT, 0, KC, t0, n), ["XT"], [xb[b]])
            dma("act", yb[:, :, :n], blk(YT, 0, KC, t0, n), ["YT"], [yb])
            stats_rstd(yb[:, :, :n], n, sq, rstd[b], ps[4 + b], [yb])
            tt("dve", yb[:, :, :n], yb[:, :, :n], rstd[b][:, :n].unsqueeze(1).broadcast_to([128, KC, n]), ALU.mult,
               [yb, rstd[b]], [yb])
            tt("pool", yb[:, :, :n], yb[:, :, :n], modD[:, l, s_, ig, :].unsqueeze(2).broadcast_to([128, KC, n]),
               ALU.mult, [yb, modD], [yb])
            tt("dve", xb[b][:, :, :n], xb[b][:, :, :n], yb[:, :, :n], ALU.add, [xb[b], yb], [xb[b]])
            dma("sp", blk(XT, 0, KC, t0, n), xb[b][:, :, :n], [xb[b]], ["XT"])
            r2 = rstd[2 + b]
            stats_rstd(xb[b][:, :, :n], n, sq, r2, ps[6 + b], [xb[b]])
            tt("dve", yb[:, :, :n], xb[b][:, :, :n], r2[:, :n].unsqueeze(1).broadcast_to([128, KC, n]), ALU.mult,
               [xb[b], r2], [yb])
            tt("pool", yb[:, :, :n], yb[:, :, :n], modD[:, l2, s_, ia, :].unsqueeze(2).broadcast_to([128, KC, n]),
               ALU.mult, [yb, modD], [yb])
            tt("pool", hxT[:, :, t0:t0 + n], yb[:, :, :n], modD[:, l2, s_, ib, :].unsqueeze(2).broadcast_to([128, KC, n]),
               ALU.add, [yb, modD], [("hxT", bi)])
        P.barrier()
        P.sb_reset(m0)

    def gemm_fm(loaders, kcs, actT, akey, blocks, evac, wt, perm_wt=None, ncs=4):
        for gi, load in enumerate(loaders):
            slab = wt[gi % 2]
            load(slab)
            pslab = None
            if perm_wt is not None and perm_wt[0](gi):
                pslab = perm_wt[1][gi % 2]
                v = slab[:].rearrange("p k (g b i) -> p k g b i", b=2, i=16)
                vp = pslab[:].rearrange("p k (g b i) -> p k g b i", b=2, i=16)
                for kh in range(2):
                    ksl = slice(kh * (kcs // 2), (kh + 1) * (kcs // 2))
                    cp("pool", vp[:, ksl, :, 0, :], v[:, ksl, :, 1, :], [slab], [(pslab, kh, 0)])
                    cp("pool", vp[:, ksl, :, 1, :], v[:, ksl, :, 0, :], [slab], [(pslab, kh, 1)])
            for bi, (t0, n) in enumerate(blocks):
                for c in range(ncs):
                    pa = bb.gbank()
                    for kc in range(kcs):
                        mm(pa[:, :n], slab[:, kc, c * 128:(c + 1) * 128], actT[:, kc, t0:t0 + n], kc == 0, kc == kcs - 1,
                           [slab, akey(bi)], [pa])
                    pb = None
                    if pslab is not None:
                        pb = bb.gbank()
                        rk = [(pslab, kh, q) for kh in range(2) for q in range(2)]
                        for kc in range(kcs):
                            mm(pb[:, :n], pslab[:, kc, c * 128:(c + 1) * 128], actT[:, kc, t0:t0 + n], kc == 0,
                               kc == kcs - 1, rk + [akey(bi)], [pb])
                    evac(gi, c, bi, t0, n, pa, pb)

    def proj_phase(l, hxT):
        m0 = P.sb_mark()
        wt = [P.sb([128, KC, 512], BF16, f"wt{i}") for i in range(2)]
        wp = [P.sb([128, KC, 512], BF16, f"wp{i}") for i in range(2)]
        cosT = P.sb([128, T], F32, "cosT")
        sinT = P.sb([128, T], F32, "sinT")
        dma("sp", cosT[:], c_cos, (), [cosT])
        dma("sp", sinT[:], c_sin, (), [sinT])
        ev = [P.sb([128, 512], F32, f"ev{i}") for i in range(4)]
        evb = [P.sb([128, 512], BF16, f"evb{i}") for i in range(4)]
        st = {"i": 0}
        W = w_in[l]

        def ld(c0):
            def f(slab):
                dma("pool", slab[:], wslab(W, 0, KC, c0, 512), (), [slab])
            return f

        def evac(gi_abs):
            def f(gi, c, bi, t0, n, pa, pb):
                i = st["i"] % 4
                st["i"] += 1
                ch = gi_abs * 4 + c
                if gi_abs < 4:
                    tt("dve", ev[i][:, :n], pa[:, :n], cosT[:, t0:t0 + n], ALU.mult, [pa, cosT], [ev[i]])
                    j = (i + 1) % 4
                    st["i"] += 1
                    tt("dve", ev[j][:, :n], pb[:, :n], sinT[:, t0:t0 + n], ALU.mult, [pb, sinT], [ev[j]])
                    tt("pool", evb[i][:, :n], ev[i][:, :n], ev[j][:, :n], ALU.add, [ev[i], ev[j]], [evb[i]])
                    dma("sp", QKT[ch, :, t0:t0 + n], evb[i][:, :n], [evb[i]], [("QKT", ch)])
                elif gi_abs < 9:
                    cp("act", ev[i][:, :n], pa[:, :n], [pa], [ev[i]])
                    dma("sp", HYT[ch - 24, :, t0:t0 + n], ev[i][:, :n], [ev[i]], [("HYT", ch - 24)])
                elif gi_abs < 10:
                    cp("act", ev[i][:, :n], pa[:, :n], [pa], [ev[i]])
                    dma("sp", PLT[ch - 36, :, t0:t0 + n], ev[i][:, :n], [ev[i]], [("PLT", ch - 36)])
                else:
                    act(evb[i][:, :n], pa[:, :n], AF.Sigmoid, [pa], [evb[i]])
                    dma("sp", GT[ch - 40, :, t0:t0 + n], evb[i][:, :n], [evb[i]], [("GT", ch - 40)])
            return f

        akey = lambda bi: ("hxT", bi)
        for gi_abs in list(range(0, 4)) + list(range(6, 22)):
            gemm_fm([ld(gi_abs * 512)], KC, hxT, akey, TB, evac(gi_abs), wt,
                    perm_wt=((lambda g: True), wp) if gi_abs < 4 else None)
            wt.reverse()
            wp.reverse()
        va = [P.sb([128, 8, 129], BF16, f"va{i}") for i in range(2)]
        for i in range(2):
            ms("dve", va[i][:, :, 128:129], 1.0, [va[i]])
        wv = [wt[0], wt[1]]
        for g in range(2):
            dma("pool", wv[g][:], wslab(W, 0, KC, 2048 + g * 512, 512), (), [wv[g]])
        for it in range(NT):
            b = it % 2
            bi = min(it // 4, 4)
            for g in range(2):
                pa = bb.gbank()
                for kc in range(KC):
                    mm(pa[:], hxT[:, kc, it * 128:(it + 1) * 128], wv[g][:, kc, :], kc == 0, kc == KC - 1,
                       [wv[g], ("hxT", bi)], [pa])
                cp("act" if g == 0 else "dve", va[b][:, g * 4:(g + 1) * 4, 0:128],
                   pa[:].rearrange("p (h e) -> p h e", h=4), [pa], [va[b]])
            dma("sp", VA[it], va[b][:], [va[b]], ["VA"])
        P.barrier()
        P.sb_reset(m0)

    def attn_phase(l):
        m0 = P.sb_mark()
        lam_init = 0.8 - 0.6 * math.exp(-0.3 * l)
        lp = P.sb([128, 4, 64], F32, "lp")
        pr = P.sb([128, 2, 64], F32, "pr")
        s12 = P.sb([128, 2], F32, "s12")
        nlam = P.sb([128, 1], F32, "nlam")
        gcol = P.sb([128, 1], F32, "gcol")
        dma("sp", lp[:].rearrange("p a d -> p (a d)"), diff_lam[l:l + 1, :].broadcast_to([128, 256]), (), [lp])
        dma("sp", gcol[:], subln[l:l + 1, :].rearrange("o e -> e o"), (), [gcol])
        ts("dve", gcol[:], gcol[:], float(1.0 - lam_init), None, ALU.mult, None, [gcol], [gcol])
        tt("dve", pr[:, 0, :], lp[:, 0, :], lp[:, 1, :], ALU.mult, [lp], [pr])
        tt("dve", pr[:, 1, :], lp[:, 2, :], lp[:, 3, :], ALU.mult, [lp], [pr])
        P.op("dve", lambda e: e.reduce_sum(out=s12[:], in_=pr[:], axis=AX.X), [pr], [s12])
        act(s12[:], s12[:], AF.Exp, [s12], [s12])
        tt("dve", nlam[:], s12[:, 1:2], s12[:, 0:1], ALU.subtract, [s12], [nlam])
        ts("dve", nlam[:], nlam[:], -lam_init, None, ALU.add, None, [nlam], [nlam])
        qT = [P.sb([128, T], BF16, f"qT{i}") for i in range(2)]
        kT = [P.sb([128, T], BF16, f"kT{i}") for i in range(2)]
        vh = [P.sb([128, NT, 128], BF16, f"vh{i}") for i in range(2)]
        E = [P.sb([128, 512], BF16, f"E{i}") for i in range(3)]
        rd = [P.sb([128, 512], F32, f"rd{i}") for i in range(2)]
        om = [P.sb([128, 512], F32, f"om{i}") for i in range(2)]
        attf = P.sb([128, 512], F32, "attf")
        sqb = P.sb([128, 512], BF16, "sqb")
        rs_ = P.sb([128, 512], F32, "ars")
        atT = [P.sb([128, 512], BF16, f"atT{i}") for i in range(2)]
        state = {"ob": 0}
        oT = [ps[2], ps[3]]
        den = [ps[4], ps[5]]
        pstat = ps[6]

        def emit_load(h):
            hb = h % 2
            dma("sp", qT[hb][:], QKT[h], [("QKT", h)], [qT[hb]])
            dma("sp", kT[hb][:], QKT[8 + h], [("QKT", 8 + h)], [kT[hb]])
            dma("act", vh[hb][:], VA[:, :, h, 0:128].rearrange("i p e -> p i e"), ["VA"], [vh[hb]])

        steps = []
        for h in range(8):
            for bi, (q0, n) in enumerate(TB):
                kcs = list(range(NT)) if q0 < LX else [16, 17]
                for m in range(2):
                    for kc in kcs:
                        steps.append(dict(h=h, bi=bi, q0=q0, n=n, m=m, kc=kc, first=kc == kcs[0], last=kc == kcs[-1]))

        def emit_S(i):
            st_ = steps[i]
            hb, m, kc, q0, n = st_["h"] % 2, st_["m"], st_["kc"], st_["q0"], st_["n"]
            sb_ = ps[i % 2]
            mm(sb_[:, :n], kT[hb][m * 64:(m + 1) * 64, kc * 128:(kc + 1) * 128],
               qT[hb][m * 64:(m + 1) * 64, q0:q0 + n], True, True, [kT[hb], qT[hb]], [sb_])

        def emit_PV(i):
            st_ = steps[i]
            h, m, kc, q0, n = st_["h"], st_["m"], st_["kc"], st_["q0"], st_["n"]
            hb = h % 2
            sb_ = ps[i % 2]
            Et = E[i % 3]
            act(Et[:, :n], sb_[:, :n], AF.Exp, [sb_], [Et], scale=0.125)
            mm(oT[m][:, :n], vh[hb][:, kc, :], Et[:, :n], st_["first"], st_["last"], [Et, vh[hb]], [oT[m]])
            mm(den[m][:, :n], ones_b[:], Et[:, :n], st_["first"], st_["last"], [Et, ones_b], [den[m]])
            if not st_["last"]:
                return
            recip(rd[m][:, :n], den[m][:, :n], [den[m]], [rd[m]])
            tt("dve", om[m][:, :n], oT[m][:, :n], rd[m][:, :n], ALU.mult, [oT[m], rd[m]], [om[m]])
            if m == 0:
                return
            stt(attf[:, :n], om[1][:, :n], nlam[:, 0:1], om[0][:, :n], ALU.mult, ALU.add, [om[0], om[1], nlam], [attf])
            tt("pool", sqb[:, :n], attf[:, :n], attf[:, :n], ALU.mult, [attf], [sqb])
            mm(pstat[:, :n], ones_b[:], sqb[:, :n], True, True, [sqb, ones_b], [pstat])
            act(rs_[:, :n], pstat[:, :n], AF.Sqrt, [pstat], [rs_], bias=EPS, scale=1.0 / 128)
            recip(rs_[:, :n], rs_[:, :n], [rs_], [rs_])
            tt("dve", attf[:, :n], attf[:, :n], rs_[:, :n], ALU.mult, [attf, rs_], [attf])
            ob = state["ob"]
            state["ob"] += 1
            o_t = atT[ob % 2]
            ts("dve", o_t[:, :n], attf[:, :n], gcol[:, 0:1], None, ALU.mult, None, [attf, gcol], [o_t])
            dma("sp", BRT[h, :, q0:q0 + n], o_t[:, :n], [o_t], [("BRT", h)])

        emit_load(0)
        emit_S(0)
        for i in range(len(steps)):
            if i + 1 < len(steps):
                if steps[i + 1]["h"] != steps[i]["h"]:
                    emit_load(steps[i + 1]["h"])
                emit_S(i + 1)
            emit_PV(i)
        P.barrier()
        P.sb_reset(m0)

    def dwconv3(eng, out, src, w3, r, w):
        n = PW - 2
        ts(eng, out[:, 1:1 + n], src[:, 0:n], w3[:, 0:1], None, ALU.mult, None, r, w)
        if eng == "dve":
            stt(out[:, 1:1 + n], src[:, 1:1 + n], w3[:, 1:2], out[:, 1:1 + n], ALU.mult, ALU.add, r + w, w)
            stt(out[:, 1:1 + n], src[:, 2:2 + n], w3[:, 2:3], out[:, 1:1 + n], ALU.mult, ALU.add, r + w, w)
        else:
            raise NotImplementedError

    def load_padded(q, dst, src3, ch, w, rkeys):
        dma(q, dst[:, 1:1 + LX], src3[ch, :, 0:LX], rkeys, w)
        dma(q, dst[:, LX + 3:LX + 3 + LCX], src3[ch, :, LX:T], rkeys, w)

    def zero_pads(eng, t, w):
        ms(eng, t[:, 0:1], 0.0, w)
        ms(eng, t[:, LX + 1:LX + 3], 0.0, w)
        ms(eng, t[:, PW - 1:PW], 0.0, w)

    def filt_phase(l, sg):
        m0 = P.sb_mark()
        L, SC = sg["L"], sg["SC"]
        NB = 512 if L >= 512 else L
        w1 = P.sb([33, 64], F32, "w1")
        w2 = P.sb([64, 64], F32, "w2")
        w3 = P.sb([64, 2048], F32, "w3")
        pbf = P.sb([64, 4], F32, "pbf")
        zT = P.sb([33, L], F32, "zT")
        h1 = P.sb([64, L], F32, "h1")
        h2 = P.sb([64, L], F32, "h2")
        tq = P.sb([64, 512], F32, "tq")
        rq = P.sb([64, 512], F32, "rq")
        MAGIC = 12582912.0
        wsc = P.sb([128, 3, SC], F32, "wsc")
        dma("sp", w1[:], hw1[l], (), [w1])
        dma("sp", w2[:], hw2[l], (), [w2])
        dma("sp", w3[:], hw3[l], (), [w3])
        dma("sp", pbf[:], hpb[l], (), [pbf])
        dma("sp", zT[:], sg["zT"], (), [zT])
        dma("sp", wsc[:], sg["ws"], (), [wsc])
        OFF = math.pi + 16 * TWO_PI
        for (wm, kk, src, dst, ib, ifr) in ((w1, 33, zT, h1, 0, 1), (w2, 64, h1, h2, 2, 3)):
            for b0 in range(0, L, NB):
                pa = bb.gbank()
                mm(pa[0:64, :NB], wm[0:kk, :], src[0:kk, b0:b0 + NB], True, True, [wm, src], [pa])
                ts("dve", tq[:, :NB], pa[0:64, :NB], pbf[:, ib:ib + 1], pbf[:, ifr:ifr + 1], ALU.add, ALU.mult,
                   [pa, pbf], [tq])
                ts("dve", rq[:, :NB], tq[:, :NB], 1.0 / TWO_PI, MAGIC, ALU.mult, ALU.add, [tq], [rq])
                ts("dve", rq[:, :NB], rq[:, :NB], -MAGIC, -TWO_PI, ALU.add, ALU.mult, [rq], [rq])
                tt("dve", tq[:, :NB], tq[:, :NB], rq[:, :NB], ALU.add, [tq, rq], [tq])
                act(dst[:, b0:b0 + NB], tq[:, :NB], AF.Sin, [tq], [dst])
        dec = [P.sb([128, 512], F32, f"dec{i}") for i in range(2)]
        kr = [P.sb([128, 512], F32, f"kr{i}") for i in range(4)]
        ab = [P.sb([128, 512], F32, f"ab{i}") for i in range(4)]
        ksum = P.sb([128, SC, 1024], BF16, "ksum")
        kdif = P.sb([128, SC, 1024], BF16, "kdif")
        rn = [P.sb([128, 512], F32, f"rn{i}") for i in range(2)]
        psN = [ps[4], ps[5]]
        for lc in range(SC):
            d_ = dec[lc % 2]
            dma("sp", d_[:], sg["decay"][:, lc, :], (), [d_])
            for cb in range(4):
                pa = bb.gbank()
                mm(pa[:], h2[:, lc * 128:(lc + 1) * 128], w3[:, cb * 512:(cb + 1) * 512], True, True, [h2, w3], [pa])
                tt("dve", kr[cb][:], pa[:], d_[:], ALU.mult, [pa, d_], [kr[cb]])
                if lc == 0 and cb >= 2:
                    ms("dve", kr[cb][0:1, :], 0.0, [kr[cb]])
                act(ab[cb][:], kr[cb][:], AF.Abs, [kr[cb]], [ab[cb]])
            for o in range(2):
                for dr in range(2):
                    mm(psN[o][:], ones_f[:], ab[dr * 2 + o][:], lc == 0 and dr == 0, lc == SC - 1 and dr == 1,
                       [ab[dr * 2 + o], ones_f], [psN[o]])
                tt("pool", ksum[:, lc, o * 512:(o + 1) * 512], kr[o][:], kr[2 + o][:], ALU.add, [kr[o], kr[2 + o]],
                   [(ksum, lc, o)])
                tt("pool", kdif[:, lc, o * 512:(o + 1) * 512], kr[o][:], kr[2 + o][:], ALU.subtract,
                   [kr[o], kr[2 + o]], [(kdif, lc, o)])
        for o in range(2):
            ts("dve", rn[o][:], psN[o][:], EPS, None, ALU.add, None, [psN[o]], [rn[o]])
            recip(rn[o][:], rn[o][:], [rn[o]], [rn[o]])
        fcs = [P.sb([128, SC, 128], BF16, f"fcs{i}") for i in range(2)]
        fss = [P.sb([128, SC, 128], BF16, f"fss{i}") for i in range(2)]
        tab = [P.sb([128, 3, 512], F32, f"tab{i}") for i in range(2)]
        N = 2 * L
        it = 0
        for fc in range(SC):
            fb = fc % 2
            dma("sp", fcs[fb][:], sg["FW"][0, fc], (), [fcs[fb]])
            dma("act", fss[fb][:], sg["FW"][1, fc], (), [fss[fb]])
            for o in range(2):
                tb_ = tab[it % 2]
                it += 1
                pA = bb.gbank()
                pB = bb.gbank()
                ksk = [(ksum, lc, o) for lc in range(SC)]
                kdk = [(kdif, lc, o) for lc in range(SC)]
                for lc in range(SC):
                    mm(pA[:], fcs[fb][:, lc, :], ksum[:, lc, o * 512:(o + 1) * 512], lc == 0, lc == SC - 1,
                       [fcs[fb]] + ksk, [pA])
                for lc in range(SC):
                    mm(pB[:], fss[fb][:, lc, :], kdif[:, lc, o * 512:(o + 1) * 512], lc == 0, lc == SC - 1,
                       [fss[fb]] + kdk, [pB])
                stt(tb_[:, 0, :], pA[:], wsc[:, 0, fc:fc + 1], rn[o][:], ALU.mult, ALU.mult, [pA, wsc, rn[o]], [tb_])
                stt(tb_[:, 1, :], pB[:], wsc[:, 1, fc:fc + 1], rn[o][:], ALU.mult, ALU.mult, [pB, wsc, rn[o]], [tb_])
                stt(tb_[:, 2, :], pA[:], wsc[:, 2, fc:fc + 1], rn[o][:], ALU.mult, ALU.mult, [pA, wsc, rn[o]], [tb_])
                if fc == 0:
                    pC = bb.gbank()
                    for lc in range(SC):
                        mm(pC[0:1, :], fss[fb][:, lc, 0:1], ksum[:, lc, o * 512:(o + 1) * 512], lc == 0, lc == SC - 1,
                           [fss[fb]] + ksk, [pC])
                    stt(tb_[0:1, 2, :], pC[0:1, :], 1.0 / N, rn[o][0:1, :], ALU.mult, ALU.mult, [pC, rn[o], tb_], [tb_])
                dma("sp", sg["KTAB"][l, o, fc], tb_[:], [tb_], [("KTAB", sg["nm"])])
        P.barrier()
        P.sb_reset(m0)

    def hyena_phase(l):
        m0 = P.sb_mark()
        sw = P.sb([128, 12, 3], F32, "sw")
        hb = P.sb([128, 2, 4], F32, "hb")
        dma("sp", sw[:], hsw[l], (), [sw])
        dma("sp", hb[:], hbias[l], (), [hb])
        uT = P.sb([128, 4, PW], F32, "uT")
        mT = P.sb([128, 4, PW], F32, "mT")
        utok = P.sb([128, NT, 512], BF16, "utok")
        mk_a = P.sb_mark()
        raw = [P.sb([128, PW], F32, f"raw{i}") for i in range(2)]
        ubf = P.sb([128, 4, T], BF16, "ubf")
        P.sb_reset(mk_a)
        Y = P.sb([128, 32, 512], BF16, "Y")
        fcs = [P.sb([128, 16, 128], BF16, f"hfc{i}") for i in range(2)]
        fss = [P.sb([128, 16, 128], BF16, f"hfs{i}") for i in range(2)]
        tab = [P.sb([128, 3, 512], F32, f"htab{i}") for i in range(2)]
        gv = [P.sb([128, 32, 256], BF16, f"gv{i}") for i in range(2)]
        tm = [P.sb([128, 512], F32, f"htm{i}") for i in range(4)]
        ob = [P.sb([128, 256], BF16, f"hob{i}") for i in range(2)]
        ri = 0

        def conv_chunks(c0, dst):
            nonlocal ri
            for cc in range(4):
                r_ = raw[ri % 2]
                ri += 1
                load_padded("sp", r_, HYT, c0 + cc, [r_], [("HYT", c0 + cc)])
                dwconv3("dve", dst[:, cc, :], r_, sw[:, c0 + cc, :], [r_, sw], [(dst, cc)])

        ti = 0
        gi = 0
        for o in range(2):
            P.barrier()
            for i in range(2):
                zero_pads("dve", raw[i], [raw[i]])
            if o == 0:
                conv_chunks(0, uT)
            conv_chunks(4 + 4 * o, mT)
            for cc in range(4):
                cp("pool", ubf[:, cc, 0:LX], uT[:, cc, 1:1 + LX], [(uT, cc)], [(ubf, cc)])
                cp("pool", ubf[:, cc, LX:T], uT[:, cc, LX + 3:LX + 3 + LCX], [(uT, cc)], [(ubf, cc)])
            tpi = 0
            for cc in range(4):
                for s0 in range(0, NT, 8):
                    ns = min(8, NT - s0)
                    pt = ps[6 + tpi % 2]
                    tpi += 1
                    ptb = pt[:].bitcast(BF16)
                    for k in range(ns):
                        tr(ptb[:, k * 128:(k + 1) * 128], ubf[:, cc, (s0 + k) * 128:(s0 + k + 1) * 128],
                           [(ubf, cc), ident], [pt])
                    cp("act", utok[:, s0:s0 + ns, cc * 128:(cc + 1) * 128],
                       ptb[:, :ns * 128].rearrange("p (k c) -> p k c", c=128), [pt], [(utok, cc, s0)])
            ukeys = [(utok, cc, s0) for cc in range(4) for s0 in range(0, NT, 8)]
            P.barrier()
            for sg in segs:
                SC, L, tk0 = sg["SC"], sg["L"], sg["t0"] // 128
                for fc in range(SC):
                    fb = gi % 2
                    gi += 1
                    dma("sp", fcs[fb][:, :SC, :], sg["FW"][0, fc], (), [fcs[fb]])
                    dma("act", fss[fb][:, :SC, :], sg["FW"][1, fc], (), [fss[fb]])
                    dma("sp", tab[fb][:], sg["KTAB"][l, o, fc], [("KTAB", sg["nm"])], [tab[fb]])
                    pc = ps[4] if fc % 2 == 0 else ps[2]
                    pS = ps[5] if fc % 2 == 0 else ps[3]
                    for sc in range(SC):
                        mm(pc[:], fcs[fb][:, sc, :], utok[:, tk0 + sc, :], sc == 0, sc == SC - 1, [fcs[fb]] + ukeys, [pc])
                    for sc in range(SC):
                        mm(pS[:], fss[fb][:, sc, :], utok[:, tk0 + sc, :], sc == 0, sc == SC - 1, [fss[fb]] + ukeys, [pS])
                    a_, b_, c_, d_ = tm[0], tm[1], tm[2], tm[3]
                    tt("dve", a_[:], pc[:], tab[fb][:, 0, :], ALU.mult, [pc, tab[fb]], [a_])
                    tt("dve", b_[:], pS[:], tab[fb][:, 1, :], ALU.mult, [pS, tab[fb]], [b_])
                    tt("pool", Y[:, fc, :], a_[:], b_[:], ALU.subtract, [a_, b_], [(Y, fc)])
                    tt("dve", c_[:], pc[:], tab[fb][:, 1, :], ALU.mult, [pc, tab[fb]], [c_])
                    tt("dve", d_[:], pS[:], tab[fb][:, 2, :], ALU.mult, [pS, tab[fb]], [d_])
                    tt("pool", Y[:, SC + fc, :], c_[:], d_[:], ALU.add, [c_, d_], [(Y, SC + fc)])
                ykeys = [(Y, k) for k in range(2 * SC)]
                TBI = sg["TBI"]
                for tbk in range(L // TBI):
                    g_ = gv[ti % 2]
                    ti += 1
                    dma("sp", g_[:, :2 * SC, :], sg["GV"][tbk], (), [g_])
                    tt0 = sg["t0"] + tbk * TBI
                    p0 = pidx(tt0)
                    for cc in range(4):
                        pa = bb.gbank()
                        for k in range(2 * SC):
                            mm(pa[:, :TBI], Y[:, k, cc * 128:(cc + 1) * 128], g_[:, k, :], k == 0, k == 2 * SC - 1,
                               ykeys + [g_], [pa])
                        t_ = tm[(cc) % 4]
                        stt(t_[:, :TBI], uT[:, cc, p0:p0 + TBI], hb[:, o, cc:cc + 1], pa[:, :TBI], ALU.mult, ALU.add,
                            [(uT, cc), hb, pa], [t_])
                        if o == 0:
                            tt("pool", uT[:, cc, p0:p0 + TBI], t_[:, :TBI], mT[:, cc, p0:p0 + TBI], ALU.mult,
                               [t_, (mT, cc)], [(uT, cc)])
                        else:
                            o_ = ob[cc % 2]
                            tt("pool", o_[:, :TBI], t_[:, :TBI], mT[:, cc, p0:p0 + TBI], ALU.mult, [t_, (mT, cc)], [o_])
                            dma("sp", BRT[8 + cc, :, tt0:tt0 + TBI], o_[:, :TBI], [o_], [("BRT", 8 + cc)])
        P.barrier()
        P.sb_reset(m0)

    def pool_phase(l):
        m0 = P.sb_mark()
        PP = T + 32
        xo, co = 8, LX + 24
        pw = P.sb([128, 4, 128], BF16, "pw")
        psc = P.sb([128, 4], F32, "psc")
        corr = P.sb([128, 4, 16], F32, "corr")
        dma("pool", pw[:], pool_w[l].rearrange("g i o -> i g o"), (), [pw])
        dma("sp", psc[:], pscale[l], (), [psc])
        dma("sp", corr[:], c_corr, (), [corr])
        for g, win in enumerate((2, 4, 8, 16)):
            pin = P.sb([128, PP], F32, f"pin{g}")
            A_ = P.sb([128, PP], F32, f"pA{g}")
            B_ = P.sb([128, PP], F32, f"pB{g}")
            pm = P.sb([128, T], BF16, f"pm{g}")
            ms("pool", pin[:], 0.0, [pin])
            dma("sp", pin[:, xo:xo + LX], PLT[g, :, 0:LX], [("PLT", g)], [pin])
            dma("sp", pin[:, co:co + LCX], PLT[g, :, LX:T], [("PLT", g)], [pin])
            lo, hi = 8, PP - 8
            nn = hi - lo
            if win == 2:
                tt("dve", A_[:, lo:hi], pin[:, lo - 1:hi - 1], pin[:, lo:hi], ALU.add, [pin], [A_])
                S = A_
            else:
                tt("dve", A_[:, 0:PP - 1], pin[:, 0:PP - 1], pin[:, 1:PP], ALU.add, [pin], [A_])
                if win == 4:
                    tt("dve", B_[:, lo:hi], A_[:, lo - 2:hi - 2], A_[:, lo:hi], ALU.add, [A_], [B_])
                    S = B_
                else:
                    tt("dve", B_[:, 0:PP - 3], A_[:, 0:PP - 3], A_[:, 2:PP - 1], ALU.add, [A_], [B_])
                    if win == 8:
                        tt("dve", A_[:, lo:hi], B_[:, lo - 4:hi - 4], B_[:, lo:hi], ALU.add, [B_, A_], [A_])
                        S = A_
                    else:
                        tt("dve", A_[:, 0:PP - 7], B_[:, 0:PP - 7], B_[:, 4:PP - 3], ALU.add, [B_, A_], [A_])
                        tt("dve", B_[:, lo:hi], A_[:, lo - 8:hi - 8], A_[:, lo:hi], ALU.add, [A_, B_], [B_])
                        S = B_
            ts("dve", S[:, lo:hi], S[:, lo:hi], 1.0 / win, None, ALU.mult, None, [S], [S])
            for (o_, Ls) in ((xo, LX), (co, LCX)):
                tt("dve", S[:, o_:o_ + 8], S[:, o_:o_ + 8], corr[:, g, 0:8], ALU.mult, [S, corr], [S])
                tt("dve", S[:, o_ + Ls - 8:o_ + Ls], S[:, o_ + Ls - 8:o_ + Ls], corr[:, g, 8:16], ALU.mult, [S, corr], [S])
            tt("dve", pm[:, 0:LX], S[:, xo:xo + LX], pin[:, xo:xo + LX], ALU.subtract, [S, pin], [pm])
            tt("dve", pm[:, LX:T], S[:, co:co + LCX], pin[:, co:co + LCX], ALU.subtract, [S, pin], [pm])
            for bi, (t0, n) in enumerate(TB):
                pa = bb.gbank()
                mm(pa[:, :n], pw[:, g, :], pm[:, t0:t0 + n], True, True, [pw, pm], [pa])
                o_t = P.sb([128, 512], BF16, f"po{g}_{bi}")
                ts("dve", o_t[:, :n], pa[:, :n], psc[:, g:g + 1], None, ALU.mult, None, [pa, psc], [o_t])
                dma("sp", BRT[12 + g, :, t0:t0 + n], o_t[:, :n], [o_t], [("BRT", 12 + g)])
        P.barrier()
        P.sb_reset(m0)

    def merge_phase(l):
        m0 = P.sb_mark()
        brt = P.sb([128, 16, T], BF16, "brt")
        mgT = P.sb([128, 16, T], BF16, "mgT")
        wt = [P.sb([128, KC, 512], BF16, f"mwt{i}") for i in range(2)]
        g3 = [P.sb([128, 3, 512], BF16, f"g3{i}") for i in range(2)]
        t3 = [P.sb([128, 3, 512], F32, f"t3{i}") for i in range(2)]
        ev = [P.sb([128, 512], F32, f"mev{i}") for i in range(2)]
        for q4 in range(4):
            dma("sp", brt[:, q4 * 4:(q4 + 1) * 4, :], BRT[q4 * 4:(q4 + 1) * 4].rearrange("c p t -> p c t"),
                [("BRT", c) for c in range(q4 * 4, q4 * 4 + 4)], [(brt, q4)])
        GT4 = GT.rearrange("(b c) p t -> b c p t", b=3)
        it = 0
        for ng in range(4):
            slab = wt[ng % 2]
            dma("pool", slab[:, 0:8, :], wslab(w_att_o[l], 0, 8, ng * 512, 512), (), [(slab, 0)])
            dma("pool", slab[:, 8:12, :], wslab(w_hy_o[l], 0, 4, ng * 512, 512), (), [(slab, 1)])
            dma("pool", slab[:, 12:16, :], wslab(w_pool_o[l], 0, 4, ng * 512, 512), (), [(slab, 2)])
            for bi, (t0, n) in enumerate(TB):
                for c in range(4):
                    nch = ng * 4 + c
                    b = it % 2
                    it += 1
                    dma("sp", g3[b][:, :, :n], GT4[:, nch, :, t0:t0 + n].rearrange("b p t -> p b t"),
                        [("GT", br * 16 + nch) for br in range(3)], [g3[b]])
                    for br, (k0, k1) in enumerate(((0, 8), (8, 12), (12, 16))):
                        pa = bb.gbank()
                        q4s = [(brt, q) for q in ((0, 1) if br == 0 else (2,) if br == 1 else (3,))]
                        for kc in range(k0, k1):
                            mm(pa[:, :n], slab[:, kc, c * 128:(c + 1) * 128], brt[:, kc, t0:t0 + n], kc == k0, kc == k1 - 1,
                               [(slab, br)] + q4s, [pa])
                        tt("dve", t3[b][:, br, :n], pa[:, :n], g3[b][:, br, :n], ALU.mult, [pa, g3[b]], [(t3[b], br)])
                    tt("pool", t3[b][:, 0, :n], t3[b][:, 0, :n], t3[b][:, 1, :n], ALU.add, [(t3[b], 0), (t3[b], 1)],
                       [(t3[b], 0)])
                    tt("pool", mgT[:, nch, t0:t0 + n], t3[b][:, 0, :n], t3[b][:, 2, :n], ALU.add,
                       [(t3[b], 0), (t3[b], 2)], [(mgT, bi, nch)])
        P.barrier()

        def ld(c0):
            def f(slab):
                dma("pool", slab[:], wslab(w_out[l], 0, KC, c0, 512), (), [slab])
            return f
        st = {"i": 0}

        def evac(gi, c, bi, t0, n, pa, pb):
            i = st["i"] % 2
            st["i"] += 1
            cp("act" if i == 0 else "dve", ev[i][:, :n], pa[:, :n], [pa], [ev[i]])
            dma("sp", YT[gi * 4 + c, :, t0:t0 + n], ev[i][:, :n], [ev[i]], ["YT"])
        gemm_fm([ld(g * 512) for g in range(4)], KC, mgT, lambda bi: "nokey", TB, evac, wt)
        P.barrier()
        P.sb_reset(m0)

    def ffn_phase(l, h2T):
        m0 = P.sb_mark()
        cw = P.sb([128, FJ, 3], F32, "cw")
        dma("sp", cw[:], fcw[l], (), [cw])
        wg = [P.sb([128, KC, 256], BF16, f"wg{i}") for i in range(2)]
        wv = [P.sb([128, KC, 256], BF16, f"wv{i}") for i in range(2)]
        gp = [P.sb([128, PW], F32, f"gp{i}") for i in range(2)]
        vv = [P.sb([128, T], F32, f"vv{i}") for i in range(2)]
        cv = P.sb([128, PW], F32, "cv")
        x2 = P.sb([128, PW], F32, "x2")
        uu = P.sb([128, PW], F32, "uu")
        sg_ = P.sb([128, PW], F32, "sg")
        abf = [P.sb([128, T], BF16, f"abf{i}") for i in range(2)]
        for i in range(2):
            zero_pads("dve", gp[i], [gp[i]])
        n_ = PW - 2
        GC = 2.0 * math.sqrt(2.0 / math.pi)
        W = w_up[l]
        pending = []

        def chain(j, b):
            def c1():
                dwconv3("dve", cv, gp[b], cw[:, j, :], [gp[b], cw], [cv])

            def c2():
                act(x2[:, 1:1 + n_], cv[:, 1:1 + n_], AF.Square, [cv], [x2])
                ts("pool", x2[:, 1:1 + n_], x2[:, 1:1 + n_], 0.044715, 1.0, ALU.mult, ALU.add, [x2], [x2])
                tt("pool", uu[:, 1:1 + n_], x2[:, 1:1 + n_], cv[:, 1:1 + n_], ALU.mult, [x2, cv], [uu])

            def c3():
                act(sg_[:, 1:1 + n_], uu[:, 1:1 + n_], AF.Sigmoid, [uu], [sg_], scale=GC)
                tt("pool", uu[:, 1:1 + n_], cv[:, 1:1 + n_], sg_[:, 1:1 + n_], ALU.mult, [cv, sg_, uu], [uu])

            def c4():
                tt("dve", abf[b][:, 0:LX], uu[:, 1:1 + LX], vv[b][:, 0:LX], ALU.mult, [uu, vv[b]], [(abf[b], 0)])
                tt("dve", abf[b][:, LX:T], uu[:, LX + 3:LX + 3 + LCX], vv[b][:, LX:T], ALU.mult, [uu, vv[b]],
                   [(abf[b], 1)])
                dma("sp", AT[j], abf[b][:], [(abf[b], 0), (abf[b], 1)], ["AT"])
            return [c1, c2, c3, c4]

        for g2 in range(FJ // 2):
            sb_ = g2 % 2
            dma("pool", wg[sb_][:], wslab(W, 0, KC, g2 * 256, 256), (), [wg[sb_]])
            dma("pool", wv[sb_][:], wslab(W, 0, KC, FF + g2 * 256, 256), (), [wv[sb_]])
            for c in range(2):
                j = g2 * 2 + c
                b = j % 2
                for bi, (t0, n) in enumerate(TB):
                    pa = bb.gbank()
                    for kc in range(KC):
                        mm(pa[:, :n], wg[sb_][:, kc, c * 128:(c + 1) * 128], h2T[:, kc, t0:t0 + n], kc == 0, kc == KC - 1,
                           [wg[sb_]], [pa])
                    cp("act", gp[b][:, pidx(t0):pidx(t0) + n], pa[:, :n], [pa], [gp[b]])
                    pb = bb.gbank()
                    for kc in range(KC):
                        mm(pb[:, :n], wv[sb_][:, kc, c * 128:(c + 1) * 128], h2T[:, kc, t0:t0 + n], kc == 0, kc == KC - 1,
                           [wv[sb_]], [pb])
                    cp("dve", vv[b][:, t0:t0 + n], pb[:, :n], [pb], [vv[b]])
                    if pending:
                        pending.pop(0)()
                pending.extend(chain(j, b))
        for f_ in pending:
            f_()
        P.barrier()
        P.sb_reset(m0)

    def down_phase(l):
        m0 = P.sb_mark()
        HT = T // 2
        aT = P.sb([128, FJ, HT], BF16, "aT")
        wd = [P.sb([128, FJ, 256], BF16, f"wd{i}") for i in range(2)]
        ev = [P.sb([128, 512], F32, f"dev{i}") for i in range(2)]
        st = {"i": 0}
        for half in range(2):
            h0 = half * HT
            for q4 in range(4):
                dma("sp", aT[:, q4 * 11:(q4 + 1) * 11, :], blk(AT, q4 * 11, (q4 + 1) * 11, h0, HT), ["AT"], [aT])

            def ld(c0):
                def f(slab):
                    dma("pool", slab[:], w_down[l][:, c0:c0 + 256].rearrange("(j p) n -> p j n", p=128), (), [slab])
                return f

            def evac(gi, c, bi, t0, n, pa, pb, h0=h0):
                i = st["i"] % 2
                st["i"] += 1
                cp("act" if i == 0 else "dve", ev[i][:, :n], pa[:, :n], [pa], [ev[i]])
                dma("sp", YT[gi * 2 + c, :, h0 + t0:h0 + t0 + n], ev[i][:, :n], [ev[i]], ["YT"])
            gemm_fm([ld(g * 256) for g in range(8)], FJ, aT, lambda bi: aT, [(0, 512), (512, 512), (1024, 128)], evac, wd,
                    ncs=2)
            P.barrier()
        P.sb_reset(m0)

    ada_phase()
    for l in range(nlayers):
        for sg in segs:
            filt_phase(l, sg)
    mk = P.sb_mark()
    hxT = P.sb([128, KC, T], BF16, "hxT")
    norm_pass(0, 0, 1, hxT)
    for l in range(nlayers):
        proj_phase(l, hxT)
        P.sb_reset(mk)
        if stop_after == "proj":
            break
        attn_phase(l)
        hyena_phase(l)
        pool_phase(l)
        if stop_after == "branches":
            break
        merge_phase(l)
        mk = P.sb_mark()
        hxT = P.sb([128, KC, T], BF16, "h2T")
        resid_norm_pass(l, 2, l, 3, 4, hxT)
        if stop_after == "mixer":
            break
        ffn_phase(l, hxT)
        P.sb_reset(mk)
        down_phase(l)
        if l + 1 < nlayers:
            mk = P.sb_mark()
            hxT = P.sb([128, KC, T], BF16, "hxT")
            resid_norm_pass(l, 5, l + 1, 0, 1, hxT)
        else:
            resid_pass(l, 5)
    dma("sp", outT, XT[:, :, 0:LX], ["XT"], ["outT"])
    P.barrier()
    return P.emit(), P


def _bf16(a):
    return np.asarray(a, dtype=np.float32).astype(ml_dtypes.bfloat16)


def _dft_consts(L):
    N = 2 * L
    SC = L // 128
    ct = np.cos(2.0 * np.pi * np.arange(N) / N)
    st = np.sin(2.0 * np.pi * np.arange(N) / N)
    s = np.arange(L)
    idx = (s[:, None] * s[None, :]) % N
    Mc = ct[idx]
    Ms = st[idx]
    Fs = Ms.copy()
    Fs[:, 0] = (-1.0) ** s
    Gs = Ms.copy()
    Gs[0, :] = (-1.0) ** s
    def fw(M):
        return M.reshape(SC, 128, SC, 128).transpose(2, 1, 0, 3)
    FW = np.stack([fw(Mc), fw(Fs)], axis=0)
    TBI = 256
    def gv(M):
        return M.reshape(SC, 128, L // TBI, TBI).transpose(2, 1, 0, 3)
    GV = np.concatenate([gv(Mc), gv(Gs)], axis=2)
    ws = np.full((128, 3, SC), 2.0 / N, dtype=np.float32)
    ws[0, 0, 0] = 1.0 / N
    ws[0, 1, 0] = 0.0
    f32 = np.float32
    t = np.linspace(0.0, 1.0, L, dtype=f32)[:, None]
    w_ang = (f32(2.0 * math.pi) * np.arange(L, dtype=f32)[:, None] / f32(L)).astype(f32)
    fb = np.linspace(1e-4, 15, 16, dtype=f32)[None, :]
    ang = (fb * w_ang).astype(f32)
    z = np.concatenate([t, np.cos(ang), -np.sin(ang)], axis=-1).astype(f32)
    zT = np.ascontiguousarray(z.T)
    dmin = math.log(1e-2) / 1.5
    dmax = math.log(1e-2) / 0.3
    deltas = np.abs(np.linspace(dmin, dmax, 512, dtype=f32))
    decay = np.exp(-t * deltas[None, :]).astype(f32)
    decay = np.ascontiguousarray(decay.reshape(SC, 128, 512).transpose(1, 0, 2))
    return dict(FW=_bf16(np.ascontiguousarray(FW)), GV=_bf16(np.ascontiguousarray(GV)), ws=ws, zT=zT, decay=decay)


def _rope_consts():
    f32 = np.float32
    inv = (f32(10000.0) ** (-np.arange(16, dtype=f32) / f32(16))).astype(f32)
    tt_ = np.arange(LX)
    row = (tt_ // 64).astype(f32)
    col = (tt_ % 64).astype(f32)
    cosT = np.ones((128, T), dtype=f32)
    sinT = np.zeros((128, T), dtype=f32)
    for m in range(2):
        for a in range(2):
            pos = row if a == 0 else col
            ang = (pos[None, :] * inv[:, None]).astype(f32)
            for b in range(2):
                p0 = m * 64 + a * 32 + b * 16
                cosT[p0:p0 + 16, :LX] = np.cos(ang)
                sinT[p0:p0 + 16, :LX] = (-1.0 if b == 0 else 1.0) * np.sin(ang)
    return cosT, sinT


def _pool_corr():
    corr = np.ones((128, 4, 16), dtype=np.float32)
    for g, win in enumerate((2, 4, 8, 16)):
        h = win // 2
        for pos in range(8):
            cnt = min(pos, h) + h
            corr[:, g, pos] = win / cnt
            rem = 8 - pos
            cnt2 = h + min(h, rem)
            corr[:, g, 8 + pos] = win / cnt2
    return corr


_CACHE = {}


def _consts():
    if "c" not in _CACHE:
        cx = _dft_consts(LX)
        cc = _dft_consts(LCX)
        cosT, sinT = _rope_consts()
        m = {"c_ident": np.eye(128, dtype=np.float32), "c_cos": cosT, "c_sin": sinT, "c_corr": _pool_corr()}
        for nm, c in (("x", cx), ("c", cc)):
            m[f"c_fw_{nm}"] = c["FW"]
            m[f"c_gv_{nm}"] = c["GV"]
            m[f"c_zT_{nm}"] = c["zT"]
            m[f"c_decay_{nm}"] = c["decay"]
            m[f"c_ws_{nm}"] = c["ws"]
        _CACHE["c"] = m
    return _CACHE["c"]


def make_in_maps(inputs, ncore=NCORE):
    f = lambda a: np.ascontiguousarray(np.asarray(a, dtype=np.float32))
    I = {k: np.asarray(v) for k, v in inputs.items()}
    shared = dict(_consts())
    shared["w_ada"] = f(I["w_ada"])
    shared["b_ada"] = f(I["b_ada"])
    shared["gT"] = f(I["norm_g"].reshape(DEPTH, 4, KC, 128).transpose(0, 3, 1, 2))
    shared["w_in"] = f(I["w_in"])
    shared["diff_lam"] = f(I["diff_lam"].reshape(DEPTH, 256))
    shared["attn_subln_g"] = f(I["attn_subln_g"])
    shared["hsw"] = f(I["hy_short_w"].reshape(DEPTH, 3, 12, 128).transpose(0, 3, 2, 1))
    shared["hy_ffn_w1"] = f(I["hy_ffn_w1"])
    shared["hy_ffn_w2"] = f(I["hy_ffn_w2"])
    shared["hy_ffn_w3"] = f(I["hy_ffn_w3"])
    shared["hpb"] = f(np.stack([I["hy_ffn_b1"], I["hy_freq"][:, 0], I["hy_ffn_b2"], I["hy_freq"][:, 1]], axis=-1))
    shared["hbias"] = f(I["hy_bias"].reshape(DEPTH, 2, 4, 128).transpose(0, 3, 1, 2))
    shared["pool_w"] = f(I["pool_w"])
    shared["pscale"] = f(I["pool_scale"].reshape(DEPTH, 4, 128).transpose(0, 2, 1))
    shared["w_att_o"] = f(I["w_att_o"])
    shared["w_hy_o"] = f(I["w_hy_o"])
    shared["w_pool_o"] = f(I["w_pool_o"])
    shared["w_out"] = f(I["w_out"])
    shared["w_up"] = f(I["w_up"])
    shared["fcw"] = f(I["ff_conv_w"].reshape(DEPTH, 3, FJ, 128).transpose(0, 3, 2, 1))
    shared["w_down"] = f(I["w_down"])
    maps = []
    for b in range(ncore):
        X = np.concatenate([I["x"][b], I["ctx"][b]], axis=0)
        m = dict(shared)
        m["xT"] = f(X.T.reshape(KC, 128, T))
        cc = np.stack([I["c"][b], I["c_ctx"]], axis=0)
        m["ccT"] = f(cc.reshape(2, KC, 128).transpose(2, 1, 0))
        maps.append(m)
    return maps


def kernel(**inputs):
    if "nc" not in _CACHE:
        _CACHE["nc"] = build()[0]
    nc = _CACHE["nc"]
    maps = make_in_maps(inputs)
    res = run_bass_kernel_spmd(nc, maps, core_ids=list(range(NCORE)))
    outs = []
    for b in range(NCORE):
        oT = np.asarray(res.results[b]["outT"], dtype=np.float32)
        outs.append(np.ascontiguousarray(oT.reshape(D, LX).T))
    return np.stack(outs, axis=0)
```

```python
import numpy as np
import concourse.bass as bass
import concourse.mybir as mybir

F32 = mybir.dt.float32
BF16 = mybir.dt.bfloat16
AF = mybir.ActivationFunctionType
ALU = mybir.AluOpType
AX = mybir.AxisListType

ENGS = ("sp", "act", "dve", "pool", "pe")
NSLOT = 12
SEM_CAP = 30000


class _Op:
    __slots__ = ("fn", "deps", "dma", "signal", "sigcount", "slot", "val", "prev", "semidx")

    def __init__(self, fn, deps, dma):
        self.fn = fn
        self.deps = deps
        self.dma = dma
        self.signal = False
        self.sigcount = 0
        self.slot = 0
        self.val = 0
        self.prev = None
        self.semidx = 0


def _key(x):
    if isinstance(x, tuple):
        return tuple(_key(y) for y in x)
    if isinstance(x, (str, int)):
        return x
    return ("id", id(x))


class Prog:
    def __init__(self, same_engine_sync=True):
        self.nc = bass.Bass("TRN2", target_bir_lowering=False)
        self.ops = {e: [] for e in ENGS}
        self.lastw = {}
        self.readers = {}
        self.same = same_engine_sync
        self.sb_off = 16384
        self.sb_hw = 0
        self.ndma = {e: 0 for e in ENGS}
        self.slot_last = {}
        self._n = 0

    def sb(self, shape, dt, name=None):
        self._n += 1
        name = name or f"sb{self._n}"
        esz = 4 if dt == F32 else 2
        if dt in (mybir.dt.int32, mybir.dt.uint32):
            esz = 4
        per_part = int(np.prod(shape[1:])) * esz
        per_part = (per_part + 63) // 64 * 64
        t = self.nc.alloc_sbuf_tensor_at(f"{name}_{self._n}", list(shape), dt, offset=self.sb_off)
        self.sb_off += per_part
        self.sb_hw = max(self.sb_hw, self.sb_off)
        assert self.sb_off <= 16384 + 212000, f"SBUF overflow {self.sb_off}"
        return t

    def sb_mark(self):
        return self.sb_off

    def sb_reset(self, mark):
        self.sb_off = mark

    def dram(self, name, shape, dt, kind="Internal"):
        return self.nc.dram_tensor(name, list(shape), dt, kind=kind)

    def op(self, eng, fn, reads=(), writes=(), dma=False):
        reads = [_key(r) for r in reads]
        writes = [_key(w) for w in writes]
        deps = set()
        for r in reads:
            lw = self.lastw.get(r)
            if lw is not None:
                deps.add(lw)
        for w in writes:
            lw = self.lastw.get(w)
            if lw is not None:
                deps.add(lw)
            for rd in self.readers.get(w, ()):
                deps.add(rd)
        idx = len(self.ops[eng])
        me = (eng, idx)
        o = _Op(fn, None, dma)
        fdeps = []
        for d in deps:
            if d == me:
                continue
            tgt = self.ops[d[0]][d[1]]
            if d[0] == eng and not tgt.dma:
                if eng == "pe" or not self.same:
                    continue
            fdeps.append(d)
        best = {}
        pruned = []
        for d in fdeps:
            if self.ops[d[0]][d[1]].dma:
                pruned.append(d)
            elif d[0] not in best or d[1] > best[d[0]]:
                best[d[0]] = d[1]
        pruned.extend(best.items())
        o.deps = pruned
        if dma:
            j = self.ndma[eng]
            self.ndma[eng] += 1
            o.slot = j % NSLOT
            o.val = 16 * (j // NSLOT + 1)
            o.prev = self.slot_last.get((eng, o.slot))
            self.slot_last[(eng, o.slot)] = me
        self.ops[eng].append(o)
        for w in writes:
            self.lastw[w] = me
            self.readers[w] = []
        for r in reads:
            self.readers.setdefault(r, []).append(me)
        return me

    def barrier(self):
        lasts = []
        for e in ENGS:
            if self.ops[e]:
                for i in range(len(self.ops[e]) - 1, -1, -1):
                    if not self.ops[e][i].dma and self.ops[e][i].fn is not None:
                        lasts.append((e, i))
                        break
            for s in range(NSLOT):
                l = self.slot_last.get((e, s))
                if l is not None:
                    lasts.append(l)
        self._barrier_deps = lasts
        for e in ENGS:
            o = _Op(None, [d for d in lasts if not (d[0] == e and not self.ops[d[0]][d[1]].dma and e == "pe")], False)
            self.ops[e].append(o)
        self.lastw = {}
        self.readers = {}

    def emit(self):
        nc = self.nc
        for e in ENGS:
            for o in self.ops[e]:
                for (e2, i2) in o.deps:
                    t = self.ops[e2][i2]
                    if not t.dma:
                        t.signal = True
        nsem = {}
        for e in ENGS:
            c = 0
            for o in self.ops[e]:
                if o.dma or o.fn is None:
                    continue
                if o.signal:
                    c += 1
                    o.semidx = (c - 1) // SEM_CAP
                    o.sigcount = c - o.semidx * SEM_CAP
            nsem[e] = max(1, (c + SEM_CAP - 1) // SEM_CAP)
        from contextlib import ExitStack
        with ExitStack() as es:
            csem = {e: [es.enter_context(nc.semaphore(f"c_{e}_{k}")) for k in range(nsem[e])] for e in ENGS}
            dsem = {e: [es.enter_context(nc.semaphore(f"d_{e}_{s}")) for s in range(NSLOT)]
                    for e in ENGS if self.ndma[e] > 0}
            block = es.enter_context(nc.Block())
            ops = self.ops

            def replay(e, eng):
                waited = {}

                def wait_for(d):
                    t = ops[d[0]][d[1]]
                    if t.dma:
                        key = ("d", d[0], t.slot)
                        sem = dsem[d[0]][t.slot]
                        val = t.val
                    else:
                        if t.fn is None:
                            return
                        key = ("c", d[0], t.semidx)
                        sem = csem[d[0]][t.semidx]
                        val = t.sigcount
                    if waited.get(key, 0) < val:
                        eng.wait_ge(sem, val)
                        waited[key] = val

                for o in ops[e]:
                    for d in o.deps:
                        wait_for(d)
                    if o.fn is None:
                        continue
                    if o.dma:
                        if o.prev is not None:
                            wait_for(o.prev)
                        inst = o.fn(eng)
                        inst.then_inc(dsem[e][o.slot], 16)
                    else:
                        inst = o.fn(eng)
                        if o.signal:
                            inst.then_inc(csem[e][o.semidx], 1)

            @block.sync
            def _(eng):
                replay("sp", eng)

            @block.scalar
            def _(eng):
                replay("act", eng)

            @block.vector
            def _(eng):
                replay("dve", eng)

            @block.gpsimd
            def _(eng):
                replay("pool", eng)

            @block.tensor
            def _(eng):
                replay("pe", eng)
        return nc


import math
import ml_dtypes
from concourse.bass_utils import run_bass_kernel_spmd

D = 2048
KC = 16
LX = 2048
LCX = 256
T = LX + LCX
NT = T // 128
TB = [(0, 512), (512, 512), (1024, 512), (1536, 512), (2048, 256)]
PW = T + 4
IN_W = 11264
FF = 5632
FJ = FF // 128
DEPTH = 4
EPS = 1e-6
NCORE = 4
TWO_PI = 2.0 * math.pi


def pidx(t):
    return t + 1 if t < LX else t + 3


class B:
    def __init__(self, dump=()):
        self.P = Prog()
        self.nc = self.P.nc
        self.dump = set(dump)
        nc = self.nc
        self.ps = [nc.alloc_psum_tensor(f"psb{i}", [128, 512], F32) for i in range(8)]
        self.rot = 0
        self.ident = None

    def din(self, name, shape, dt=F32):
        return self.P.dram(name, shape, dt, kind="ExternalInput").ap()

    def dsc(self, name, shape, dt=F32):
        kind = "ExternalOutput" if name in self.dump else "Internal"
        return self.P.dram(name, shape, dt, kind=kind).ap()

    def gbank(self):
        b = self.ps[self.rot % 4]
        self.rot += 1
        return b

    def mm(self, out, lhsT, rhs, start, stop, r, w, skip=False):
        if skip:
            self.P.op("pe", lambda e: e.matmul(out, lhsT=lhsT, rhs=rhs, start=start, stop=stop,
                                               skip_group_check=True), r, w)
        else:
            self.P.op("pe", lambda e: e.matmul(out, lhsT=lhsT, rhs=rhs, start=start, stop=stop), r, w)

    def tr(self, out, in_, r, w, ident=None):
        idn = self.ident if ident is None else ident
        self.P.op("pe", lambda e: e.transpose(out=out, in_=in_, identity=idn), r, w)

    def act(self, out, in_, func, r, w, bias=0.0, scale=1.0, accum=None):
        if accum is None:
            self.P.op("act", lambda e: e.activation(out=out, in_=in_, func=func, bias=bias, scale=scale), r, w)
        else:
            self.P.op("act", lambda e: e.activation(out=out, in_=in_, func=func, bias=bias, scale=scale,
                                                    accum_out=accum), r, w)

    def tt(self, eng, out, in0, in1, op, r, w):
        self.P.op(eng, lambda e: e.tensor_tensor(out=out, in0=in0, in1=in1, op=op), r, w)

    def ts(self, eng, out, in0, s1, s2, op0, op1, r, w):
        if s2 is None:
            self.P.op(eng, lambda e: e.tensor_scalar(out=out, in0=in0, scalar1=s1, scalar2=None, op0=op0), r, w)
        else:
            self.P.op(eng, lambda e: e.tensor_scalar(out=out, in0=in0, scalar1=s1, scalar2=s2, op0=op0, op1=op1), r, w)

    def stt(self, out, in0, scalar, in1, op0, op1, r, w):
        self.P.op("dve", lambda e: e.scalar_tensor_tensor(out=out, in0=in0, scalar=scalar, in1=in1, op0=op0, op1=op1),
                  r, w)

    def cp(self, eng, out, in_, r, w):
        if eng == "act":
            self.P.op("act", lambda e: e.copy(out=out, in_=in_), r, w)
        else:
            self.P.op(eng, lambda e: e.tensor_copy(out=out, in_=in_), r, w)

    def ms(self, eng, ap, val, w):
        self.P.op(eng, lambda e: e.memset(ap, val), (), w)

    def recip(self, out, in_, r, w):
        self.P.op("dve", lambda e: e.reciprocal(out=out, in_=in_), r, w)

    def dma(self, q, out, in_, r, w):
        self.P.op(q, lambda e: e.dma_start(out=out, in_=in_), r, w, dma=True)


def blk(ap3, c0, c1, t0, n):
    return ap3[c0:c1, :, t0:t0 + n].rearrange("c p t -> p c t")


def wslab(w2d, k0, kc, c0, ncol):
    return w2d[k0 * 128:(k0 + kc) * 128, c0:c0 + ncol].rearrange("(kc p) n -> p kc n", p=128)


def build(nlayers=DEPTH, dump=(), stop_after=None):
    bb = B(dump)
    P = bb.P
    nc = bb.nc
    ps = bb.ps
    mm, tr, act, tt, ts, stt, cp, ms, recip, dma = bb.mm, bb.tr, bb.act, bb.tt, bb.ts, bb.stt, bb.cp, bb.ms, bb.recip, bb.dma
    din, dsc = bb.din, bb.dsc

    xT_in = din("xT", [KC, 128, T])
    ccT = din("ccT", [128, KC, 2])
    w_ada = din("w_ada", [DEPTH, D, 6 * D])
    b_ada = din("b_ada", [DEPTH, 6 * D])
    gT = din("gT", [DEPTH, 128, 4, KC])
    w_in = din("w_in", [DEPTH, D, IN_W])
    diff_lam = din("diff_lam", [DEPTH, 4 * 64])
    subln = din("attn_subln_g", [DEPTH, 128])
    hsw = din("hsw", [DEPTH, 128, 12, 3])
    hw1 = din("hy_ffn_w1", [DEPTH, 33, 64])
    hw2 = din("hy_ffn_w2", [DEPTH, 64, 64])
    hw3 = din("hy_ffn_w3", [DEPTH, 64, 2048])
    hpb = din("hpb", [DEPTH, 64, 4])
    hbias = din("hbias", [DEPTH, 128, 2, 4])
    pool_w = din("pool_w", [DEPTH, 4, 128, 128])
    pscale = din("pscale", [DEPTH, 128, 4])
    w_att_o = din("w_att_o", [DEPTH, 1024, D])
    w_hy_o = din("w_hy_o", [DEPTH, 512, D])
    w_pool_o = din("w_pool_o", [DEPTH, 512, D])
    w_out = din("w_out", [DEPTH, D, D])
    w_up = din("w_up", [DEPTH, D, 2 * FF])
    fcw = din("fcw", [DEPTH, 128, FJ, 3])
    w_down = din("w_down", [DEPTH, FF, D])
    c_ident = din("c_ident", [128, 128])
    c_cos = din("c_cos", [128, T])
    c_sin = din("c_sin", [128, T])
    c_corr = din("c_corr", [128, 4, 16])
    segs = []
    for nm, L in (("x", LX), ("c", LCX)):
        SC = L // 128
        TBI = 256
        segs.append(dict(
            nm=nm, L=L, SC=SC, t0=0 if nm == "x" else LX, TBI=TBI,
            FW=din(f"c_fw_{nm}", [2, SC, 128, SC, 128], BF16),
            GV=din(f"c_gv_{nm}", [L // TBI, 128, 2 * SC, TBI], BF16),
            zT=din(f"c_zT_{nm}", [33, L]),
            decay=din(f"c_decay_{nm}", [128, SC, 512]),
            ws=din(f"c_ws_{nm}", [128, 3, SC]),
            KTAB=dsc(f"KTAB_{nm}", [DEPTH, 2, SC, 128, 3, 512]),
        ))
    outT = P.dram("outT", [KC, 128, LX], F32, kind="ExternalOutput").ap()

    XT = dsc("XT", [KC, 128, T])
    MOD = dsc("MOD", [DEPTH, 2, 6 * D])
    QKT = dsc("QKT", [16, 128, T], BF16)
    VA = dsc("VA", [NT, 128, 8, 129], BF16)
    HYT = dsc("HYT", [12, 128, T])
    PLT = dsc("PLT", [4, 128, T])
    GT = dsc("GT", [48, 128, T], BF16)
    BRT = dsc("BRT", [16, 128, T], BF16)
    YT = dsc("YT", [KC, 128, T])
    AT = dsc("AT", [FJ, 128, T], BF16)

    identf = P.sb([128, 128], F32, "identf")
    ident = P.sb([128, 128], BF16, "ident")
    ones_b = P.sb([128, 128], BF16, "ones_b")
    ones_f = P.sb([128, 128], F32, "ones_f")
    modD = P.sb([128, DEPTH, 2, 6, KC], F32, "modD")
    bb.ident = ident[:]
    dma("sp", identf[:], c_ident, (), [identf])
    cp("dve", ident[:], identf[:], [identf], [ident])
    ms("dve", ones_b[:], 1.0, [ones_b])
    ms("dve", ones_f[:], 1.0, [ones_f])
    dma("sp", XT, xT_in, (), ["XT"])
    base_mark = P.sb_mark()

    def ada_phase():
        m0 = P.sb_mark()
        sc = P.sb([128, KC, 2], F32, "sc")
        NB_ = 4
        wa = [P.sb([128, KC, 512], F32, f"wa{i}") for i in range(NB_)]
        bt = [P.sb([2, 512], F32, f"bt{i}") for i in range(NB_)]
        mo = [P.sb([2, 512], F32, f"mo{i}") for i in range(NB_)]
        dma("sp", sc[:], ccT, (), [sc])
        act(sc[:], sc[:], AF.Silu, [sc], [sc])
        it = 0
        for l in range(nlayers):
            for ng in range(24):
                b = it % NB_
                it += 1
                dma("sp" if it % 2 == 0 else "act", wa[b][:], wslab(w_ada[l], 0, KC, ng * 512, 512), (), [wa[b]])
                dma("act", bt[b][:], b_ada[l:l + 1, ng * 512:(ng + 1) * 512].broadcast_to([2, 512]), (), [bt[b]])
                pb = ps[4 + b % 2]
                for kc in range(KC):
                    mm(pb[0:2, :], sc[:, kc, :], wa[b][:, kc, :], kc == 0, kc == KC - 1, [sc, wa[b]], [pb])
                tt("dve", mo[b][:], pb[0:2, :], bt[b][:], ALU.add, [pb, bt[b]], [mo[b]])
                dma("sp", MOD[l, :, ng * 512:(ng + 1) * 512], mo[b][:], [mo[b]], [("MOD", l)])
        P.barrier()
        P.sb_reset(m0)
        mr = [P.sb([96, 128], F32, f"mr{i}") for i in range(2)]
        mt = P.sb([128, 2, 6, KC], F32, "mt")
        g_sb = P.sb([128, 4, KC], F32, "g_sb")
        tmp = P.sb([128, KC], F32, "tmpm")
        it = 0
        for l in range(nlayers):
            dma("sp", g_sb[:], gT[l], (), [g_sb])
            for s in range(2):
                b = it % 2
                it += 1
                dma("sp", mr[b][:], MOD[l, s].rearrange("(r p) -> r p", p=128), (), [mr[b]])
                pb = ps[4 + b]
                tr(pb[:, 0:96], mr[b][:], [mr[b], identf], [pb], ident=identf[0:96, 0:96])
                cp("dve", mt[:, s].rearrange("p m j -> p (m j)"), pb[:, 0:96], [pb], [(mt, s)])
                ts("dve", tmp[:], mt[:, s, 1, :], 1.0, None, ALU.add, None, [(mt, s)], [tmp])
                tt("dve", modD[:, l, s, 0, :], tmp[:], g_sb[:, 0, :], ALU.mult, [tmp, g_sb], [modD])
                cp("dve", modD[:, l, s, 1, :], mt[:, s, 0, :], [(mt, s)], [modD])
                tt("dve", modD[:, l, s, 2, :], mt[:, s, 2, :], g_sb[:, 1, :], ALU.mult, [(mt, s), g_sb], [modD])
                ts("dve", tmp[:], mt[:, s, 4, :], 1.0, None, ALU.add, None, [(mt, s)], [tmp])
                tt("dve", modD[:, l, s, 3, :], tmp[:], g_sb[:, 2, :], ALU.mult, [tmp, g_sb], [modD])
                cp("dve", modD[:, l, s, 4, :], mt[:, s, 3, :], [(mt, s)], [modD])
                tt("dve", modD[:, l, s, 5, :], mt[:, s, 5, :], g_sb[:, 3, :], ALU.mult, [(mt, s), g_sb], [modD])
        P.barrier()
        P.sb_reset(m0)

    def stats_rstd(src, n, sq, rstd, pstat, rkeys, nchunks=KC, dim=D):
        act(sq[:, :, :n], src, AF.Square, rkeys, [sq])
        for j in range(nchunks):
            mm(pstat[:, :n], ones_b[:], sq[:, j, :n], j == 0, j == nchunks - 1, [sq, ones_b], [pstat])
        act(rstd[:, :n], pstat[:, :n], AF.Sqrt, [pstat], [rstd], bias=EPS, scale=1.0 / dim)
        recip(rstd[:, :n], rstd[:, :n], [rstd], [rstd])

    def seg_of(t0):
        return 0 if t0 < LX else 1

    def norm_pass(l, ia, ib, hxT):
        m0 = P.sb_mark()
        xb = [P.sb([128, KC, 512], F32, f"xb{i}") for i in range(2)]
        sq = P.sb([128, KC, 512], BF16, "sq")
        t1 = P.sb([128, KC, 512], F32, "t1")
        rstd = [P.sb([128, 512], F32, f"rstd{i}") for i in range(2)]
        for bi, (t0, n) in enumerate(TB):
            b = bi % 2
            s = seg_of(t0)
            dma("sp", xb[b][:, :, :n], blk(XT, 0, KC, t0, n), ["XT"], [xb[b]])
            stats_rstd(xb[b][:, :, :n], n, sq, rstd[b], ps[4 + b], [xb[b]])
            tt("dve", t1[:, :, :n], xb[b][:, :, :n], rstd[b][:, :n].unsqueeze(1).broadcast_to([128, KC, n]), ALU.mult,
               [xb[b], rstd[b]], [t1])
            tt("pool", t1[:, :, :n], t1[:, :, :n], modD[:, l, s, ia, :].unsqueeze(2).broadcast_to([128, KC, n]), ALU.mult,
               [t1, modD], [t1])
            tt("dve", hxT[:, :, t0:t0 + n], t1[:, :, :n], modD[:, l, s, ib, :].unsqueeze(2).broadcast_to([128, KC, n]),
               ALU.add, [t1, modD], [("hxT", bi)])
        P.barrier()
        P.sb_reset(m0)

    def resid_pass(l, ig):
        m0 = P.sb_mark()
        xb = [P.sb([128, KC, 512], F32, f"rxb{i}") for i in range(2)]
        yb = [P.sb([128, KC, 512], F32, f"ryb{i}") for i in range(2)]
        sq = P.sb([128, KC, 512], BF16, "rsq")
        rstd = [P.sb([128, 512], F32, f"rrstd{i}") for i in range(2)]
        for bi, (t0, n) in enumerate(TB):
            b = bi % 2
            s = seg_of(t0)
            dma("sp", xb[b][:, :, :n], blk(XT, 0, KC, t0, n), ["XT"], [xb[b]])
            dma("act", yb[b][:, :, :n], blk(YT, 0, KC, t0, n), ["YT"], [yb[b]])
            stats_rstd(yb[b][:, :, :n], n, sq, rstd[b], ps[4 + b], [yb[b]])
            tt("dve", yb[b][:, :, :n], yb[b][:, :, :n], rstd[b][:, :n].unsqueeze(1).broadcast_to([128, KC, n]), ALU.mult,
               [yb[b], rstd[b]], [yb[b]])
            tt("pool", yb[b][:, :, :n], yb[b][:, :, :n], modD[:, l, s, ig, :].unsqueeze(2).broadcast_to([128, KC, n]),
               ALU.mult, [yb[b], modD], [yb[b]])
            tt("dve", xb[b][:, :, :n], xb[b][:, :, :n], yb[b][:, :, :n], ALU.add, [xb[b], yb[b]], [xb[b]])
            dma("sp", blk(XT, 0, KC, t0, n), xb[b][:, :, :n], [xb[b]], ["XT"])
        P.barrier()
        P.sb_reset(m0)

    def resid_norm_pass(l, ig, l2, ia, ib, hxT):
        m0 = P.sb_mark()
        xb = [P.sb([128, KC, 512], F32, f"fxb{i}") for i in range(2)]
        yb = P.sb([128, KC, 512], F32, "fyb")
        sq = P.sb([128, KC, 512], BF16, "fsq")
        rstd = [P.sb([128, 512], F32, f"frstd{i}") for i in range(4)]
        for bi, (t0, n) in enumerate(TB):
            b = bi % 2
            s_ = seg_of(t0)
            dma("sp", xb[b][:, :, :n], blk(XT, 0, KC, t0, n), ["XT"], [xb[b]])
            dma("act", yb[:, :, :n], blk(YT, 0, KC, t0, n), ["YT"], [yb])
            stats_rstd(yb[:, :, :n], n, sq, rstd[b], ps[4 + b], [yb])
            tt("dve", yb[:, :, :n], yb[:, :, :n], rstd[b][:, :n].unsqueeze(1).broadcast_to([128, KC, n]), ALU.mult,
               [yb, rstd[b]], [yb])
            tt("pool", yb[:, :, :n], yb[:, :, :n], modD[:, l, s_, ig, :].unsqueeze(2).broadcast_to([128, KC, n]),
               ALU.mult, [yb, modD], [yb])
            tt("dve", xb[b][:, :, :n], xb[b][:, :, :n], yb[:, :, :n], ALU.add, [xb[b], yb], [xb[b]])
            dma("sp", blk(XT, 0, KC, t0, n), xb[b][:, :, :n], [xb[b]], ["XT"])
            r2 = rstd[2 + b]
            stats_rstd(xb[b][:, :, :n], n, sq, r2, ps[6 + b], [xb[b]])
            tt("dve", yb[:, :, :n], xb[b][:, :, :n], r2[:, :n].unsqueeze(1).broadcast_to([128, KC, n]), ALU.mult,
               [xb[b], r2], [yb])
            tt("pool", yb[:, :, :n], yb[:, :, :n], modD[:, l2, s_, ia, :].unsqueeze(2).broadcast_to([128, KC, n]),
               ALU.mult, [yb, modD], [yb])
            tt("pool", hxT[:, :, t0:t0 + n], yb[:, :, :n], modD[:, l2, s_, ib, :].unsqueeze(2).broadcast_to([128, KC, n]),
               ALU.add, [yb, modD], [("hxT", bi)])
        P.barrier()
        P.sb_reset(m0)

    def gemm_fm(loaders, kcs, actT, akey, blocks, evac, wt, perm_wt=None, ncs=4):
        for gi, load in enumerate(loaders):
            slab = wt[gi % 2]
            load(slab)
            pslab = None
            if perm_wt is not None and perm_wt[0](gi):
                pslab = perm_wt[1][gi % 2]
                v = slab[:].rearrange("p k (g b i) -> p k g b i", b=2, i=16)
                vp = pslab[:].rearrange("p k (g b i) -> p k g b i", b=2, i=16)
                for kh in range(2):
                    ksl = slice(kh * (kcs // 2), (kh + 1) * (kcs // 2))
                    cp("pool", vp[:, ksl, :, 0, :], v[:, ksl, :, 1, :], [slab], [(pslab, kh, 0)])
                    cp("pool", vp[:, ksl, :, 1, :], v[:, ksl, :, 0, :], [slab], [(pslab, kh, 1)])
            for bi, (t0, n) in enumerate(blocks):
                for c in range(ncs):
                    pa = bb.gbank()
                    for kc in range(kcs):
                        mm(pa[:, :n], slab[:, kc, c * 128:(c + 1) * 128], actT[:, kc, t0:t0 + n], kc == 0, kc == kcs - 1,
                           [slab, akey(bi)], [pa])
                    pb = None
                    if pslab is not None:
                        pb = bb.gbank()
                        rk = [(pslab, kh, q) for kh in range(2) for q in range(2)]
                        for kc in range(kcs):
                            mm(pb[:, :n], pslab[:, kc, c * 128:(c + 1) * 128], actT[:, kc, t0:t0 + n], kc == 0,
                               kc == kcs - 1, rk + [akey(bi)], [pb])
                    evac(gi, c, bi, t0, n, pa, pb)

    def proj_phase(l, hxT):
        m0 = P.sb_mark()
        wt = [P.sb([128, KC, 512], BF16, f"wt{i}") for i in range(2)]
        wp = [P.sb([128, KC, 512], BF16, f"wp{i}") for i in range(2)]
        cosT = P.sb([128, T], F32, "cosT")
        sinT = P.sb([128, T], F32, "sinT")
        dma("sp", cosT[:], c_cos, (), [cosT])
        dma("sp", sinT[:], c_sin, (), [sinT])
        ev = [P.sb([128, 512], F32, f"ev{i}") for i in range(4)]
        evb = [P.sb([128, 512], BF16, f"evb{i}") for i in range(4)]
        st = {"i": 0}
        W = w_in[l]

        def ld(c0):
            def f(slab):
                dma("pool", slab[:], wslab(W, 0, KC, c0, 512), (), [slab])
            return f

        def evac(gi_abs):
            def f(gi, c, bi, t0, n, pa, pb):
                i = st["i"] % 4
                st["i"] += 1
                ch = gi_abs * 4 + c
                if gi_abs < 4:
                    tt("dve", ev[i][:, :n], pa[:, :n], cosT[:, t0:t0 + n], ALU.mult, [pa, cosT], [ev[i]])
                    j = (i + 1) % 4
                    st["i"] += 1
                    tt("dve", ev[j][:, :n], pb[:, :n], sinT[:, t0:t0 + n], ALU.mult, [pb, sinT], [ev[j]])
                    tt("pool", evb[i][:, :n], ev[i][:, :n], ev[j][:, :n], ALU.add, [ev[i], ev[j]], [evb[i]])
                    dma("sp", QKT[ch, :, t0:t0 + n], evb[i][:, :n], [evb[i]], [("QKT", ch)])
                elif gi_abs < 9:
                    cp("act", ev[i][:, :n], pa[:, :n], [pa], [ev[i]])
                    dma("sp", HYT[ch - 24, :, t0:t0 + n], ev[i][:, :n], [ev[i]], [("HYT", ch - 24)])
                elif gi_abs < 10:
                    cp("act", ev[i][:, :n], pa[:, :n], [pa], [ev[i]])
                    dma("sp", PLT[ch - 36, :, t0:t0 + n], ev[i][:, :n], [ev[i]], [("PLT", ch - 36)])
                else:
                    act(evb[i][:, :n], pa[:, :n], AF.Sigmoid, [pa], [evb[i]])
                    dma("sp", GT[ch - 40, :, t0:t0 + n], evb[i][:, :n], [evb[i]], [("GT", ch - 40)])
            return f

        akey = lambda bi: ("hxT", bi)
        for gi_abs in list(range(0, 4)) + list(range(6, 22)):
            gemm_fm([ld(gi_abs * 512)], KC, hxT, akey, TB, evac(gi_abs), wt,
                    perm_wt=((lambda g: True), wp) if gi_abs < 4 else None)
            wt.reverse()
            wp.reverse()
        va = [P.sb([128, 8, 129], BF16, f"va{i}") for i in range(2)]
        for i in range(2):
            ms("dve", va[i][:, :, 128:129], 1.0, [va[i]])
        wv = [wt[0], wt[1]]
        for g in range(2):
            dma("pool", wv[g][:], wslab(W, 0, KC, 2048 + g * 512, 512), (), [wv[g]])
        for it in range(NT):
            b = it % 2
            bi = min(it // 4, 4)
            for g in range(2):
                pa = bb.gbank()
                for kc in range(KC):
                    mm(pa[:], hxT[:, kc, it * 128:(it + 1) * 128], wv[g][:, kc, :], kc == 0, kc == KC - 1,
                       [wv[g], ("hxT", bi)], [pa])
                cp("act" if g == 0 else "dve", va[b][:, g * 4:(g + 1) * 4, 0:128],
                   pa[:].rearrange("p (h e) -> p h e", h=4), [pa], [va[b]])
            dma("sp", VA[it], va[b][:], [va[b]], ["VA"])
        P.barrier()
        P.sb_reset(m0)

    def attn_phase(l):
        m0 = P.sb_mark()
        lam_init = 0.8 - 0.6 * math.exp(-0.3 * l)
        lp = P.sb([128, 4, 64], F32, "lp")
        pr = P.sb([128, 2, 64], F32, "pr")
        s12 = P.sb([128, 2], F32, "s12")
        nlam = P.sb([128, 1], F32, "nlam")
        gcol = P.sb([128, 1], F32, "gcol")
        dma("sp", lp[:].rearrange("p a d -> p (a d)"), diff_lam[l:l + 1, :].broadcast_to([128, 256]), (), [lp])
        dma("sp", gcol[:], subln[l:l + 1, :].rearrange("o e -> e o"), (), [gcol])
        ts("dve", gcol[:], gcol[:], float(1.0 - lam_init), None, ALU.mult, None, [gcol], [gcol])
        tt("dve", pr[:, 0, :], lp[:, 0, :], lp[:, 1, :], ALU.mult, [lp], [pr])
        tt("dve", pr[:, 1, :], lp[:, 2, :], lp[:, 3, :], ALU.mult, [lp], [pr])
        P.op("dve", lambda e: e.reduce_sum(out=s12[:], in_=pr[:], axis=AX.X), [pr], [s12])
        act(s12[:], s12[:], AF.Exp, [s12], [s12])
        tt("dve", nlam[:], s12[:, 1:2], s12[:, 0:1], ALU.subtract, [s12], [nlam])
        ts("dve", nlam[:], nlam[:], -lam_init, None, ALU.add, None, [nlam], [nlam])
        qT = [P.sb([128, T], BF16, f"qT{i}") for i in range(2)]
        kT = [P.sb([128, T], BF16, f"kT{i}") for i in range(2)]
        vh = [P.sb([128, NT, 128], BF16, f"vh{i}") for i in range(2)]
        qz = [[P.sb([128, T], BF16, f"qz{i}_{m}") for m in range(2)] for i in range(2)]
        for i in range(2):
            ms("pool", qz[i][0][64:128, :], 0.0, [qz[i][0]])
            ms("pool", qz[i][1][0:64, :], 0.0, [qz[i][1]])
        E = [P.sb([128, 512], BF16, f"E{i}") for i in range(4)]
        rd = [P.sb([128, 512], F32, f"rd{i}") for i in range(2)]
        SBK = [ps[0], ps[1]]
        NJUNK = 0
        om = [P.sb([128, 512], F32, f"om{i}") for i in range(2)]
        attf = P.sb([128, 512], F32, "attf")
        sqb = P.sb([128, 512], BF16, "sqb")
        rs_ = P.sb([128, 512], F32, "ars")
        atT = [P.sb([128, 512], BF16, f"atT{i}") for i in range(2)]
        state = {"ob": 0}
        oT = [ps[2], ps[3]]
        den = [ps[4], ps[5]]
        pstat = ps[6]

        def emit_load(h):
            hb = h % 2
            dma("sp", qT[hb][:], QKT[h], [("QKT", h)], [qT[hb]])
            dma("sp", kT[hb][:], QKT[8 + h], [("QKT", 8 + h)], [kT[hb]])
            dma("act", vh[hb][:], VA[:, :, h, 0:128].rearrange("i p e -> p i e"), ["VA"], [vh[hb]])
            cp("pool", qz[hb][0][0:64, :], qT[hb][0:64, :], [qT[hb]], [qz[hb][0]])
            cp("pool", qz[hb][1][64:128, :], qT[hb][64:128, :], [qT[hb]], [qz[hb][1]])

        steps = []
        for h in range(8):
            for bi, (q0, n) in enumerate(TB):
                kcs = list(range(NT)) if q0 < LX else [16, 17]
                for m in range(2):
                    for kc in kcs:
                        steps.append(dict(h=h, bi=bi, q0=q0, n=n, m=m, kc=kc, first=kc == kcs[0], last=kc == kcs[-1]))

        def emit_S(i):
            st_ = steps[i]
            hb, m, kc, q0, n = st_["h"] % 2, st_["m"], st_["kc"], st_["q0"], st_["n"]
            sb_ = SBK[i % 2]
            mm(sb_[:, :n], kT[hb][:, kc * 128:(kc + 1) * 128],
               qz[hb][m][:, q0:q0 + n], True, True, [kT[hb], qz[hb][m]], [sb_])

        def emit_PV(i):
            st_ = steps[i]
            h, m, kc, q0, n = st_["h"], st_["m"], st_["kc"], st_["q0"], st_["n"]
            hb = h % 2
            sb_ = SBK[i % 2]
            Et = E[i % 4]
            act(Et[:, :n], sb_[:, :n], AF.Exp, [sb_], [Et], scale=0.125)
            mm(oT[m][:, :n], vh[hb][:, kc, :], Et[:, :n], st_["first"], st_["last"], [Et, vh[hb]], [oT[m]])
            mm(den[m][:, :n], ones_b[:], Et[:, :n], st_["first"], st_["last"], [Et, ones_b], [den[m]])
            for _ in range(NJUNK):
                mm(ps[7][:, :n], ones_b[:], Et[:, :n], True, True, [], [])
            if not st_["last"]:
                return
            recip(rd[m][:, :n], den[m][:, :n], [den[m]], [rd[m]])
            tt("dve", om[m][:, :n], oT[m][:, :n], rd[m][:, :n], ALU.mult, [oT[m], rd[m]], [om[m]])
            if m == 0:
                return
            stt(attf[:, :n], om[1][:, :n], nlam[:, 0:1], om[0][:, :n], ALU.mult, ALU.add, [om[0], om[1], nlam], [attf])
            tt("pool", sqb[:, :n], attf[:, :n], attf[:, :n], ALU.mult, [attf], [sqb])
            mm(pstat[:, :n], ones_b[:], sqb[:, :n], True, True, [sqb, ones_b], [pstat])
            act(rs_[:, :n], pstat[:, :n], AF.Sqrt, [pstat], [rs_], bias=EPS, scale=1.0 / 128)
            recip(rs_[:, :n], rs_[:, :n], [rs_], [rs_])
            tt("dve", attf[:, :n], attf[:, :n], rs_[:, :n], ALU.mult, [attf, rs_], [attf])
            ob = state["ob"]
            state["ob"] += 1
            o_t = atT[ob % 2]
            ts("dve", o_t[:, :n], attf[:, :n], gcol[:, 0:1], None, ALU.mult, None, [attf, gcol], [o_t])
            dma("sp", BRT[h, :, q0:q0 + n], o_t[:, :n], [o_t], [("BRT", h)])

        emit_load(0)
        emit_S(0)
        for i in range(len(steps)):
            if i + 1 < len(steps):
                if steps[i + 1]["h"] != steps[i]["h"]:
                    emit_load(steps[i + 1]["h"])
                emit_S(i + 1)
            emit_PV(i)
        P.barrier()
        P.sb_reset(m0)

    def dwconv3(eng, out, src, w3, r, w):
        n = PW - 2
        ts(eng, out[:, 1:1 + n], src[:, 0:n], w3[:, 0:1], None, ALU.mult, None, r, w)
        if eng == "dve":
            stt(out[:, 1:1 + n], src[:, 1:1 + n], w3[:, 1:2], out[:, 1:1 + n], ALU.mult, ALU.add, r + w, w)
            stt(out[:, 1:1 + n], src[:, 2:2 + n], w3[:, 2:3], out[:, 1:1 + n], ALU.mult, ALU.add, r + w, w)
        else:
            raise NotImplementedError

    def load_padded(q, dst, src3, ch, w, rkeys):
        dma(q, dst[:, 1:1 + LX], src3[ch, :, 0:LX], rkeys, w)
        dma(q, dst[:, LX + 3:LX + 3 + LCX], src3[ch, :, LX:T], rkeys, w)

    def zero_pads(eng, t, w):
        ms(eng, t[:, 0:1], 0.0, w)
        ms(eng, t[:, LX + 1:LX + 3], 0.0, w)
        ms(eng, t[:, PW - 1:PW], 0.0, w)

    def filt_phase(l, sg):
        m0 = P.sb_mark()
        L, SC = sg["L"], sg["SC"]
        NB = 512 if L >= 512 else L
        w1 = P.sb([33, 64], F32, "w1")
        w2 = P.sb([64, 64], F32, "w2")
        w3 = P.sb([64, 2048], F32, "w3")
        pbf = P.sb([64, 4], F32, "pbf")
        zT = P.sb([33, L], F32, "zT")
        h1 = P.sb([64, L], F32, "h1")
        h2 = P.sb([64, L], F32, "h2")
        tq = P.sb([64, 512], F32, "tq")
        rq = P.sb([64, 512], F32, "rq")
        MAGIC = 12582912.0
        wsc = P.sb([128, 3, SC], F32, "wsc")
        dma("sp", w1[:], hw1[l], (), [w1])
        dma("sp", w2[:], hw2[l], (), [w2])
        dma("sp", w3[:], hw3[l], (), [w3])
        dma("sp", pbf[:], hpb[l], (), [pbf])
        dma("sp", zT[:], sg["zT"], (), [zT])
        dma("sp", wsc[:], sg["ws"], (), [wsc])
        OFF = math.pi + 16 * TWO_PI
        for (wm, kk, src, dst, ib, ifr) in ((w1, 33, zT, h1, 0, 1), (w2, 64, h1, h2, 2, 3)):
            for b0 in range(0, L, NB):
                pa = bb.gbank()
                mm(pa[0:64, :NB], wm[0:kk, :], src[0:kk, b0:b0 + NB], True, True, [wm, src], [pa])
                ts("dve", tq[:, :NB], pa[0:64, :NB], pbf[:, ib:ib + 1], pbf[:, ifr:ifr + 1], ALU.add, ALU.mult,
                   [pa, pbf], [tq])
                ts("dve", rq[:, :NB], tq[:, :NB], 1.0 / TWO_PI, MAGIC, ALU.mult, ALU.add, [tq], [rq])
                ts("dve", rq[:, :NB], rq[:, :NB], -MAGIC, -TWO_PI, ALU.add, ALU.mult, [rq], [rq])
                tt("dve", tq[:, :NB], tq[:, :NB], rq[:, :NB], ALU.add, [tq, rq], [tq])
                act(dst[:, b0:b0 + NB], tq[:, :NB], AF.Sin, [tq], [dst])
        dec = [P.sb([128, 512], F32, f"dec{i}") for i in range(2)]
        kr = [P.sb([128, 512], F32, f"kr{i}") for i in range(4)]
        ab = [P.sb([128, 512], F32, f"ab{i}") for i in range(4)]
        ksum = P.sb([128, SC, 1024], BF16, "ksum")
        kdif = P.sb([128, SC, 1024], BF16, "kdif")
        rn = [P.sb([128, 512], F32, f"rn{i}") for i in range(2)]
        psN = [ps[4], ps[5]]
        for lc in range(SC):
            d_ = dec[lc % 2]
            dma("sp", d_[:], sg["decay"][:, lc, :], (), [d_])
            for cb in range(4):
                pa = bb.gbank()
                mm(pa[:], h2[:, lc * 128:(lc + 1) * 128], w3[:, cb * 512:(cb + 1) * 512], True, True, [h2, w3], [pa])
                tt("dve", kr[cb][:], pa[:], d_[:], ALU.mult, [pa, d_], [kr[cb]])
                if lc == 0 and cb >= 2:
                    ms("dve", kr[cb][0:1, :], 0.0, [kr[cb]])
                act(ab[cb][:], kr[cb][:], AF.Abs, [kr[cb]], [ab[cb]])
            for o in range(2):
                for dr in range(2):
                    mm(psN[o][:], ones_f[:], ab[dr * 2 + o][:], lc == 0 and dr == 0, lc == SC - 1 and dr == 1,
                       [ab[dr * 2 + o], ones_f], [psN[o]])
                tt("pool", ksum[:, lc, o * 512:(o + 1) * 512], kr[o][:], kr[2 + o][:], ALU.add, [kr[o], kr[2 + o]],
                   [(ksum, lc, o)])
                tt("pool", kdif[:, lc, o * 512:(o + 1) * 512], kr[o][:], kr[2 + o][:], ALU.subtract,
                   [kr[o], kr[2 + o]], [(kdif, lc, o)])
        for o in range(2):
            ts("dve", rn[o][:], psN[o][:], EPS, None, ALU.add, None, [psN[o]], [rn[o]])
            recip(rn[o][:], rn[o][:], [rn[o]], [rn[o]])
        fcs = [P.sb([128, SC, 128], BF16, f"fcs{i}") for i in range(2)]
        fss = [P.sb([128, SC, 128], BF16, f"fss{i}") for i in range(2)]
        tab = [P.sb([128, 3, 512], F32, f"tab{i}") for i in range(2)]
        N = 2 * L
        it = 0
        for fc in range(SC):
            fb = fc % 2
            dma("sp", fcs[fb][:], sg["FW"][0, fc], (), [fcs[fb]])
            dma("act", fss[fb][:], sg["FW"][1, fc], (), [fss[fb]])
            for o in range(2):
                tb_ = tab[it % 2]
                it += 1
                pA = bb.gbank()
                pB = bb.gbank()
                ksk = [(ksum, lc, o) for lc in range(SC)]
                kdk = [(kdif, lc, o) for lc in range(SC)]
                for lc in range(SC):
                    mm(pA[:], fcs[fb][:, lc, :], ksum[:, lc, o * 512:(o + 1) * 512], lc == 0, lc == SC - 1,
                       [fcs[fb]] + ksk, [pA])
                for lc in range(SC):
                    mm(pB[:], fss[fb][:, lc, :], kdif[:, lc, o * 512:(o + 1) * 512], lc == 0, lc == SC - 1,
                       [fss[fb]] + kdk, [pB])
                stt(tb_[:, 0, :], pA[:], wsc[:, 0, fc:fc + 1], rn[o][:], ALU.mult, ALU.mult, [pA, wsc, rn[o]], [tb_])
                stt(tb_[:, 1, :], pB[:], wsc[:, 1, fc:fc + 1], rn[o][:], ALU.mult, ALU.mult, [pB, wsc, rn[o]], [tb_])
                stt(tb_[:, 2, :], pA[:], wsc[:, 2, fc:fc + 1], rn[o][:], ALU.mult, ALU.mult, [pA, wsc, rn[o]], [tb_])
                if fc == 0:
                    pC = bb.gbank()
                    for lc in range(SC):
                        mm(pC[0:1, :], fss[fb][:, lc, 0:1], ksum[:, lc, o * 512:(o + 1) * 512], lc == 0, lc == SC - 1,
                           [fss[fb]] + ksk, [pC])
                    stt(tb_[0:1, 2, :], pC[0:1, :], 1.0 / N, rn[o][0:1, :], ALU.mult, ALU.mult, [pC, rn[o], tb_], [tb_])
                dma("sp", sg["KTAB"][l, o, fc], tb_[:], [tb_], [("KTAB", sg["nm"])])
        P.barrier()
        P.sb_reset(m0)

    def hyena_phase(l):
        m0 = P.sb_mark()
        sw = P.sb([128, 12, 3], F32, "sw")
        hb = P.sb([128, 2, 4], F32, "hb")
        dma("sp", sw[:], hsw[l], (), [sw])
        dma("sp", hb[:], hbias[l], (), [hb])
        uT = P.sb([128, 4, PW], F32, "uT")
        mT = P.sb([128, 4, PW], F32, "mT")
        utok = P.sb([128, NT, 512], BF16, "utok")
        mk_a = P.sb_mark()
        raw = [P.sb([128, PW], F32, f"raw{i}") for i in range(2)]
        ubf = P.sb([128, 4, T], BF16, "ubf")
        P.sb_reset(mk_a)
        Y = P.sb([128, 32, 512], BF16, "Y")
        fcs = [P.sb([128, 16, 128], BF16, f"hfc{i}") for i in range(2)]
        fss = [P.sb([128, 16, 128], BF16, f"hfs{i}") for i in range(2)]
        tab = [P.sb([128, 3, 512], F32, f"htab{i}") for i in range(2)]
        gv = [P.sb([128, 32, 256], BF16, f"gv{i}") for i in range(2)]
        tm = [P.sb([128, 512], F32, f"htm{i}") for i in range(4)]
        ob = [P.sb([128, 256], BF16, f"hob{i}") for i in range(2)]
        ri = 0

        def conv_chunks(c0, dst):
            nonlocal ri
            for cc in range(4):
                r_ = raw[ri % 2]
                ri += 1
                load_padded("sp", r_, HYT, c0 + cc, [r_], [("HYT", c0 + cc)])
                dwconv3("dve", dst[:, cc, :], r_, sw[:, c0 + cc, :], [r_, sw], [(dst, cc)])

        ti = 0
        gi = 0
        for o in range(2):
            P.barrier()
            for i in range(2):
                zero_pads("dve", raw[i], [raw[i]])
            if o == 0:
                conv_chunks(0, uT)
            conv_chunks(4 + 4 * o, mT)
            for cc in range(4):
                cp("pool", ubf[:, cc, 0:LX], uT[:, cc, 1:1 + LX], [(uT, cc)], [(ubf, cc)])
                cp("pool", ubf[:, cc, LX:T], uT[:, cc, LX + 3:LX + 3 + LCX], [(uT, cc)], [(ubf, cc)])
            tpi = 0
            for cc in range(4):
                for s0 in range(0, NT, 8):
                    ns = min(8, NT - s0)
                    pt = ps[6 + tpi % 2]
                    tpi += 1
                    ptb = pt[:].bitcast(BF16)
                    for k in range(ns):
                        tr(ptb[:, k * 128:(k + 1) * 128], ubf[:, cc, (s0 + k) * 128:(s0 + k + 1) * 128],
                           [(ubf, cc), ident], [pt])
                    cp("act", utok[:, s0:s0 + ns, cc * 128:(cc + 1) * 128],
                       ptb[:, :ns * 128].rearrange("p (k c) -> p k c", c=128), [pt], [(utok, cc, s0)])
            ukeys = [(utok, cc, s0) for cc in range(4) for s0 in range(0, NT, 8)]
            P.barrier()
            for sg in segs:
                SC, L, tk0 = sg["SC"], sg["L"], sg["t0"] // 128
                for fc in range(SC):
                    fb = gi % 2
                    gi += 1
                    dma("sp", fcs[fb][:, :SC, :], sg["FW"][0, fc], (), [fcs[fb]])
                    dma("act", fss[fb][:, :SC, :], sg["FW"][1, fc], (), [fss[fb]])
                    dma("sp", tab[fb][:], sg["KTAB"][l, o, fc], [("KTAB", sg["nm"])], [tab[fb]])
                    pc = ps[4] if fc % 2 == 0 else ps[2]
                    pS = ps[5] if fc % 2 == 0 else ps[3]
                    for sc in range(SC):
                        mm(pc[:], fcs[fb][:, sc, :], utok[:, tk0 + sc, :], sc == 0, sc == SC - 1, [fcs[fb]] + ukeys, [pc])
                    for sc in range(SC):
                        mm(pS[:], fss[fb][:, sc, :], utok[:, tk0 + sc, :], sc == 0, sc == SC - 1, [fss[fb]] + ukeys, [pS])
                    a_, b_, c_, d_ = tm[0], tm[1], tm[2], tm[3]
                    tt("dve", a_[:], pc[:], tab[fb][:, 0, :], ALU.mult, [pc, tab[fb]], [a_])
                    tt("dve", b_[:], pS[:], tab[fb][:, 1, :], ALU.mult, [pS, tab[fb]], [b_])
                    tt("pool", Y[:, fc, :], a_[:], b_[:], ALU.subtract, [a_, b_], [(Y, fc)])
                    tt("dve", c_[:], pc[:], tab[fb][:, 1, :], ALU.mult, [pc, tab[fb]], [c_])
                    tt("dve", d_[:], pS[:], tab[fb][:, 2, :], ALU.mult, [pS, tab[fb]], [d_])
                    tt("pool", Y[:, SC + fc, :], c_[:], d_[:], ALU.add, [c_, d_], [(Y, SC + fc)])
                ykeys = [(Y, k) for k in range(2 * SC)]
                TBI = sg["TBI"]
                for tbk in range(L // TBI):
                    g_ = gv[ti % 2]
                    ti += 1
                    dma("sp", g_[:, :2 * SC, :], sg["GV"][tbk], (), [g_])
                    tt0 = sg["t0"] + tbk * TBI
                    p0 = pidx(tt0)
                    for cc in range(4):
                        pa = bb.gbank()
                        for k in range(2 * SC):
                            mm(pa[:, :TBI], Y[:, k, cc * 128:(cc + 1) * 128], g_[:, k, :], k == 0, k == 2 * SC - 1,
                               ykeys + [g_], [pa])
                        t_ = tm[(cc) % 4]
                        stt(t_[:, :TBI], uT[:, cc, p0:p0 + TBI], hb[:, o, cc:cc + 1], pa[:, :TBI], ALU.mult, ALU.add,
                            [(uT, cc), hb, pa], [t_])
                        if o == 0:
                            tt("pool", uT[:, cc, p0:p0 + TBI], t_[:, :TBI], mT[:, cc, p0:p0 + TBI], ALU.mult,
                               [t_, (mT, cc)], [(uT, cc)])
                        else:
                            o_ = ob[cc % 2]
                            tt("pool", o_[:, :TBI], t_[:, :TBI], mT[:, cc, p0:p0 + TBI], ALU.mult, [t_, (mT, cc)], [o_])
                            dma("sp", BRT[8 + cc, :, tt0:tt0 + TBI], o_[:, :TBI], [o_], [("BRT", 8 + cc)])
        P.barrier()
        P.sb_reset(m0)

    def pool_phase(l):
        m0 = P.sb_mark()
        PP = T + 32
        xo, co = 8, LX + 24
        pw = P.sb([128, 4, 128], BF16, "pw")
        psc = P.sb([128, 4], F32, "psc")
        corr = P.sb([128, 4, 16], F32, "corr")
        dma("pool", pw[:], pool_w[l].rearrange("g i o -> i g o"), (), [pw])
        dma("sp", psc[:], pscale[l], (), [psc])
        dma("sp", corr[:], c_corr, (), [corr])
        for g, win in enumerate((2, 4, 8, 16)):
            pin = P.sb([128, PP], F32, f"pin{g}")
            A_ = P.sb([128, PP], F32, f"pA{g}")
            B_ = P.sb([128, PP], F32, f"pB{g}")
            pm = P.sb([128, T], BF16, f"pm{g}")
            ms("pool", pin[:], 0.0, [pin])
            dma("sp", pin[:, xo:xo + LX], PLT[g, :, 0:LX], [("PLT", g)], [pin])
            dma("sp", pin[:, co:co + LCX], PLT[g, :, LX:T], [("PLT", g)], [pin])
            lo, hi = 8, PP - 8
            nn = hi - lo
            if win == 2:
                tt("dve", A_[:, lo:hi], pin[:, lo - 1:hi - 1], pin[:, lo:hi], ALU.add, [pin], [A_])
                S = A_
            else:
                tt("dve", A_[:, 0:PP - 1], pin[:, 0:PP - 1], pin[:, 1:PP], ALU.add, [pin], [A_])
                if win == 4:
                    tt("dve", B_[:, lo:hi], A_[:, lo - 2:hi - 2], A_[:, lo:hi], ALU.add, [A_], [B_])
                    S = B_
                else:
                    tt("dve", B_[:, 0:PP - 3], A_[:, 0:PP - 3], A_[:, 2:PP - 1], ALU.add, [A_], [B_])
                    if win == 8:
                        tt("dve", A_[:, lo:hi], B_[:, lo - 4:hi - 4], B_[:, lo:hi], ALU.add, [B_, A_], [A_])
                        S = A_
                    else:
                        tt("dve", A_[:, 0:PP - 7], B_[:, 0:PP - 7], B_[:, 4:PP - 3], ALU.add, [B_, A_], [A_])
                        tt("dve", B_[:, lo:hi], A_[:, lo - 8:hi - 8], A_[:, lo:hi], ALU.add, [A_, B_], [B_])
                        S = B_
            ts("dve", S[:, lo:hi], S[:, lo:hi], 1.0 / win, None, ALU.mult, None, [S], [S])
            for (o_, Ls) in ((xo, LX), (co, LCX)):
                tt("dve", S[:, o_:o_ + 8], S[:, o_:o_ + 8], corr[:, g, 0:8], ALU.mult, [S, corr], [S])
                tt("dve", S[:, o_ + Ls - 8:o_ + Ls], S[:, o_ + Ls - 8:o_ + Ls], corr[:, g, 8:16], ALU.mult, [S, corr], [S])
            tt("dve", pm[:, 0:LX], S[:, xo:xo + LX], pin[:, xo:xo + LX], ALU.subtract, [S, pin], [pm])
            tt("dve", pm[:, LX:T], S[:, co:co + LCX], pin[:, co:co + LCX], ALU.subtract, [S, pin], [pm])
            for bi, (t0, n) in enumerate(TB):
                pa = bb.gbank()
                mm(pa[:, :n], pw[:, g, :], pm[:, t0:t0 + n], True, True, [pw, pm], [pa])
                o_t = P.sb([128, 512], BF16, f"po{g}_{bi}")
                ts("dve", o_t[:, :n], pa[:, :n], psc[:, g:g + 1], None, ALU.mult, None, [pa, psc], [o_t])
                dma("sp", BRT[12 + g, :, t0:t0 + n], o_t[:, :n], [o_t], [("BRT", 12 + g)])
        P.barrier()
        P.sb_reset(m0)

    def merge_phase(l):
        m0 = P.sb_mark()
        brt = P.sb([128, 16, T], BF16, "brt")
        mgT = P.sb([128, 16, T], BF16, "mgT")
        wt = [P.sb([128, KC, 512], BF16, f"mwt{i}") for i in range(2)]
        g3 = [P.sb([128, 3, 512], BF16, f"g3{i}") for i in range(2)]
        t3 = [P.sb([128, 3, 512], F32, f"t3{i}") for i in range(2)]
        ev = [P.sb([128, 512], F32, f"mev{i}") for i in range(2)]
        for q4 in range(4):
            dma("sp", brt[:, q4 * 4:(q4 + 1) * 4, :], BRT[q4 * 4:(q4 + 1) * 4].rearrange("c p t -> p c t"),
                [("BRT", c) for c in range(q4 * 4, q4 * 4 + 4)], [(brt, q4)])
        GT4 = GT.rearrange("(b c) p t -> b c p t", b=3)
        it = 0
        for ng in range(4):
            slab = wt[ng % 2]
            dma("pool", slab[:, 0:8, :], wslab(w_att_o[l], 0, 8, ng * 512, 512), (), [(slab, 0)])
            dma("pool", slab[:, 8:12, :], wslab(w_hy_o[l], 0, 4, ng * 512, 512), (), [(slab, 1)])
            dma("pool", slab[:, 12:16, :], wslab(w_pool_o[l], 0, 4, ng * 512, 512), (), [(slab, 2)])
            for bi, (t0, n) in enumerate(TB):
                for c in range(4):
                    nch = ng * 4 + c
                    b = it % 2
                    it += 1
                    dma("sp", g3[b][:, :, :n], GT4[:, nch, :, t0:t0 + n].rearrange("b p t -> p b t"),
                        [("GT", br * 16 + nch) for br in range(3)], [g3[b]])
                    for br, (k0, k1) in enumerate(((0, 8), (8, 12), (12, 16))):
                        pa = bb.gbank()
                        q4s = [(brt, q) for q in ((0, 1) if br == 0 else (2,) if br == 1 else (3,))]
                        for kc in range(k0, k1):
                            mm(pa[:, :n], slab[:, kc, c * 128:(c + 1) * 128], brt[:, kc, t0:t0 + n], kc == k0, kc == k1 - 1,
                               [(slab, br)] + q4s, [pa])
                        tt("dve", t3[b][:, br, :n], pa[:, :n], g3[b][:, br, :n], ALU.mult, [pa, g3[b]], [(t3[b], br)])
                    tt("pool", t3[b][:, 0, :n], t3[b][:, 0, :n], t3[b][:, 1, :n], ALU.add, [(t3[b], 0), (t3[b], 1)],
                       [(t3[b], 0)])
                    tt("pool", mgT[:, nch, t0:t0 + n], t3[b][:, 0, :n], t3[b][:, 2, :n], ALU.add,
                       [(t3[b], 0), (t3[b], 2)], [(mgT, bi, nch)])
        P.barrier()

        def ld(c0):
            def f(slab):
                dma("pool", slab[:], wslab(w_out[l], 0, KC, c0, 512), (), [slab])
            return f
        st = {"i": 0}

        def evac(gi, c, bi, t0, n, pa, pb):
            i = st["i"] % 2
            st["i"] += 1
            cp("act" if i == 0 else "dve", ev[i][:, :n], pa[:, :n], [pa], [ev[i]])
            dma("sp", YT[gi * 4 + c, :, t0:t0 + n], ev[i][:, :n], [ev[i]], ["YT"])
        gemm_fm([ld(g * 512) for g in range(4)], KC, mgT, lambda bi: "nokey", TB, evac, wt)
        P.barrier()
        P.sb_reset(m0)

    def ffn_phase(l, h2T):
        m0 = P.sb_mark()
        cw = P.sb([128, FJ, 3], F32, "cw")
        dma("sp", cw[:], fcw[l], (), [cw])
        wg = [P.sb([128, KC, 256], BF16, f"wg{i}") for i in range(2)]
        wv = [P.sb([128, KC, 256], BF16, f"wv{i}") for i in range(2)]
        gp = [P.sb([128, PW], F32, f"gp{i}") for i in range(2)]
        vv = [P.sb([128, T], F32, f"vv{i}") for i in range(2)]
        cv = P.sb([128, PW], F32, "cv")
        x2 = P.sb([128, PW], F32, "x2")
        uu = P.sb([128, PW], F32, "uu")
        sg_ = P.sb([128, PW], F32, "sg")
        abf = [P.sb([128, T], BF16, f"abf{i}") for i in range(2)]
        for i in range(2):
            zero_pads("dve", gp[i], [gp[i]])
        n_ = PW - 2
        GC = 2.0 * math.sqrt(2.0 / math.pi)
        W = w_up[l]
        pending = []

        def chain(j, b):
            def c1():
                dwconv3("dve", cv, gp[b], cw[:, j, :], [gp[b], cw], [cv])

            def c2():
                act(x2[:, 1:1 + n_], cv[:, 1:1 + n_], AF.Square, [cv], [x2])
                ts("pool", x2[:, 1:1 + n_], x2[:, 1:1 + n_], 0.044715, 1.0, ALU.mult, ALU.add, [x2], [x2])
                tt("pool", uu[:, 1:1 + n_], x2[:, 1:1 + n_], cv[:, 1:1 + n_], ALU.mult, [x2, cv], [uu])

            def c3():
                act(sg_[:, 1:1 + n_], uu[:, 1:1 + n_], AF.Sigmoid, [uu], [sg_], scale=GC)
                tt("pool", uu[:, 1:1 + n_], cv[:, 1:1 + n_], sg_[:, 1:1 + n_], ALU.mult, [cv, sg_, uu], [uu])

            def c4():
                tt("dve", abf[b][:, 0:LX], uu[:, 1:1 + LX], vv[b][:, 0:LX], ALU.mult, [uu, vv[b]], [(abf[b], 0)])
                tt("dve", abf[b][:, LX:T], uu[:, LX + 3:LX + 3 + LCX], vv[b][:, LX:T], ALU.mult, [uu, vv[b]],
                   [(abf[b], 1)])
                dma("sp", AT[j], abf[b][:], [(abf[b], 0), (abf[b], 1)], ["AT"])
            return [c1, c2, c3, c4]

        for g2 in range(FJ // 2):
            sb_ = g2 % 2
            dma("pool", wg[sb_][:], wslab(W, 0, KC, g2 * 256, 256), (), [wg[sb_]])
            dma("pool", wv[sb_][:], wslab(W, 0, KC, FF + g2 * 256, 256), (), [wv[sb_]])
            for c in range(2):
                j = g2 * 2 + c
                b = j % 2
                for bi, (t0, n) in enumerate(TB):
                    pa = bb.gbank()
                    for kc in range(KC):
                        mm(pa[:, :n], wg[sb_][:, kc, c * 128:(c + 1) * 128], h2T[:, kc, t0:t0 + n], kc == 0, kc == KC - 1,
                           [wg[sb_]], [pa])
                    cp("act", gp[b][:, pidx(t0):pidx(t0) + n], pa[:, :n], [pa], [gp[b]])
                    pb = bb.gbank()
                    for kc in range(KC):
                        mm(pb[:, :n], wv[sb_][:, kc, c * 128:(c + 1) * 128], h2T[:, kc, t0:t0 + n], kc == 0, kc == KC - 1,
                           [wv[sb_]], [pb])
                    cp("dve", vv[b][:, t0:t0 + n], pb[:, :n], [pb], [vv[b]])
                    if pending:
                        pending.pop(0)()
                pending.extend(chain(j, b))
        for f_ in pending:
            f_()
        P.barrier()
        P.sb_reset(m0)

    def down_phase(l):
        m0 = P.sb_mark()
        HT = T // 2
        aT = P.sb([128, FJ, HT], BF16, "aT")
        wd = [P.sb([128, FJ, 256], BF16, f"wd{i}") for i in range(2)]
        ev = [P.sb([128, 512], F32, f"dev{i}") for i in range(2)]
        st = {"i": 0}
        for half in range(2):
            h0 = half * HT
            for q4 in range(4):
                dma("sp", aT[:, q4 * 11:(q4 + 1) * 11, :], blk(AT, q4 * 11, (q4 + 1) * 11, h0, HT), ["AT"], [aT])

            def ld(c0):
                def f(slab):
                    dma("pool", slab[:], w_down[l][:, c0:c0 + 256].rearrange("(j p) n -> p j n", p=128), (), [slab])
                return f

            def evac(gi, c, bi, t0, n, pa, pb, h0=h0):
                i = st["i"] % 2
                st["i"] += 1
                cp("act" if i == 0 else "dve", ev[i][:, :n], pa[:, :n], [pa], [ev[i]])
                dma("sp", YT[gi * 2 + c, :, h0 + t0:h0 + t0 + n], ev[i][:, :n], [ev[i]], ["YT"])
            gemm_fm([ld(g * 256) for g in range(8)], FJ, aT, lambda bi: aT, [(0, 512), (512, 512), (1024, 128)], evac, wd,
                    ncs=2)
            P.barrier()
        P.sb_reset(m0)

    ada_phase()
    for l in range(nlayers):
        for sg in segs:
            filt_phase(l, sg)
    mk = P.sb_mark()
    hxT = P.sb([128, KC, T], BF16, "hxT")
    norm_pass(0, 0, 1, hxT)
    for l in range(nlayers):
        proj_phase(l, hxT)
        P.sb_reset(mk)
        if stop_after == "proj":
            break
        attn_phase(l)
        hyena_phase(l)
        pool_phase(l)
        if stop_after == "branches":
            break
        merge_phase(l)
        mk = P.sb_mark()
        hxT = P.sb([128, KC, T], BF16, "h2T")
        resid_norm_pass(l, 2, l, 3, 4, hxT)
        if stop_after == "mixer":
            break
        ffn_phase(l, hxT)
        P.sb_reset(mk)
        down_phase(l)
        if l + 1 < nlayers:
            mk = P.sb_mark()
            hxT = P.sb([128, KC, T], BF16, "hxT")
            resid_norm_pass(l, 5, l + 1, 0, 1, hxT)
        else:
            resid_pass(l, 5)
    dma("sp", outT, XT[:, :, 0:LX], ["XT"], ["outT"])
    P.barrier()
    return P.emit(), P


def _bf16(a):
    return np.asarray(a, dtype=np.float32).astype(ml_dtypes.bfloat16)


def _dft_consts(L):
    N = 2 * L
    SC = L // 128
    ct = np.cos(2.0 * np.pi * np.arange(N) / N)
    st = np.sin(2.0 * np.pi * np.arange(N) / N)
    s = np.arange(L)
    idx = (s[:, None] * s[None, :]) % N
    Mc = ct[idx]
    Ms = st[idx]
    Fs = Ms.copy()
    Fs[:, 0] = (-1.0) ** s
    Gs = Ms.copy()
    Gs[0, :] = (-1.0) ** s
    def fw(M):
        return M.reshape(SC, 128, SC, 128).transpose(2, 1, 0, 3)
    FW = np.stack([fw(Mc), fw(Fs)], axis=0)
    TBI = 256
    def gv(M):
        return M.reshape(SC, 128, L // TBI, TBI).transpose(2, 1, 0, 3)
    GV = np.concatenate([gv(Mc), gv(Gs)], axis=2)
    ws = np.full((128, 3, SC), 2.0 / N, dtype=np.float32)
    ws[0, 0, 0] = 1.0 / N
    ws[0, 1, 0] = 0.0
    f32 = np.float32
    t = np.linspace(0.0, 1.0, L, dtype=f32)[:, None]
    w_ang = (f32(2.0 * math.pi) * np.arange(L, dtype=f32)[:, None] / f32(L)).astype(f32)
    fb = np.linspace(1e-4, 15, 16, dtype=f32)[None, :]
    ang = (fb * w_ang).astype(f32)
    z = np.concatenate([t, np.cos(ang), -np.sin(ang)], axis=-1).astype(f32)
    zT = np.ascontiguousarray(z.T)
    dmin = math.log(1e-2) / 1.5
    dmax = math.log(1e-2) / 0.3
    deltas = np.abs(np.linspace(dmin, dmax, 512, dtype=f32))
    decay = np.exp(-t * deltas[None, :]).astype(f32)
    decay = np.ascontiguousarray(decay.reshape(SC, 128, 512).transpose(1, 0, 2))
    return dict(FW=_bf16(np.ascontiguousarray(FW)), GV=_bf16(np.ascontiguousarray(GV)), ws=ws, zT=zT, decay=decay)


def _rope_consts():
    f32 = np.float32
    inv = (f32(10000.0) ** (-np.arange(16, dtype=f32) / f32(16))).astype(f32)
    tt_ = np.arange(LX)
    row = (tt_ // 64).astype(f32)
    col = (tt_ % 64).astype(f32)
    cosT = np.ones((128, T), dtype=f32)
    sinT = np.zeros((128, T), dtype=f32)
    for m in range(2):
        for a in range(2):
            pos = row if a == 0 else col
            ang = (pos[None, :] * inv[:, None]).astype(f32)
            for b in range(2):
                p0 = m * 64 + a * 32 + b * 16
                cosT[p0:p0 + 16, :LX] = np.cos(ang)
                sinT[p0:p0 + 16, :LX] = (-1.0 if b == 0 else 1.0) * np.sin(ang)
    return cosT, sinT


def _pool_corr():
    corr = np.ones((128, 4, 16), dtype=np.float32)
    for g, win in enumerate((2, 4, 8, 16)):
        h = win // 2
        for pos in range(8):
            cnt = min(pos, h) + h
            corr[:, g, pos] = win / cnt
            rem = 8 - pos
            cnt2 = h + min(h, rem)
            corr[:, g, 8 + pos] = win / cnt2
    return corr


_CACHE = {}


def _consts():
    if "c" not in _CACHE:
        cx = _dft_consts(LX)
        cc = _dft_consts(LCX)
        cosT, sinT = _rope_consts()
        m = {"c_ident": np.eye(128, dtype=np.float32), "c_cos": cosT, "c_sin": sinT, "c_corr": _pool_corr()}
        for nm, c in (("x", cx), ("c", cc)):
            m[f"c_fw_{nm}"] = c["FW"]
            m[f"c_gv_{nm}"] = c["GV"]
            m[f"c_zT_{nm}"] = c["zT"]
            m[f"c_decay_{nm}"] = c["decay"]
            m[f"c_ws_{nm}"] = c["ws"]
        _CACHE["c"] = m
    return _CACHE["c"]


def make_in_maps(inputs, ncore=NCORE):
    f = lambda a: np.ascontiguousarray(np.asarray(a, dtype=np.float32))
    I = {k: np.asarray(v) for k, v in inputs.items()}
    shared = dict(_consts())
    shared["w_ada"] = f(I["w_ada"])
    shared["b_ada"] = f(I["b_ada"])
    shared["gT"] = f(I["norm_g"].reshape(DEPTH, 4, KC, 128).transpose(0, 3, 1, 2))
    shared["w_in"] = f(I["w_in"])
    shared["diff_lam"] = f(I["diff_lam"].reshape(DEPTH, 256))
    shared["attn_subln_g"] = f(I["attn_subln_g"])
    shared["hsw"] = f(I["hy_short_w"].reshape(DEPTH, 3, 12, 128).transpose(0, 3, 2, 1))
    shared["hy_ffn_w1"] = f(I["hy_ffn_w1"])
    shared["hy_ffn_w2"] = f(I["hy_ffn_w2"])
    shared["hy_ffn_w3"] = f(I["hy_ffn_w3"])
    shared["hpb"] = f(np.stack([I["hy_ffn_b1"], I["hy_freq"][:, 0], I["hy_ffn_b2"], I["hy_freq"][:, 1]], axis=-1))
    shared["hbias"] = f(I["hy_bias"].reshape(DEPTH, 2, 4, 128).transpose(0, 3, 1, 2))
    shared["pool_w"] = f(I["pool_w"])
    shared["pscale"] = f(I["pool_scale"].reshape(DEPTH, 4, 128).transpose(0, 2, 1))
    shared["w_att_o"] = f(I["w_att_o"])
    shared["w_hy_o"] = f(I["w_hy_o"])
    shared["w_pool_o"] = f(I["w_pool_o"])
    shared["w_out"] = f(I["w_out"])
    shared["w_up"] = f(I["w_up"])
    shared["fcw"] = f(I["ff_conv_w"].reshape(DEPTH, 3, FJ, 128).transpose(0, 3, 2, 1))
    shared["w_down"] = f(I["w_down"])
    maps = []
    for b in range(ncore):
        X = np.concatenate([I["x"][b], I["ctx"][b]], axis=0)
        m = dict(shared)
        m["xT"] = f(X.T.reshape(KC, 128, T))
        cc = np.stack([I["c"][b], I["c_ctx"]], axis=0)
        m["ccT"] = f(cc.reshape(2, KC, 128).transpose(2, 1, 0))
        maps.append(m)
    return maps


def kernel(**inputs):
    if "nc" not in _CACHE:
        _CACHE["nc"] = build()[0]
    nc = _CACHE["nc"]
    maps = make_in_maps(inputs)
    res = run_bass_kernel_spmd(nc, maps, core_ids=list(range(NCORE)))
    outs = []
    for b in range(NCORE):
        oT = np.asarray(res.results[b]["outT"], dtype=np.float32)
        outs.append(np.ascontiguousarray(oT.reshape(D, LX).T))
    return np.stack(outs, axis=0)
```

```python
import numpy as np
import concourse.bass as bass
import concourse.mybir as mybir

F32 = mybir.dt.float32
BF16 = mybir.dt.bfloat16
AF = mybir.ActivationFunctionType
ALU = mybir.AluOpType
AX = mybir.AxisListType

ENGS = ("sp", "act", "dve", "pool", "pe")
NSLOT = 12
SEM_CAP = 30000


class _Op:
    __slots__ = ("fn", "deps", "dma", "signal", "sigcount", "slot", "val", "prev", "semidx")

    def __init__(self, fn, deps, dma):
        self.fn = fn
        self.deps = deps
        self.dma = dma
        self.signal = False
        self.sigcount = 0
        self.slot = 0
        self.val = 0
        self.prev = None
        self.semidx = 0


def _key(x):
    if isinstance(x, tuple):
        return tuple(_key(y) for y in x)
    if isinstance(x, (str, int)):
        return x
    return ("id", id(x))


class Prog:
    def __init__(self, same_engine_sync=True):
        self.nc = bass.Bass("TRN2", target_bir_lowering=False)
        self.ops = {e: [] for e in ENGS}
        self.lastw = {}
        self.readers = {}
        self.same = same_engine_sync
        self.sb_off = 16384
        self.sb_hw = 0
        self.ndma = {e: 0 for e in ENGS}
        self.slot_last = {}
        self._n = 0

    def sb(self, shape, dt, name=None):
        self._n += 1
        name = name or f"sb{self._n}"
        esz = 4 if dt == F32 else 2
        if dt in (mybir.dt.int32, mybir.dt.uint32):
            esz = 4
        per_part = int(np.prod(shape[1:])) * esz
        per_part = (per_part + 63) // 64 * 64
        t = self.nc.alloc_sbuf_tensor_at(f"{name}_{self._n}", list(shape), dt, offset=self.sb_off)
        self.sb_off += per_part
        self.sb_hw = max(self.sb_hw, self.sb_off)
        assert self.sb_off <= 16384 + 212000, f"SBUF overflow {self.sb_off}"
        return t

    def sb_mark(self):
        return self.sb_off

    def sb_reset(self, mark):
        self.sb_off = mark

    def dram(self, name, shape, dt, kind="Internal"):
        return self.nc.dram_tensor(name, list(shape), dt, kind=kind)

    def op(self, eng, fn, reads=(), writes=(), dma=False):
        reads = [_key(r) for r in reads]
        writes = [_key(w) for w in writes]
        deps = set()
        for r in reads:
            lw = self.lastw.get(r)
            if lw is not None:
                deps.add(lw)
        for w in writes:
            lw = self.lastw.get(w)
            if lw is not None:
                deps.add(lw)
            for rd in self.readers.get(w, ()):
                deps.add(rd)
        idx = len(self.ops[eng])
        me = (eng, idx)
        o = _Op(fn, None, dma)
        fdeps = []
        for d in deps:
            if d == me:
                continue
            tgt = self.ops[d[0]][d[1]]
            if d[0] == eng and not tgt.dma:
                if eng == "pe" or not self.same:
                    continue
            fdeps.append(d)
        best = {}
        pruned = []
        for d in fdeps:
            if self.ops[d[0]][d[1]].dma:
                pruned.append(d)
            elif d[0] not in best or d[1] > best[d[0]]:
                best[d[0]] = d[1]
        pruned.extend(best.items())
        o.deps = pruned
        if dma:
            j = self.ndma[eng]
            self.ndma[eng] += 1
            o.slot = j % NSLOT
            o.val = 16 * (j // NSLOT + 1)
            o.prev = self.slot_last.get((eng, o.slot))
            self.slot_last[(eng, o.slot)] = me
        self.ops[eng].append(o)
        for w in writes:
            self.lastw[w] = me
            self.readers[w] = []
        for r in reads:
            self.readers.setdefault(r, []).append(me)
        return me

    def barrier(self):
        lasts = []
        for e in ENGS:
            if self.ops[e]:
                for i in range(len(self.ops[e]) - 1, -1, -1):
                    if not self.ops[e][i].dma and self.ops[e][i].fn is not None:
                        lasts.append((e, i))
                        break
            for s in range(NSLOT):
                l = self.slot_last.get((e, s))
                if l is not None:
                    lasts.append(l)
        self._barrier_deps = lasts
        for e in ENGS:
            o = _Op(None, [d for d in lasts if not (d[0] == e and not self.ops[d[0]][d[1]].dma and e == "pe")], False)
            self.ops[e].append(o)
        self.lastw = {}
        self.readers = {}

    def emit(self):
        nc = self.nc
        for e in ENGS:
            for o in self.ops[e]:
                for (e2, i2) in o.deps:
                    t = self.ops[e2][i2]
                    if not t.dma:
                        t.signal = True
        nsem = {}
        for e in ENGS:
            c = 0
            for o in self.ops[e]:
                if o.dma or o.fn is None:
                    continue
                if o.signal:
                    c += 1
                    o.semidx = (c - 1) // SEM_CAP
                    o.sigcount = c - o.semidx * SEM_CAP
            nsem[e] = max(1, (c + SEM_CAP - 1) // SEM_CAP)
        from contextlib import ExitStack
        with ExitStack() as es:
            csem = {e: [es.enter_context(nc.semaphore(f"c_{e}_{k}")) for k in range(nsem[e])] for e in ENGS}
            dsem = {e: [es.enter_context(nc.semaphore(f"d_{e}_{s}")) for s in range(NSLOT)]
                    for e in ENGS if self.ndma[e] > 0}
            block = es.enter_context(nc.Block())
            ops = self.ops

            def replay(e, eng):
                waited = {}

                def wait_for(d):
                    t = ops[d[0]][d[1]]
                    if t.dma:
                        key = ("d", d[0], t.slot)
                        sem = dsem[d[0]][t.slot]
                        val = t.val
                    else:
                        if t.fn is None:
                            return
                        key = ("c", d[0], t.semidx)
                        sem = csem[d[0]][t.semidx]
                        val = t.sigcount
                    if waited.get(key, 0) < val:
                        eng.wait_ge(sem, val)
                        waited[key] = val

                for o in ops[e]:
                    for d in o.deps:
                        wait_for(d)
                    if o.fn is None:
                        continue
                    if o.dma:
                        if o.prev is not None:
                            wait_for(o.prev)
                        inst = o.fn(eng)
                        inst.then_inc(dsem[e][o.slot], 16)
                    else:
                        inst = o.fn(eng)
                        if o.signal:
                            inst.then_inc(csem[e][o.semidx], 1)

            @block.sync
            def _(eng):
                replay("sp", eng)

            @block.scalar
            def _(eng):
                replay("act", eng)

            @block.vector
            def _(eng):
                replay("dve", eng)

            @block.gpsimd
            def _(eng):
                replay("pool", eng)

            @block.tensor
            def _(eng):
                replay("pe", eng)
        return nc


import math
import ml_dtypes
from concourse.bass_utils import run_bass_kernel_spmd

D = 2048
KC = 16
LX = 2048
LCX = 256
T = LX + LCX
NT = T // 128
TB = [(0, 512), (512, 512), (1024, 512), (1536, 512), (2048, 256)]
PW = T + 4
IN_W = 11264
FF = 5632
FJ = FF // 128
DEPTH = 4
EPS = 1e-6
NCORE = 4
TWO_PI = 2.0 * math.pi


def pidx(t):
    return t + 1 if t < LX else t + 3


class B:
    def __init__(self, dump=()):
        self.P = Prog()
        self.nc = self.P.nc
        self.dump = set(dump)
        nc = self.nc
        self.ps = [nc.alloc_psum_tensor(f"psb{i}", [128, 512], F32) for i in range(8)]
        self.rot = 0
        self.ident = None

    def din(self, name, shape, dt=F32):
        return self.P.dram(name, shape, dt, kind="ExternalInput").ap()

    def dsc(self, name, shape, dt=F32):
        kind = "ExternalOutput" if name in self.dump else "Internal"
        return self.P.dram(name, shape, dt, kind=kind).ap()

    def gbank(self):
        b = self.ps[self.rot % 4]
        self.rot += 1
        return b

    def mm(self, out, lhsT, rhs, start, stop, r, w, skip=False):
        if skip:
            self.P.op("pe", lambda e: e.matmul(out, lhsT=lhsT, rhs=rhs, start=start, stop=stop,
                                               skip_group_check=True), r, w)
        else:
            self.P.op("pe", lambda e: e.matmul(out, lhsT=lhsT, rhs=rhs, start=start, stop=stop), r, w)

    def tr(self, out, in_, r, w, ident=None):
        idn = self.ident if ident is None else ident
        self.P.op("pe", lambda e: e.transpose(out=out, in_=in_, identity=idn), r, w)

    def act(self, out, in_, func, r, w, bias=0.0, scale=1.0, accum=None):
        if accum is None:
            self.P.op("act", lambda e: e.activation(out=out, in_=in_, func=func, bias=bias, scale=scale), r, w)
        else:
            self.P.op("act", lambda e: e.activation(out=out, in_=in_, func=func, bias=bias, scale=scale,
                                                    accum_out=accum), r, w)

    def tt(self, eng, out, in0, in1, op, r, w):
        self.P.op(eng, lambda e: e.tensor_tensor(out=out, in0=in0, in1=in1, op=op), r, w)

    def ts(self, eng, out, in0, s1, s2, op0, op1, r, w):
        if s2 is None:
            self.P.op(eng, lambda e: e.tensor_scalar(out=out, in0=in0, scalar1=s1, scalar2=None, op0=op0), r, w)
        else:
            self.P.op(eng, lambda e: e.tensor_scalar(out=out, in0=in0, scalar1=s1, scalar2=s2, op0=op0, op1=op1), r, w)

    def stt(self, out, in0, scalar, in1, op0, op1, r, w):
        self.P.op("dve", lambda e: e.scalar_tensor_tensor(out=out, in0=in0, scalar=scalar, in1=in1, op0=op0, op1=op1),
                  r, w)

    def cp(self, eng, out, in_, r, w):
        if eng == "act":
            self.P.op("act", lambda e: e.copy(out=out, in_=in_), r, w)
        else:
            self.P.op(eng, lambda e: e.tensor_copy(out=out, in_=in_), r, w)

    def ms(self, eng, ap, val, w):
        self.P.op(eng, lambda e: e.memset(ap, val), (), w)

    def recip(self, out, in_, r, w):
        self.P.op("dve", lambda e: e.reciprocal(out=out, in_=in_), r, w)

    def dma(self, q, out, in_, r, w):
        self.P.op(q, lambda e: e.dma_start(out=out, in_=in_), r, w, dma=True)


def blk(ap3, c0, c1, t0, n):
    return ap3[c0:c1, :, t0:t0 + n].rearrange("c p t -> p c t")


def wslab(w2d, k0, kc, c0, ncol):
    return w2d[k0 * 128:(k0 + kc) * 128, c0:c0 + ncol].rearrange("(kc p) n -> p kc n", p=128)


def build(nlayers=DEPTH, dump=(), stop_after=None):
    bb = B(dump)
    P = bb.P
    nc = bb.nc
    ps = bb.ps
    mm, tr, act, tt, ts, stt, cp, ms, recip, dma = bb.mm, bb.tr, bb.act, bb.tt, bb.ts, bb.stt, bb.cp, bb.ms, bb.recip, bb.dma
    din, dsc = bb.din, bb.dsc

    xT_in = din("xT", [KC, 128, T])
    ccT = din("ccT", [128, KC, 2])
    w_ada = din("w_ada", [DEPTH, D, 6 * D])
    b_ada = din("b_ada", [DEPTH, 6 * D])
    gT = din("gT", [DEPTH, 128, 4, KC])
    w_in = din("w_in", [DEPTH, D, IN_W])
    diff_lam = din("diff_lam", [DEPTH, 4 * 64])
    subln = din("attn_subln_g", [DEPTH, 128])
    hsw = din("hsw", [DEPTH, 128, 12, 3])
    hw1 = din("hy_ffn_w1", [DEPTH, 33, 64])
    hw2 = din("hy_ffn_w2", [DEPTH, 64, 64])
    hw3 = din("hy_ffn_w3", [DEPTH, 64, 2048])
    hpb = din("hpb", [DEPTH, 64, 4])
    hbias = din("hbias", [DEPTH, 128, 2, 4])
    pool_w = din("pool_w", [DEPTH, 4, 128, 128])
    pscale = din("pscale", [DEPTH, 128, 4])
    w_att_o = din("w_att_o", [DEPTH, 1024, D])
    w_hy_o = din("w_hy_o", [DEPTH, 512, D])
    w_pool_o = din("w_pool_o", [DEPTH, 512, D])
    w_out = din("w_out", [DEPTH, D, D])
    w_up = din("w_up", [DEPTH, D, 2 * FF])
    fcw = din("fcw", [DEPTH, 128, FJ, 3])
    w_down = din("w_down", [DEPTH, FF, D])
    c_ident = din("c_ident", [128, 128])
    c_cos = din("c_cos", [128, T])
    c_sin = din("c_sin", [128, T])
    c_corr = din("c_corr", [128, 4, 16])
    segs = []
    for nm, L in (("x", LX), ("c", LCX)):
        SC = L // 128
        TBI = 256
        segs.append(dict(
            nm=nm, L=L, SC=SC, t0=0 if nm == "x" else LX, TBI=TBI,
            FW=din(f"c_fw_{nm}", [2, SC, 128, SC, 128], BF16),
            GV=din(f"c_gv_{nm}", [L // TBI, 128, 2 * SC, TBI], BF16),
            zT=din(f"c_zT_{nm}", [33, L]),
            decay=din(f"c_decay_{nm}", [128, SC, 512]),
            ws=din(f"c_ws_{nm}", [128, 3, SC]),
            KTAB=dsc(f"KTAB_{nm}", [DEPTH, 2, SC, 128, 3, 512]),
        ))
    outT = P.dram("outT", [KC, 128, LX], F32, kind="ExternalOutput").ap()

    XT = dsc("XT", [KC, 128, T])
    MOD = dsc("MOD", [DEPTH, 2, 6 * D])
    QKT = dsc("QKT", [16, 128, T], BF16)
    VA = dsc("VA", [NT, 128, 8, 129], BF16)
    HYT = dsc("HYT", [12, 128, T])
    PLT = dsc("PLT", [4, 128, T])
    GT = dsc("GT", [48, 128, T], BF16)
    BRT = dsc("BRT", [16, 128, T], BF16)
    YT = dsc("YT", [KC, 128, T])
    AT = dsc("AT", [FJ, 128, T], BF16)

    identf = P.sb([128, 128], F32, "identf")
    ident = P.sb([128, 128], BF16, "ident")
    ones_b = P.sb([128, 128], BF16, "ones_b")
    ones_f = P.sb([128, 128], F32, "ones_f")
    modD = P.sb([128, DEPTH, 2, 6, KC], F32, "modD")
    bb.ident = ident[:]
    dma("sp", identf[:], c_ident, (), [identf])
    cp("dve", ident[:], identf[:], [identf], [ident])
    ms("dve", ones_b[:], 1.0, [ones_b])
    ms("dve", ones_f[:], 1.0, [ones_f])
    dma("sp", XT, xT_in, (), ["XT"])
    base_mark = P.sb_mark()

    def ada_phase():
        m0 = P.sb_mark()
        sc = P.sb([128, KC, 2], F32, "sc")
        NB_ = 5
        wa = [P.sb([128, KC, 512], F32, f"wa{i}") for i in range(NB_)]
        bt = [P.sb([2, 512], F32, f"bt{i}") for i in range(NB_)]
        mo = [P.sb([2, 512], F32, f"mo{i}") for i in range(NB_)]
        dma("sp", sc[:], ccT, (), [sc])
        act(sc[:], sc[:], AF.Silu, [sc], [sc])
        it = 0
        for l in range(nlayers):
            for ng in range(24):
                b = it % NB_
                it += 1
                dma("sp" if it % 2 == 0 else "act", wa[b][:], wslab(w_ada[l], 0, KC, ng * 512, 512), (), [wa[b]])
                dma("act", bt[b][:], b_ada[l:l + 1, ng * 512:(ng + 1) * 512].broadcast_to([2, 512]), (), [bt[b]])
                pb = ps[4 + b % 2]
                for kc in range(KC):
                    mm(pb[0:2, :], sc[:, kc, :], wa[b][:, kc, :], kc == 0, kc == KC - 1, [sc, wa[b]], [pb])
                tt("dve", mo[b][:], pb[0:2, :], bt[b][:], ALU.add, [pb, bt[b]], [mo[b]])
                dma("sp", MOD[l, :, ng * 512:(ng + 1) * 512], mo[b][:], [mo[b]], [("MOD", l)])
        P.barrier()
        P.sb_reset(m0)
        mr = [P.sb([96, 128], F32, f"mr{i}") for i in range(2)]
        mt = P.sb([128, 2, 6, KC], F32, "mt")
        g_sb = P.sb([128, 4, KC], F32, "g_sb")
        tmp = P.sb([128, KC], F32, "tmpm")
        it = 0
        for l in range(nlayers):
            dma("sp", g_sb[:], gT[l], (), [g_sb])
            for s in range(2):
                b = it % 2
                it += 1
                dma("sp", mr[b][:], MOD[l, s].rearrange("(r p) -> r p", p=128), (), [mr[b]])
                pb = ps[4 + b]
                tr(pb[:, 0:96], mr[b][:], [mr[b], identf], [pb], ident=identf[0:96, 0:96])
                cp("dve", mt[:, s].rearrange("p m j -> p (m j)"), pb[:, 0:96], [pb], [(mt, s)])
                ts("dve", tmp[:], mt[:, s, 1, :], 1.0, None, ALU.add, None, [(mt, s)], [tmp])
                tt("dve", modD[:, l, s, 0, :], tmp[:], g_sb[:, 0, :], ALU.mult, [tmp, g_sb], [modD])
                cp("dve", modD[:, l, s, 1, :], mt[:, s, 0, :], [(mt, s)], [modD])
                tt("dve", modD[:, l, s, 2, :], mt[:, s, 2, :], g_sb[:, 1, :], ALU.mult, [(mt, s), g_sb], [modD])
                ts("dve", tmp[:], mt[:, s, 4, :], 1.0, None, ALU.add, None, [(mt, s)], [tmp])
                tt("dve", modD[:, l, s, 3, :], tmp[:], g_sb[:, 2, :], ALU.mult, [tmp, g_sb], [modD])
                cp("dve", modD[:, l, s, 4, :], mt[:, s, 3, :], [(mt, s)], [modD])
                tt("dve", modD[:, l, s, 5, :], mt[:, s, 5, :], g_sb[:, 3, :], ALU.mult, [(mt, s), g_sb], [modD])
        P.barrier()
        P.sb_reset(m0)

    def stats_rstd(src, n, sq, rstd, pstat, rkeys, nchunks=KC, dim=D):
        act(sq[:, :, :n], src, AF.Square, rkeys, [sq])
        for j in range(nchunks):
            mm(pstat[:, :n], ones_b[:], sq[:, j, :n], j == 0, j == nchunks - 1, [sq, ones_b], [pstat])
        act(rstd[:, :n], pstat[:, :n], AF.Ln, [pstat], [rstd], bias=EPS, scale=1.0 / dim)
        act(rstd[:, :n], rstd[:, :n], AF.Exp, [rstd], [rstd], scale=-0.5)

    def seg_of(t0):
        return 0 if t0 < LX else 1

    def norm_pass(l, ia, ib, hxT):
        m0 = P.sb_mark()
        xb = [P.sb([128, KC, 512], F32, f"xb{i}") for i in range(2)]
        sq = P.sb([128, KC, 512], BF16, "sq")
        t1 = P.sb([128, KC, 512], F32, "t1")
        rstd = [P.sb([128, 512], F32, f"rstd{i}") for i in range(2)]
        for bi, (t0, n) in enumerate(TB):
            b = bi % 2
            s = seg_of(t0)
            dma("sp", xb[b][:, :, :n], blk(XT, 0, KC, t0, n), ["XT"], [xb[b]])
            stats_rstd(xb[b][:, :, :n], n, sq, rstd[b], ps[4 + b], [xb[b]])
            tt("dve", t1[:, :, :n], xb[b][:, :, :n], rstd[b][:, :n].unsqueeze(1).broadcast_to([128, KC, n]), ALU.mult,
               [xb[b], rstd[b]], [t1])
            tt("pool", t1[:, :, :n], t1[:, :, :n], modD[:, l, s, ia, :].unsqueeze(2).broadcast_to([128, KC, n]), ALU.mult,
               [t1, modD], [t1])
            tt("dve", hxT[:, :, t0:t0 + n], t1[:, :, :n], modD[:, l, s, ib, :].unsqueeze(2).broadcast_to([128, KC, n]),
               ALU.add, [t1, modD], [("hxT", bi)])
        P.barrier()
        P.sb_reset(m0)

    def resid_pass(l, ig):
        m0 = P.sb_mark()
        xb = [P.sb([128, KC, 512], F32, f"rxb{i}") for i in range(2)]
        yb = [P.sb([128, KC, 512], F32, f"ryb{i}") for i in range(2)]
        sq = P.sb([128, KC, 512], BF16, "rsq")
        rstd = [P.sb([128, 512], F32, f"rrstd{i}") for i in range(2)]
        for bi, (t0, n) in enumerate(TB):
            b = bi % 2
            s = seg_of(t0)
            dma("sp", xb[b][:, :, :n], blk(XT, 0, KC, t0, n), ["XT"], [xb[b]])
            dma("act", yb[b][:, :, :n], blk(YT, 0, KC, t0, n), ["YT"], [yb[b]])
            stats_rstd(yb[b][:, :, :n], n, sq, rstd[b], ps[4 + b], [yb[b]])
            tt("dve", yb[b][:, :, :n], yb[b][:, :, :n], rstd[b][:, :n].unsqueeze(1).broadcast_to([128, KC, n]), ALU.mult,
               [yb[b], rstd[b]], [yb[b]])
            tt("pool", yb[b][:, :, :n], yb[b][:, :, :n], modD[:, l, s, ig, :].unsqueeze(2).broadcast_to([128, KC, n]),
               ALU.mult, [yb[b], modD], [yb[b]])
            tt("dve", xb[b][:, :, :n], xb[b][:, :, :n], yb[b][:, :, :n], ALU.add, [xb[b], yb[b]], [xb[b]])
            dma("sp", blk(XT, 0, KC, t0, n), xb[b][:, :, :n], [xb[b]], ["XT"])
        P.barrier()
        P.sb_reset(m0)

    def resid_norm_pass(l, ig, l2, ia, ib, hxT):
        m0 = P.sb_mark()
        NB2 = 256
        xb = [P.sb([128, KC, NB2], F32, f"fxb{i}") for i in range(2)]
        yb = [P.sb([128, KC, NB2], F32, f"fyb{i}") for i in range(2)]
        sq = [P.sb([128, KC, NB2], BF16, f"fsq{i}") for i in range(2)]
        rstd = [P.sb([128, NB2], F32, f"frstd{i}") for i in range(4)]
        n = NB2
        for k in range(T // NB2):
            t0 = k * NB2
            bi = min(t0 // 512, 4)
            b = k % 2
            s_ = seg_of(t0)
            dma("sp", xb[b][:], blk(XT, 0, KC, t0, n), ["XT"], [xb[b]])
            dma("act", yb[b][:], blk(YT, 0, KC, t0, n), ["YT"], [yb[b]])
            stats_rstd(yb[b][:], n, sq[b], rstd[b], ps[4 + b], [yb[b]])
            tt("dve", yb[b][:], yb[b][:], rstd[b][:, :n].unsqueeze(1).broadcast_to([128, KC, n]), ALU.mult,
               [yb[b], rstd[b]], [yb[b]])
            tt("pool", yb[b][:], yb[b][:], modD[:, l, s_, ig, :].unsqueeze(2).broadcast_to([128, KC, n]),
               ALU.mult, [yb[b], modD], [yb[b]])
            tt("dve", xb[b][:], xb[b][:], yb[b][:], ALU.add, [xb[b], yb[b]], [xb[b]])
            dma("sp", blk(XT, 0, KC, t0, n), xb[b][:], [xb[b]], ["XT"])
            r2 = rstd[2 + b]
            stats_rstd(xb[b][:], n, sq[b], r2, ps[6 + b], [xb[b]])
            tt("dve", yb[b][:], xb[b][:], r2[:, :n].unsqueeze(1).broadcast_to([128, KC, n]), ALU.mult,
               [xb[b], r2], [yb[b]])
            tt("pool", yb[b][:], yb[b][:], modD[:, l2, s_, ia, :].unsqueeze(2).broadcast_to([128, KC, n]),
               ALU.mult, [yb[b], modD], [yb[b]])
            tt("dve", hxT[:, :, t0:t0 + n], yb[b][:], modD[:, l2, s_, ib, :].unsqueeze(2).broadcast_to([128, KC, n]),
               ALU.add, [yb[b], modD], [("hxT", bi)])
        P.barrier()
        P.sb_reset(m0)

    def gemm_fm(loaders, kcs, actT, akey, blocks, evac, wt, perm_wt=None, ncs=4, akc=None):
        for gi, load in enumerate(loaders):
            slab = wt[gi % 2]
            load(slab)
            pslab = None
            if perm_wt is not None and perm_wt[0](gi):
                pslab = perm_wt[1][gi % 2]
                v = slab[:].rearrange("p k (g b i) -> p k g b i", b=2, i=16)
                vp = pslab[:].rearrange("p k (g b i) -> p k g b i", b=2, i=16)
                for kh in range(2):
                    ksl = slice(kh * (kcs // 2), (kh + 1) * (kcs // 2))
                    cp("pool", vp[:, ksl, :, 0, :], v[:, ksl, :, 1, :], [slab], [(pslab, kh, 0)])
                    cp("pool", vp[:, ksl, :, 1, :], v[:, ksl, :, 0, :], [slab], [(pslab, kh, 1)])
            for bi, (t0, n) in enumerate(blocks):
                for c in range(ncs):
                    pa = bb.gbank()
                    for kc in range(kcs):
                        mm(pa[:, :n], slab[:, kc, c * 128:(c + 1) * 128], actT[:, kc, t0:t0 + n], kc == 0, kc == kcs - 1,
                           [slab, akc(kc) if akc is not None else akey(bi)], [pa])
                    pb = None
                    if pslab is not None:
                        pb = bb.gbank()
                        rk = [(pslab, kh, q) for kh in range(2) for q in range(2)]
                        for kc in range(kcs):
                            mm(pb[:, :n], pslab[:, kc, c * 128:(c + 1) * 128], actT[:, kc, t0:t0 + n], kc == 0,
                               kc == kcs - 1, rk + [akey(bi)], [pb])
                    evac(gi, c, bi, t0, n, pa, pb)

    def proj_phase(l, hxT):
        m0 = P.sb_mark()
        wt = [P.sb([128, KC, 512], BF16, f"wt{i}") for i in range(2)]
        wp = [P.sb([128, KC, 512], BF16, f"wp{i}") for i in range(2)]
        cosT = P.sb([128, T], F32, "cosT")
        sinT = P.sb([128, T], F32, "sinT")
        dma("sp", cosT[:], c_cos, (), [cosT])
        dma("sp", sinT[:], c_sin, (), [sinT])
        ev = [P.sb([128, 512], F32, f"ev{i}") for i in range(4)]
        evb = [P.sb([128, 512], BF16, f"evb{i}") for i in range(4)]
        st = {"i": 0}
        W = w_in[l]

        def ld(c0):
            def f(slab):
                dma("pool", slab[:], wslab(W, 0, KC, c0, 512), (), [slab])
            return f

        def evac(gi_abs):
            def f(gi, c, bi, t0, n, pa, pb):
                i = st["i"] % 4
                st["i"] += 1
                ch = gi_abs * 4 + c
                if gi_abs < 4:
                    tt("dve", ev[i][:, :n], pa[:, :n], cosT[:, t0:t0 + n], ALU.mult, [pa, cosT], [ev[i]])
                    j = (i + 1) % 4
                    st["i"] += 1
                    tt("dve", ev[j][:, :n], pb[:, :n], sinT[:, t0:t0 + n], ALU.mult, [pb, sinT], [ev[j]])
                    tt("pool", evb[i][:, :n], ev[i][:, :n], ev[j][:, :n], ALU.add, [ev[i], ev[j]], [evb[i]])
                    dma("sp", QKT[ch, :, t0:t0 + n], evb[i][:, :n], [evb[i]], [("QKT", ch)])
                elif gi_abs < 9:
                    cp("act", ev[i][:, :n], pa[:, :n], [pa], [ev[i]])
                    dma("sp", HYT[ch - 24, :, t0:t0 + n], ev[i][:, :n], [ev[i]], [("HYT", ch - 24)])
                elif gi_abs < 10:
                    cp("act", ev[i][:, :n], pa[:, :n], [pa], [ev[i]])
                    dma("sp", PLT[ch - 36, :, t0:t0 + n], ev[i][:, :n], [ev[i]], [("PLT", ch - 36)])
                else:
                    act(evb[i][:, :n], pa[:, :n], AF.Sigmoid, [pa], [evb[i]])
                    dma("sp", GT[ch - 40, :, t0:t0 + n], evb[i][:, :n], [evb[i]], [("GT", ch - 40)])
            return f

        akey = lambda bi: ("hxT", bi)
        for gi_abs in list(range(0, 4)) + list(range(6, 22)):
            gemm_fm([ld(gi_abs * 512)], KC, hxT, akey, TB, evac(gi_abs), wt,
                    perm_wt=((lambda g: True), wp) if gi_abs < 4 else None)
            wt.reverse()
            wp.reverse()
        va = [P.sb([128, 8, 129], BF16, f"va{i}") for i in range(2)]
        for i in range(2):
            ms("dve", va[i][:, :, 128:129], 1.0, [va[i]])
        wv = [wt[0], wt[1]]
        for g in range(2):
            dma("pool", wv[g][:], wslab(W, 0, KC, 2048 + g * 512, 512), (), [wv[g]])
        for it in range(NT):
            b = it % 2
            bi = min(it // 4, 4)
            for g in range(2):
                pa = bb.gbank()
                for kc in range(KC):
                    mm(pa[:], hxT[:, kc, it * 128:(it + 1) * 128], wv[g][:, kc, :], kc == 0, kc == KC - 1,
                       [wv[g], ("hxT", bi)], [pa])
                cp("act" if g == 0 else "dve", va[b][:, g * 4:(g + 1) * 4, 0:128],
                   pa[:].rearrange("p (h e) -> p h e", h=4), [pa], [va[b]])
            dma("sp", VA[it], va[b][:], [va[b]], ["VA"])
        P.barrier()
        P.sb_reset(m0)

    def attn_phase(l):
        m0 = P.sb_mark()
        lam_init = 0.8 - 0.6 * math.exp(-0.3 * l)
        lp = P.sb([128, 4, 64], F32, "lp")
        pr = P.sb([128, 2, 64], F32, "pr")
        s12 = P.sb([128, 2], F32, "s12")
        nlam = P.sb([128, 1], F32, "nlam")
        gcol = P.sb([128, 1], F32, "gcol")
        dma("sp", lp[:].rearrange("p a d -> p (a d)"), diff_lam[l:l + 1, :].broadcast_to([128, 256]), (), [lp])
        dma("sp", gcol[:], subln[l:l + 1, :].rearrange("o e -> e o"), (), [gcol])
        ts("dve", gcol[:], gcol[:], float(1.0 - lam_init), None, ALU.mult, None, [gcol], [gcol])
        tt("dve", pr[:, 0, :], lp[:, 0, :], lp[:, 1, :], ALU.mult, [lp], [pr])
        tt("dve", pr[:, 1, :], lp[:, 2, :], lp[:, 3, :], ALU.mult, [lp], [pr])
        P.op("dve", lambda e: e.reduce_sum(out=s12[:], in_=pr[:], axis=AX.X), [pr], [s12])
        act(s12[:], s12[:], AF.Exp, [s12], [s12])
        tt("dve", nlam[:], s12[:, 1:2], s12[:, 0:1], ALU.subtract, [s12], [nlam])
        ts("dve", nlam[:], nlam[:], -lam_init, None, ALU.add, None, [nlam], [nlam])
        qT = [P.sb([128, T], BF16, f"qT{i}") for i in range(2)]
        kT = [P.sb([128, T], BF16, f"kT{i}") for i in range(2)]
        vh = [P.sb([128, NT, 128], BF16, f"vh{i}") for i in range(2)]
        qz = [[P.sb([128, T], BF16, f"qz{i}_{m}") for m in range(2)] for i in range(2)]
        for i in range(2):
            ms("pool", qz[i][0][64:128, :], 0.0, [qz[i][0]])
            ms("pool", qz[i][1][0:64, :], 0.0, [qz[i][1]])
        E = [P.sb([128, 512], BF16, f"E{i}") for i in range(4)]
        rd = [P.sb([128, 512], F32, f"rd{i}") for i in range(2)]
        SBK = [ps[0], ps[1]]
        NJUNK = 0
        om = [P.sb([128, 512], F32, f"om{i}") for i in range(2)]
        attf_ = [P.sb([128, 512], F32, f"attf{i}") for i in range(2)]
        sqb_ = [P.sb([128, 512], BF16, f"sqb{i}") for i in range(2)]
        rs__ = [P.sb([128, 512], F32, f"ars{i}") for i in range(2)]
        pending = []
        atT = [P.sb([128, 512], BF16, f"atT{i}") for i in range(2)]
        state = {"ob": 0}
        oT = [ps[2], ps[3]]
        den = [ps[4], ps[5]]
        pstat = ps[6]

        def emit_load(h):
            hb = h % 2
            dma("sp", qT[hb][:], QKT[h], [("QKT", h)], [qT[hb]])
            dma("sp", kT[hb][:], QKT[8 + h], [("QKT", 8 + h)], [kT[hb]])
            dma("act", vh[hb][:], VA[:, :, h, 0:128].rearrange("i p e -> p i e"), ["VA"], [vh[hb]])
            cp("pool", qz[hb][0][0:64, :], qT[hb][0:64, :], [qT[hb]], [qz[hb][0]])
            cp("pool", qz[hb][1][64:128, :], qT[hb][64:128, :], [qT[hb]], [qz[hb][1]])

        steps = []
        for h in range(8):
            for bi, (q0, n) in enumerate(TB):
                kcs = list(range(NT)) if q0 < LX else [16, 17]
                for m in range(2):
                    for kc in kcs:
                        steps.append(dict(h=h, bi=bi, q0=q0, n=n, m=m, kc=kc, first=kc == kcs[0], last=kc == kcs[-1]))

        def emit_S(i):
            st_ = steps[i]
            hb, m, kc, q0, n = st_["h"] % 2, st_["m"], st_["kc"], st_["q0"], st_["n"]
            sb_ = SBK[i % 2]
            mm(sb_[:, :n], kT[hb][:, kc * 128:(kc + 1) * 128],
               qz[hb][m][:, q0:q0 + n], True, True, [kT[hb], qz[hb][m]], [sb_])

        def emit_PV(i):
            st_ = steps[i]
            h, m, kc, q0, n = st_["h"], st_["m"], st_["kc"], st_["q0"], st_["n"]
            hb = h % 2
            sb_ = SBK[i % 2]
            Et = E[i % 4]
            act(Et[:, :n], sb_[:, :n], AF.Exp, [sb_], [Et], scale=0.125)
            mm(oT[m][:, :n], vh[hb][:, kc, :], Et[:, :n], st_["first"], st_["last"], [Et, vh[hb]], [oT[m]])
            mm(den[m][:, :n], ones_b[:], Et[:, :n], st_["first"], st_["last"], [Et, ones_b], [den[m]])
            for _ in range(NJUNK):
                mm(ps[7][:, :n], ones_b[:], Et[:, :n], True, True, [], [])
            if not st_["last"]:
                return
            recip(rd[m][:, :n], den[m][:, :n], [den[m]], [rd[m]])
            tt("dve", om[m][:, :n], oT[m][:, :n], rd[m][:, :n], ALU.mult, [oT[m], rd[m]], [om[m]])
            if m == 0:
                return
            ob = state["ob"]
            state["ob"] += 1
            attf, sqb, rs_, o_t = attf_[ob % 2], sqb_[ob % 2], rs__[ob % 2], atT[ob % 2]
            stt(attf[:, :n], om[1][:, :n], nlam[:, 0:1], om[0][:, :n], ALU.mult, ALU.add, [om[0], om[1], nlam], [attf])
            tt("pool", sqb[:, :n], attf[:, :n], attf[:, :n], ALU.mult, [attf], [sqb])

            def stC():
                mm(pstat[:, :n], ones_b[:], sqb[:, :n], True, True, [sqb, ones_b], [pstat])

            def stD():
                act(rs_[:, :n], pstat[:, :n], AF.Ln, [pstat], [rs_], bias=EPS, scale=1.0 / 128)
                act(rs_[:, :n], rs_[:, :n], AF.Exp, [rs_], [rs_], scale=-0.5)

            def stE():
                tt("dve", attf[:, :n], attf[:, :n], rs_[:, :n], ALU.mult, [attf, rs_], [attf])
                ts("dve", o_t[:, :n], attf[:, :n], gcol[:, 0:1], None, ALU.mult, None, [attf, gcol], [o_t])
                dma("sp", BRT[h, :, q0:q0 + n], o_t[:, :n], [o_t], [("BRT", h)])
            pending.append((i + 5, stC))
            pending.append((i + 7, stD))
            pending.append((i + 9, stE))

        emit_load(0)
        emit_S(0)
        loaded = {0}
        for i in range(len(steps)):
            hn = steps[i]["h"] + 1
            if hn < 8 and hn not in loaded and steps[i]["bi"] >= 1:
                emit_load(hn)
                loaded.add(hn)
            if i + 1 < len(steps):
                emit_S(i + 1)
            emit_PV(i)
            while pending and pending[0][0] <= i:
                pending.pop(0)[1]()
        while pending:
            pending.pop(0)[1]()
        P.barrier()
        P.sb_reset(m0)

    def dwconv3(eng, out, src, w3, r, w):
        n = PW - 2
        ts(eng, out[:, 1:1 + n], src[:, 0:n], w3[:, 0:1], None, ALU.mult, None, r, w)
        if eng == "dve":
            stt(out[:, 1:1 + n], src[:, 1:1 + n], w3[:, 1:2], out[:, 1:1 + n], ALU.mult, ALU.add, r + w, w)
            stt(out[:, 1:1 + n], src[:, 2:2 + n], w3[:, 2:3], out[:, 1:1 + n], ALU.mult, ALU.add, r + w, w)
        else:
            raise NotImplementedError

    def load_padded(q, dst, src3, ch, w, rkeys):
        dma(q, dst[:, 1:1 + LX], src3[ch, :, 0:LX], rkeys, w)
        dma(q, dst[:, LX + 3:LX + 3 + LCX], src3[ch, :, LX:T], rkeys, w)

    def zero_pads(eng, t, w):
        ms(eng, t[:, 0:1], 0.0, w)
        ms(eng, t[:, LX + 1:LX + 3], 0.0, w)
        ms(eng, t[:, PW - 1:PW], 0.0, w)

    def filt_phase(l, sg):
        m0 = P.sb_mark()
        L, SC = sg["L"], sg["SC"]
        NB = 512 if L >= 512 else L
        w1 = P.sb([33, 64], F32, "w1")
        w2 = P.sb([64, 64], F32, "w2")
        w3 = P.sb([64, 2048], F32, "w3")
        pbf = P.sb([64, 4], F32, "pbf")
        zT = P.sb([33, L], F32, "zT")
        h1 = P.sb([64, L], F32, "h1")
        h2 = P.sb([64, L], F32, "h2")
        tq = P.sb([64, 512], F32, "tq")
        rq = P.sb([64, 512], F32, "rq")
        MAGIC = 12582912.0
        wsc = P.sb([128, 3, SC], F32, "wsc")
        dma("sp", w1[:], hw1[l], (), [w1])
        dma("sp", w2[:], hw2[l], (), [w2])
        dma("sp", w3[:], hw3[l], (), [w3])
        dma("sp", pbf[:], hpb[l], (), [pbf])
        dma("sp", zT[:], sg["zT"], (), [zT])
        dma("sp", wsc[:], sg["ws"], (), [wsc])
        OFF = math.pi + 16 * TWO_PI
        for (wm, kk, src, dst, ib, ifr) in ((w1, 33, zT, h1, 0, 1), (w2, 64, h1, h2, 2, 3)):
            for b0 in range(0, L, NB):
                pa = bb.gbank()
                mm(pa[0:64, :NB], wm[0:kk, :], src[0:kk, b0:b0 + NB], True, True, [wm, src], [pa])
                ts("dve", tq[:, :NB], pa[0:64, :NB], pbf[:, ib:ib + 1], pbf[:, ifr:ifr + 1], ALU.add, ALU.mult,
                   [pa, pbf], [tq])
                ts("dve", rq[:, :NB], tq[:, :NB], 1.0 / TWO_PI, MAGIC, ALU.mult, ALU.add, [tq], [rq])
                ts("dve", rq[:, :NB], rq[:, :NB], -MAGIC, -TWO_PI, ALU.add, ALU.mult, [rq], [rq])
                tt("dve", tq[:, :NB], tq[:, :NB], rq[:, :NB], ALU.add, [tq, rq], [tq])
                act(dst[:, b0:b0 + NB], tq[:, :NB], AF.Sin, [tq], [dst])
        dec = [P.sb([128, 512], F32, f"dec{i}") for i in range(2)]
        kr = [P.sb([128, 512], F32, f"kr{i}") for i in range(4)]
        ab = [P.sb([128, 512], F32, f"ab{i}") for i in range(4)]
        ksum = P.sb([128, SC, 1024], BF16, "ksum")
        kdif = P.sb([128, SC, 1024], BF16, "kdif")
        rn = [P.sb([128, 512], F32, f"rn{i}") for i in range(2)]
        psN = [ps[4], ps[5]]
        for lc in range(SC):
            d_ = dec[lc % 2]
            dma("sp", d_[:], sg["decay"][:, lc, :], (), [d_])
            for cb in range(4):
                pa = bb.gbank()
                mm(pa[:], h2[:, lc * 128:(lc + 1) * 128], w3[:, cb * 512:(cb + 1) * 512], True, True, [h2, w3], [pa])
                tt("dve", kr[cb][:], pa[:], d_[:], ALU.mult, [pa, d_], [kr[cb]])
                if lc == 0 and cb >= 2:
                    ms("dve", kr[cb][0:1, :], 0.0, [kr[cb]])
                act(ab[cb][:], kr[cb][:], AF.Abs, [kr[cb]], [ab[cb]])
            for o in range(2):
                for dr in range(2):
                    mm(psN[o][:], ones_f[:], ab[dr * 2 + o][:], lc == 0 and dr == 0, lc == SC - 1 and dr == 1,
                       [ab[dr * 2 + o], ones_f], [psN[o]])
                tt("pool", ksum[:, lc, o * 512:(o + 1) * 512], kr[o][:], kr[2 + o][:], ALU.add, [kr[o], kr[2 + o]],
                   [(ksum, lc, o)])
                tt("pool", kdif[:, lc, o * 512:(o + 1) * 512], kr[o][:], kr[2 + o][:], ALU.subtract,
                   [kr[o], kr[2 + o]], [(kdif, lc, o)])
        for o in range(2):
            ts("dve", rn[o][:], psN[o][:], EPS, None, ALU.add, None, [psN[o]], [rn[o]])
            recip(rn[o][:], rn[o][:], [rn[o]], [rn[o]])
        fcs = [P.sb([128, SC, 128], BF16, f"fcs{i}") for i in range(2)]
        fss = [P.sb([128, SC, 128], BF16, f"fss{i}") for i in range(2)]
        tab = [P.sb([128, 3, 512], F32, f"tab{i}") for i in range(2)]
        N = 2 * L
        it = 0
        for fc in range(SC):
            fb = fc % 2
            dma("sp", fcs[fb][:], sg["FW"][0, fc], (), [fcs[fb]])
            dma("act", fss[fb][:], sg["FW"][1, fc], (), [fss[fb]])
            for o in range(2):
                tb_ = tab[it % 2]
                it += 1
                pA = bb.gbank()
                pB = bb.gbank()
                ksk = [(ksum, lc, o) for lc in range(SC)]
                kdk = [(kdif, lc, o) for lc in range(SC)]
                for lc in range(SC):
                    mm(pA[:], fcs[fb][:, lc, :], ksum[:, lc, o * 512:(o + 1) * 512], lc == 0, lc == SC - 1,
                       [fcs[fb]] + ksk, [pA])
                for lc in range(SC):
                    mm(pB[:], fss[fb][:, lc, :], kdif[:, lc, o * 512:(o + 1) * 512], lc == 0, lc == SC - 1,
                       [fss[fb]] + kdk, [pB])
                stt(tb_[:, 0, :], pA[:], wsc[:, 0, fc:fc + 1], rn[o][:], ALU.mult, ALU.mult, [pA, wsc, rn[o]], [tb_])
                stt(tb_[:, 1, :], pB[:], wsc[:, 1, fc:fc + 1], rn[o][:], ALU.mult, ALU.mult, [pB, wsc, rn[o]], [tb_])
                stt(tb_[:, 2, :], pA[:], wsc[:, 2, fc:fc + 1], rn[o][:], ALU.mult, ALU.mult, [pA, wsc, rn[o]], [tb_])
                if fc == 0:
                    pC = bb.gbank()
                    for lc in range(SC):
                        mm(pC[0:1, :], fss[fb][:, lc, 0:1], ksum[:, lc, o * 512:(o + 1) * 512], lc == 0, lc == SC - 1,
                           [fss[fb]] + ksk, [pC])
                    stt(tb_[0:1, 2, :], pC[0:1, :], 1.0 / N, rn[o][0:1, :], ALU.mult, ALU.mult, [pC, rn[o], tb_], [tb_])
                dma("sp", sg["KTAB"][l, o, fc], tb_[:], [tb_], [("KTAB", sg["nm"])])
        P.barrier()
        P.sb_reset(m0)

    def hyena_phase(l):
        m0 = P.sb_mark()
        sw = P.sb([128, 12, 3], F32, "sw")
        hb = P.sb([128, 2, 4], F32, "hb")
        dma("sp", sw[:], hsw[l], (), [sw])
        dma("sp", hb[:], hbias[l], (), [hb])
        uT = P.sb([128, 4, PW], F32, "uT")
        mT = P.sb([128, 4, PW], F32, "mT")
        utok = P.sb([128, NT, 512], BF16, "utok")
        mk_a = P.sb_mark()
        raw = [P.sb([128, PW], F32, f"raw{i}") for i in range(2)]
        ubf = P.sb([128, 4, T], BF16, "ubf")
        P.sb_reset(mk_a)
        Y = P.sb([128, 32, 512], BF16, "Y")
        fcs = [P.sb([128, 16, 128], BF16, f"hfc{i}") for i in range(2)]
        fss = [P.sb([128, 16, 128], BF16, f"hfs{i}") for i in range(2)]
        tab = [P.sb([128, 3, 512], F32, f"htab{i}") for i in range(2)]
        gv = [P.sb([128, 32, 256], BF16, f"gv{i}") for i in range(2)]
        tm = [P.sb([128, 512], F32, f"htm{i}") for i in range(4)]
        ob = [P.sb([128, 256], BF16, f"hob{i}") for i in range(2)]
        ri = 0

        def conv_chunks(c0, dst):
            nonlocal ri
            for cc in range(4):
                r_ = raw[ri % 2]
                ri += 1
                load_padded("sp", r_, HYT, c0 + cc, [r_], [("HYT", c0 + cc)])
                dwconv3("dve", dst[:, cc, :], r_, sw[:, c0 + cc, :], [r_, sw], [(dst, cc)])

        ti = 0
        gi = 0
        for o in range(2):
            P.barrier()
            for i in range(2):
                zero_pads("dve", raw[i], [raw[i]])
            if o == 0:
                conv_chunks(0, uT)
            conv_chunks(4 + 4 * o, mT)
            for cc in range(4):
                cp("pool", ubf[:, cc, 0:LX], uT[:, cc, 1:1 + LX], [(uT, cc)], [(ubf, cc)])
                cp("pool", ubf[:, cc, LX:T], uT[:, cc, LX + 3:LX + 3 + LCX], [(uT, cc)], [(ubf, cc)])
            tpi = 0
            for cc in range(4):
                for s0 in range(0, NT, 8):
                    ns = min(8, NT - s0)
                    pt = ps[6 + tpi % 2]
                    tpi += 1
                    ptb = pt[:].bitcast(BF16)
                    for k in range(ns):
                        tr(ptb[:, k * 128:(k + 1) * 128], ubf[:, cc, (s0 + k) * 128:(s0 + k + 1) * 128],
                           [(ubf, cc), ident], [pt])
                    cp("act", utok[:, s0:s0 + ns, cc * 128:(cc + 1) * 128],
                       ptb[:, :ns * 128].rearrange("p (k c) -> p k c", c=128), [pt], [(utok, cc, s0)])
            ukeys = [(utok, cc, s0) for cc in range(4) for s0 in range(0, NT, 8)]
            P.barrier()
            for sg in segs:
                SC, L, tk0 = sg["SC"], sg["L"], sg["t0"] // 128
                for fc in range(SC):
                    fb = gi % 2
                    gi += 1
                    dma("sp", fcs[fb][:, :SC, :], sg["FW"][0, fc], (), [fcs[fb]])
                    dma("act", fss[fb][:, :SC, :], sg["FW"][1, fc], (), [fss[fb]])
                    dma("sp", tab[fb][:], sg["KTAB"][l, o, fc], [("KTAB", sg["nm"])], [tab[fb]])
                    pc = ps[4] if fc % 2 == 0 else ps[2]
                    pS = ps[5] if fc % 2 == 0 else ps[3]
                    for sc in range(SC):
                        mm(pc[:], fcs[fb][:, sc, :], utok[:, tk0 + sc, :], sc == 0, sc == SC - 1, [fcs[fb]] + ukeys, [pc])
                    for sc in range(SC):
                        mm(pS[:], fss[fb][:, sc, :], utok[:, tk0 + sc, :], sc == 0, sc == SC - 1, [fss[fb]] + ukeys, [pS])
                    a_, b_, c_, d_ = tm[0], tm[1], tm[2], tm[3]
                    tt("dve", a_[:], pc[:], tab[fb][:, 0, :], ALU.mult, [pc, tab[fb]], [a_])
                    tt("dve", b_[:], pS[:], tab[fb][:, 1, :], ALU.mult, [pS, tab[fb]], [b_])
                    tt("pool", Y[:, fc, :], a_[:], b_[:], ALU.subtract, [a_, b_], [(Y, fc)])
                    tt("dve", c_[:], pc[:], tab[fb][:, 1, :], ALU.mult, [pc, tab[fb]], [c_])
                    tt("dve", d_[:], pS[:], tab[fb][:, 2, :], ALU.mult, [pS, tab[fb]], [d_])
                    tt("pool", Y[:, SC + fc, :], c_[:], d_[:], ALU.add, [c_, d_], [(Y, SC + fc)])
                ykeys = [(Y, k) for k in range(2 * SC)]
                TBI = sg["TBI"]
                for tbk in range(L // TBI):
                    g_ = gv[ti % 2]
                    ti += 1
                    dma("sp", g_[:, :2 * SC, :], sg["GV"][tbk], (), [g_])
                    tt0 = sg["t0"] + tbk * TBI
                    p0 = pidx(tt0)
                    for cc in range(4):
                        pa = bb.gbank()
                        for k in range(2 * SC):
                            mm(pa[:, :TBI], Y[:, k, cc * 128:(cc + 1) * 128], g_[:, k, :], k == 0, k == 2 * SC - 1,
                               ykeys + [g_], [pa])
                        t_ = tm[(cc) % 4]
                        stt(t_[:, :TBI], uT[:, cc, p0:p0 + TBI], hb[:, o, cc:cc + 1], pa[:, :TBI], ALU.mult, ALU.add,
                            [(uT, cc), hb, pa], [t_])
                        if o == 0:
                            tt("pool", uT[:, cc, p0:p0 + TBI], t_[:, :TBI], mT[:, cc, p0:p0 + TBI], ALU.mult,
                               [t_, (mT, cc)], [(uT, cc)])
                        else:
                            o_ = ob[cc % 2]
                            tt("pool", o_[:, :TBI], t_[:, :TBI], mT[:, cc, p0:p0 + TBI], ALU.mult, [t_, (mT, cc)], [o_])
                            dma("sp", BRT[8 + cc, :, tt0:tt0 + TBI], o_[:, :TBI], [o_], [("BRT", 8 + cc)])
        P.barrier()
        P.sb_reset(m0)

    def pool_phase(l):
        m0 = P.sb_mark()
        PP = T + 32
        xo, co = 8, LX + 24
        pw = P.sb([128, 4, 128], BF16, "pw")
        psc = P.sb([128, 4], F32, "psc")
        corr = P.sb([128, 4, 16], F32, "corr")
        dma("pool", pw[:], pool_w[l].rearrange("g i o -> i g o"), (), [pw])
        dma("sp", psc[:], pscale[l], (), [psc])
        dma("sp", corr[:], c_corr, (), [corr])
        for g, win in enumerate((2, 4, 8, 16)):
            pin = P.sb([128, PP], F32, f"pin{g}")
            A_ = P.sb([128, PP], F32, f"pA{g}")
            B_ = P.sb([128, PP], F32, f"pB{g}")
            pm = P.sb([128, T], BF16, f"pm{g}")
            ms("pool", pin[:], 0.0, [pin])
            dma("sp", pin[:, xo:xo + LX], PLT[g, :, 0:LX], [("PLT", g)], [pin])
            dma("sp", pin[:, co:co + LCX], PLT[g, :, LX:T], [("PLT", g)], [pin])
            lo, hi = 8, PP - 8
            nn = hi - lo
            if win == 2:
                tt("dve", A_[:, lo:hi], pin[:, lo - 1:hi - 1], pin[:, lo:hi], ALU.add, [pin], [A_])
                S = A_
            else:
                tt("dve", A_[:, 0:PP - 1], pin[:, 0:PP - 1], pin[:, 1:PP], ALU.add, [pin], [A_])
                if win == 4:
                    tt("dve", B_[:, lo:hi], A_[:, lo - 2:hi - 2], A_[:, lo:hi], ALU.add, [A_], [B_])
                    S = B_
                else:
                    tt("dve", B_[:, 0:PP - 3], A_[:, 0:PP - 3], A_[:, 2:PP - 1], ALU.add, [A_], [B_])
                    if win == 8:
                        tt("dve", A_[:, lo:hi], B_[:, lo - 4:hi - 4], B_[:, lo:hi], ALU.add, [B_, A_], [A_])
                        S = A_
                    else:
                        tt("dve", A_[:, 0:PP - 7], B_[:, 0:PP - 7], B_[:, 4:PP - 3], ALU.add, [B_, A_], [A_])
                        tt("dve", B_[:, lo:hi], A_[:, lo - 8:hi - 8], A_[:, lo:hi], ALU.add, [A_, B_], [B_])
                        S = B_
            ts("dve", S[:, lo:hi], S[:, lo:hi], 1.0 / win, None, ALU.mult, None, [S], [S])
            for (o_, Ls) in ((xo, LX), (co, LCX)):
                tt("dve", S[:, o_:o_ + 8], S[:, o_:o_ + 8], corr[:, g, 0:8], ALU.mult, [S, corr], [S])
                tt("dve", S[:, o_ + Ls - 8:o_ + Ls], S[:, o_ + Ls - 8:o_ + Ls], corr[:, g, 8:16], ALU.mult, [S, corr], [S])
            tt("dve", pm[:, 0:LX], S[:, xo:xo + LX], pin[:, xo:xo + LX], ALU.subtract, [S, pin], [pm])
            tt("dve", pm[:, LX:T], S[:, co:co + LCX], pin[:, co:co + LCX], ALU.subtract, [S, pin], [pm])
            for bi, (t0, n) in enumerate(TB):
                pa = bb.gbank()
                mm(pa[:, :n], pw[:, g, :], pm[:, t0:t0 + n], True, True, [pw, pm], [pa])
                o_t = P.sb([128, 512], BF16, f"po{g}_{bi}")
                ts("dve", o_t[:, :n], pa[:, :n], psc[:, g:g + 1], None, ALU.mult, None, [pa, psc], [o_t])
                dma("sp", BRT[12 + g, :, t0:t0 + n], o_t[:, :n], [o_t], [("BRT", 12 + g)])
        P.barrier()
        P.sb_reset(m0)

    def merge_phase(l):
        m0 = P.sb_mark()
        brt = P.sb([128, 16, T], BF16, "brt")
        mgT = P.sb([128, 16, T], BF16, "mgT")
        wt = [P.sb([128, KC, 512], BF16, f"mwt{i}") for i in range(2)]
        g3 = [P.sb([128, 3, 512], BF16, f"g3{i}") for i in range(2)]
        t3 = [P.sb([128, 3, 512], F32, f"t3{i}") for i in range(2)]
        ev = [P.sb([128, 512], F32, f"mev{i}") for i in range(2)]
        for q4 in range(4):
            dma("sp", brt[:, q4 * 4:(q4 + 1) * 4, :], BRT[q4 * 4:(q4 + 1) * 4].rearrange("c p t -> p c t"),
                [("BRT", c) for c in range(q4 * 4, q4 * 4 + 4)], [(brt, q4)])
        GT4 = GT.rearrange("(b c) p t -> b c p t", b=3)
        it = 0
        for ng in range(4):
            slab = wt[ng % 2]
            dma("pool", slab[:, 0:8, :], wslab(w_att_o[l], 0, 8, ng * 512, 512), (), [(slab, 0)])
            dma("pool", slab[:, 8:12, :], wslab(w_hy_o[l], 0, 4, ng * 512, 512), (), [(slab, 1)])
            dma("pool", slab[:, 12:16, :], wslab(w_pool_o[l], 0, 4, ng * 512, 512), (), [(slab, 2)])
            for bi, (t0, n) in enumerate(TB):
                for c in range(4):
                    nch = ng * 4 + c
                    b = it % 2
                    it += 1
                    dma("sp", g3[b][:, :, :n], GT4[:, nch, :, t0:t0 + n].rearrange("b p t -> p b t"),
                        [("GT", br * 16 + nch) for br in range(3)], [g3[b]])
                    for br, (k0, k1) in enumerate(((0, 8), (8, 12), (12, 16))):
                        pa = bb.gbank()
                        q4s = [(brt, q) for q in ((0, 1) if br == 0 else (2,) if br == 1 else (3,))]
                        for kc in range(k0, k1):
                            mm(pa[:, :n], slab[:, kc, c * 128:(c + 1) * 128], brt[:, kc, t0:t0 + n], kc == k0, kc == k1 - 1,
                               [(slab, br)] + q4s, [pa])
                        tt("dve", t3[b][:, br, :n], pa[:, :n], g3[b][:, br, :n], ALU.mult, [pa, g3[b]], [(t3[b], br)])
                    tt("pool", t3[b][:, 0, :n], t3[b][:, 0, :n], t3[b][:, 1, :n], ALU.add, [(t3[b], 0), (t3[b], 1)],
                       [(t3[b], 0)])
                    tt("pool", mgT[:, nch, t0:t0 + n], t3[b][:, 0, :n], t3[b][:, 2, :n], ALU.add,
                       [(t3[b], 0), (t3[b], 2)], [(mgT, bi, nch)])
        P.barrier()

        def ld(c0):
            def f(slab):
                dma("pool", slab[:], wslab(w_out[l], 0, KC, c0, 512), (), [slab])
            return f
        st = {"i": 0}

        def evac(gi, c, bi, t0, n, pa, pb):
            i = st["i"] % 2
            st["i"] += 1
            cp("act" if i == 0 else "dve", ev[i][:, :n], pa[:, :n], [pa], [ev[i]])
            dma("sp", YT[gi * 4 + c, :, t0:t0 + n], ev[i][:, :n], [ev[i]], ["YT"])
        gemm_fm([ld(g * 512) for g in range(4)], KC, mgT, lambda bi: "nokey", TB, evac, wt)
        P.barrier()
        P.sb_reset(m0)

    def ffn_phase(l, h2T):
        m0 = P.sb_mark()
        cw = P.sb([128, FJ, 3], F32, "cw")
        dma("sp", cw[:], fcw[l], (), [cw])
        wg = [P.sb([128, KC, 256], BF16, f"wg{i}") for i in range(2)]
        wv = [P.sb([128, KC, 256], BF16, f"wv{i}") for i in range(2)]
        gp = [P.sb([128, PW], F32, f"gp{i}") for i in range(2)]
        vv = [P.sb([128, T], F32, f"vv{i}") for i in range(2)]
        cv = P.sb([128, PW], F32, "cv")
        x2 = P.sb([128, PW], F32, "x2")
        uu = P.sb([128, PW], F32, "uu")
        sg_ = P.sb([128, PW], F32, "sg")
        abf = [P.sb([128, T], BF16, f"abf{i}") for i in range(2)]
        for i in range(2):
            zero_pads("dve", gp[i], [gp[i]])
        n_ = PW - 2
        GC = 2.0 * math.sqrt(2.0 / math.pi)
        W = w_up[l]
        pending = []

        def chain(j, b):
            def c1():
                dwconv3("dve", cv, gp[b], cw[:, j, :], [gp[b], cw], [cv])

            def c2():
                act(x2[:, 1:1 + n_], cv[:, 1:1 + n_], AF.Square, [cv], [x2])
                ts("pool", x2[:, 1:1 + n_], x2[:, 1:1 + n_], 0.044715, 1.0, ALU.mult, ALU.add, [x2], [x2])
                tt("pool", uu[:, 1:1 + n_], x2[:, 1:1 + n_], cv[:, 1:1 + n_], ALU.mult, [x2, cv], [uu])

            def c3():
                act(sg_[:, 1:1 + n_], uu[:, 1:1 + n_], AF.Sigmoid, [uu], [sg_], scale=GC)
                tt("pool", uu[:, 1:1 + n_], cv[:, 1:1 + n_], sg_[:, 1:1 + n_], ALU.mult, [cv, sg_, uu], [uu])

            def c4():
                tt("dve", abf[b][:, 0:LX], uu[:, 1:1 + LX], vv[b][:, 0:LX], ALU.mult, [uu, vv[b]], [(abf[b], 0)])
                tt("dve", abf[b][:, LX:T], uu[:, LX + 3:LX + 3 + LCX], vv[b][:, LX:T], ALU.mult, [uu, vv[b]],
                   [(abf[b], 1)])
                dma("sp", AT[j], abf[b][:], [(abf[b], 0), (abf[b], 1)], ["AT"])
            return [c1, c2, c3, c4]

        for g2 in range(FJ // 2):
            sb_ = g2 % 2
            dma("pool", wg[sb_][:], wslab(W, 0, KC, g2 * 256, 256), (), [wg[sb_]])
            dma("pool", wv[sb_][:], wslab(W, 0, KC, FF + g2 * 256, 256), (), [wv[sb_]])
            for c in range(2):
                j = g2 * 2 + c
                b = j % 2
                for bi, (t0, n) in enumerate(TB):
                    pa = bb.gbank()
                    for kc in range(KC):
                        mm(pa[:, :n], wg[sb_][:, kc, c * 128:(c + 1) * 128], h2T[:, kc, t0:t0 + n], kc == 0, kc == KC - 1,
                           [wg[sb_]], [pa])
                    cp("act", gp[b][:, pidx(t0):pidx(t0) + n], pa[:, :n], [pa], [gp[b]])
                    pb = bb.gbank()
                    for kc in range(KC):
                        mm(pb[:, :n], wv[sb_][:, kc, c * 128:(c + 1) * 128], h2T[:, kc, t0:t0 + n], kc == 0, kc == KC - 1,
                           [wv[sb_]], [pb])
                    cp("dve", vv[b][:, t0:t0 + n], pb[:, :n], [pb], [vv[b]])
                    if pending:
                        pending.pop(0)()
                pending.extend(chain(j, b))
        for f_ in pending:
            f_()
        P.barrier()
        P.sb_reset(m0)

    def down_phase(l):
        m0 = P.sb_mark()
        HT = T // 2
        aT = P.sb([128, FJ, HT], BF16, "aT")
        wd = [P.sb([128, FJ, 256], BF16, f"wd{i}") for i in range(2)]
        ev = [P.sb([128, 512], F32, f"dev{i}") for i in range(2)]
        st = {"i": 0}
        for half in range(2):
            h0 = half * HT
            for q4 in range(4):
                dma("sp" if q4 % 2 == 0 else "act", aT[:, q4 * 11:(q4 + 1) * 11, :],
                    blk(AT, q4 * 11, (q4 + 1) * 11, h0, HT), ["AT"], [(aT, q4)])

            def ld(c0):
                def f(slab):
                    dma("pool", slab[:], w_down[l][:, c0:c0 + 256].rearrange("(j p) n -> p j n", p=128), (), [slab])
                return f

            def evac(gi, c, bi, t0, n, pa, pb, h0=h0):
                i = st["i"] % 2
                st["i"] += 1
                cp("act" if i == 0 else "dve", ev[i][:, :n], pa[:, :n], [pa], [ev[i]])
                dma("sp", YT[gi * 2 + c, :, h0 + t0:h0 + t0 + n], ev[i][:, :n], [ev[i]], ["YT"])
            gemm_fm([ld(g * 256) for g in range(8)], FJ, aT, None, [(0, 512), (512, 512), (1024, 128)], evac, wd,
                    ncs=2, akc=lambda kc: (aT, kc // 11))
            P.barrier()
        P.sb_reset(m0)

    ada_phase()
    for l in range(nlayers):
        for sg in segs:
            filt_phase(l, sg)
    mk = P.sb_mark()
    hxT = P.sb([128, KC, T], BF16, "hxT")
    norm_pass(0, 0, 1, hxT)
    for l in range(nlayers):
        proj_phase(l, hxT)
        P.sb_reset(mk)
        if stop_after == "proj":
            break
        attn_phase(l)
        hyena_phase(l)
        pool_phase(l)
        if stop_after == "branches":
            break
        merge_phase(l)
        mk = P.sb_mark()
        hxT = P.sb([128, KC, T], BF16, "h2T")
        resid_norm_pass(l, 2, l, 3, 4, hxT)
        if stop_after == "mixer":
            break
        ffn_phase(l, hxT)
        P.sb_reset(mk)
        down_phase(l)
        if l + 1 < nlayers:
            mk = P.sb_mark()
            hxT = P.sb([128, KC, T], BF16, "hxT")
            resid_norm_pass(l, 5, l + 1, 0, 1, hxT)
        else:
            resid_pass(l, 5)
    dma("sp", outT, XT[:, :, 0:LX], ["XT"], ["outT"])
    P.barrier()
    return P.emit(), P


def _bf16(a):
    return np.asarray(a, dtype=np.float32).astype(ml_dtypes.bfloat16)


def _dft_consts(L):
    N = 2 * L
    SC = L // 128
    ct = np.cos(2.0 * np.pi * np.arange(N) / N)
    st = np.sin(2.0 * np.pi * np.arange(N) / N)
    s = np.arange(L)
    idx = (s[:, None] * s[None, :]) % N
    Mc = ct[idx]
    Ms = st[idx]
    Fs = Ms.copy()
    Fs[:, 0] = (-1.0) ** s
    Gs = Ms.copy()
    Gs[0, :] = (-1.0) ** s
    def fw(M):
        return M.reshape(SC, 128, SC, 128).transpose(2, 1, 0, 3)
    FW = np.stack([fw(Mc), fw(Fs)], axis=0)
    TBI = 256
    def gv(M):
        return M.reshape(SC, 128, L // TBI, TBI).transpose(2, 1, 0, 3)
    GV = np.concatenate([gv(Mc), gv(Gs)], axis=2)
    ws = np.full((128, 3, SC), 2.0 / N, dtype=np.float32)
    ws[0, 0, 0] = 1.0 / N
    ws[0, 1, 0] = 0.0
    f32 = np.float32
    t = np.linspace(0.0, 1.0, L, dtype=f32)[:, None]
    w_ang = (f32(2.0 * math.pi) * np.arange(L, dtype=f32)[:, None] / f32(L)).astype(f32)
    fb = np.linspace(1e-4, 15, 16, dtype=f32)[None, :]
    ang = (fb * w_ang).astype(f32)
    z = np.concatenate([t, np.cos(ang), -np.sin(ang)], axis=-1).astype(f32)
    zT = np.ascontiguousarray(z.T)
    dmin = math.log(1e-2) / 1.5
    dmax = math.log(1e-2) / 0.3
    deltas = np.abs(np.linspace(dmin, dmax, 512, dtype=f32))
    decay = np.exp(-t * deltas[None, :]).astype(f32)
    decay = np.ascontiguousarray(decay.reshape(SC, 128, 512).transpose(1, 0, 2))
    return dict(FW=_bf16(np.ascontiguousarray(FW)), GV=_bf16(np.ascontiguousarray(GV)), ws=ws, zT=zT, decay=decay)


def _rope_consts():
    f32 = np.float32
    inv = (f32(10000.0) ** (-np.arange(16, dtype=f32) / f32(16))).astype(f32)
    tt_ = np.arange(LX)
    row = (tt_ // 64).astype(f32)
    col = (tt_ % 64).astype(f32)
    cosT = np.ones((128, T), dtype=f32)
    sinT = np.zeros((128, T), dtype=f32)
    for m in range(2):
        for a in range(2):
            pos = row if a == 0 else col
            ang = (pos[None, :] * inv[:, None]).astype(f32)
            for b in range(2):
                p0 = m * 64 + a * 32 + b * 16
                cosT[p0:p0 + 16, :LX] = np.cos(ang)
                sinT[p0:p0 + 16, :LX] = (-1.0 if b == 0 else 1.0) * np.sin(ang)
    return cosT, sinT


def _pool_corr():
    corr = np.ones((128, 4, 16), dtype=np.float32)
    for g, win in enumerate((2, 4, 8, 16)):
        h = win // 2
        for pos in range(8):
            cnt = min(pos, h) + h
            corr[:, g, pos] = win / cnt
            rem = 8 - pos
            cnt2 = h + min(h, rem)
            corr[:, g, 8 + pos] = win / cnt2
    return corr


_CACHE = {}


def _consts():
    if "c" not in _CACHE:
        cx = _dft_consts(LX)
        cc = _dft_consts(LCX)
        cosT, sinT = _rope_consts()
        m = {"c_ident": np.eye(128, dtype=np.float32), "c_cos": cosT, "c_sin": sinT, "c_corr": _pool_corr()}
        for nm, c in (("x", cx), ("c", cc)):
            m[f"c_fw_{nm}"] = c["FW"]
            m[f"c_gv_{nm}"] = c["GV"]
            m[f"c_zT_{nm}"] = c["zT"]
            m[f"c_decay_{nm}"] = c["decay"]
            m[f"c_ws_{nm}"] = c["ws"]
        _CACHE["c"] = m
    return _CACHE["c"]


def make_in_maps(inputs, ncore=NCORE):
    f = lambda a: np.ascontiguousarray(np.asarray(a, dtype=np.float32))
    I = {k: np.asarray(v) for k, v in inputs.items()}
    shared = dict(_consts())
    shared["w_ada"] = f(I["w_ada"])
    shared["b_ada"] = f(I["b_ada"])
    shared["gT"] = f(I["norm_g"].reshape(DEPTH, 4, KC, 128).transpose(0, 3, 1, 2))
    shared["w_in"] = f(I["w_in"])
    shared["diff_lam"] = f(I["diff_lam"].reshape(DEPTH, 256))
    shared["attn_subln_g"] = f(I["attn_subln_g"])
    shared["hsw"] = f(I["hy_short_w"].reshape(DEPTH, 3, 12, 128).transpose(0, 3, 2, 1))
    shared["hy_ffn_w1"] = f(I["hy_ffn_w1"])
    shared["hy_ffn_w2"] = f(I["hy_ffn_w2"])
    shared["hy_ffn_w3"] = f(I["hy_ffn_w3"])
    shared["hpb"] = f(np.stack([I["hy_ffn_b1"], I["hy_freq"][:, 0], I["hy_ffn_b2"], I["hy_freq"][:, 1]], axis=-1))
    shared["hbias"] = f(I["hy_bias"].reshape(DEPTH, 2, 4, 128).transpose(0, 3, 1, 2))
    shared["pool_w"] = f(I["pool_w"])
    shared["pscale"] = f(I["pool_scale"].reshape(DEPTH, 4, 128).transpose(0, 2, 1))
    shared["w_att_o"] = f(I["w_att_o"])
    shared["w_hy_o"] = f(I["w_hy_o"])
    shared["w_pool_o"] = f(I["w_pool_o"])
    shared["w_out"] = f(I["w_out"])
    shared["w_up"] = f(I["w_up"])
    shared["fcw"] = f(I["ff_conv_w"].reshape(DEPTH, 3, FJ, 128).transpose(0, 3, 2, 1))
    shared["w_down"] = f(I["w_down"])
    maps = []
    for b in range(ncore):
        X = np.concatenate([I["x"][b], I["ctx"][b]], axis=0)
        m = dict(shared)
        m["xT"] = f(X.T.reshape(KC, 128, T))
        cc = np.stack([I["c"][b], I["c_ctx"]], axis=0)
        m["ccT"] = f(cc.reshape(2, KC, 128).transpose(2, 1, 0))
        maps.append(m)
    return maps


def kernel(**inputs):
    if "nc" not in _CACHE:
        _CACHE["nc"] = build()[0]
    nc = _CACHE["nc"]
    maps = make_in_maps(inputs)
    res = run_bass_kernel_spmd(nc, maps, core_ids=list(range(NCORE)))
    outs = []
    for b in range(NCORE):
        oT = np.asarray(res.results[b]["outT"], dtype=np.float32)
        outs.append(np.ascontiguousarray(oT.reshape(D, LX).T))
    return np.stack(outs, axis=0)
```

```python
import numpy as np
import concourse.bass as bass
import concourse.mybir as mybir

F32 = mybir.dt.float32
BF16 = mybir.dt.bfloat16
AF = mybir.ActivationFunctionType
ALU = mybir.AluOpType
AX = mybir.AxisListType

ENGS = ("sp", "act", "dve", "pool", "pe")
NSLOT = 12
SEM_CAP = 30000


class _Op:
    __slots__ = ("fn", "deps", "dma", "signal", "sigcount", "slot", "val", "prev", "semidx")

    def __init__(self, fn, deps, dma):
        self.fn = fn
        self.deps = deps
        self.dma = dma
        self.signal = False
        self.sigcount = 0
        self.slot = 0
        self.val = 0
        self.prev = None
        self.semidx = 0


def _key(x):
    if isinstance(x, tuple):
        return tuple(_key(y) for y in x)
    if isinstance(x, (str, int)):
        return x
    return ("id", id(x))


class Prog:
    def __init__(self, same_engine_sync=True):
        self.nc = bass.Bass("TRN2", target_bir_lowering=False)
        self.ops = {e: [] for e in ENGS}
        self.lastw = {}
        self.readers = {}
        self.same = same_engine_sync
        self.sb_off = 16384
        self.sb_hw = 0
        self.ndma = {e: 0 for e in ENGS}
        self.slot_last = {}
        self._n = 0

    def sb(self, shape, dt, name=None):
        self._n += 1
        name = name or f"sb{self._n}"
        esz = 4 if dt == F32 else 2
        if dt in (mybir.dt.int32, mybir.dt.uint32):
            esz = 4
        per_part = int(np.prod(shape[1:])) * esz
        per_part = (per_part + 63) // 64 * 64
        t = self.nc.alloc_sbuf_tensor_at(f"{name}_{self._n}", list(shape), dt, offset=self.sb_off)
        self.sb_off += per_part
        self.sb_hw = max(self.sb_hw, self.sb_off)
        assert self.sb_off <= 16384 + 212000, f"SBUF overflow {self.sb_off}"
        return t

    def sb_mark(self):
        return self.sb_off

    def sb_reset(self, mark):
        self.sb_off = mark

    def dram(self, name, shape, dt, kind="Internal"):
        return self.nc.dram_tensor(name, list(shape), dt, kind=kind)

    def op(self, eng, fn, reads=(), writes=(), dma=False):
        reads = [_key(r) for r in reads]
        writes = [_key(w) for w in writes]
        deps = set()
        for r in reads:
            lw = self.lastw.get(r)
            if lw is not None:
                deps.add(lw)
        for w in writes:
            lw = self.lastw.get(w)
            if lw is not None:
                deps.add(lw)
            for rd in self.readers.get(w, ()):
                deps.add(rd)
        idx = len(self.ops[eng])
        me = (eng, idx)
        o = _Op(fn, None, dma)
        fdeps = []
        for d in deps:
            if d == me:
                continue
            tgt = self.ops[d[0]][d[1]]
            if d[0] == eng and not tgt.dma:
                if eng == "pe" or not self.same:
                    continue
            fdeps.append(d)
        best = {}
        pruned = []
        for d in fdeps:
            if self.ops[d[0]][d[1]].dma:
                pruned.append(d)
            elif d[0] not in best or d[1] > best[d[0]]:
                best[d[0]] = d[1]
        pruned.extend(best.items())
        o.deps = pruned
        if dma:
            j = self.ndma[eng]
            self.ndma[eng] += 1
            o.slot = j % NSLOT
            o.val = 16 * (j // NSLOT + 1)
            o.prev = self.slot_last.get((eng, o.slot))
            self.slot_last[(eng, o.slot)] = me
        self.ops[eng].append(o)
        for w in writes:
            self.lastw[w] = me
            self.readers[w] = []
        for r in reads:
            self.readers.setdefault(r, []).append(me)
        return me

    def barrier(self):
        lasts = []
        for e in ENGS:
            if self.ops[e]:
                for i in range(len(self.ops[e]) - 1, -1, -1):
                    if not self.ops[e][i].dma and self.ops[e][i].fn is not None:
                        lasts.append((e, i))
                        break
            for s in range(NSLOT):
                l = self.slot_last.get((e, s))
                if l is not None:
                    lasts.append(l)
        self._barrier_deps = lasts
        for e in ENGS:
            o = _Op(None, [d for d in lasts if not (d[0] == e and not self.ops[d[0]][d[1]].dma and e == "pe")], False)
            self.ops[e].append(o)
        self.lastw = {}
        self.readers = {}

    def emit(self):
        nc = self.nc
        for e in ENGS:
            for o in self.ops[e]:
                for (e2, i2) in o.deps:
                    t = self.ops[e2][i2]
                    if not t.dma:
                        t.signal = True
        nsem = {}
        for e in ENGS:
            c = 0
            for o in self.ops[e]:
                if o.dma or o.fn is None:
                    continue
                if o.signal:
                    c += 1
                    o.semidx = (c - 1) // SEM_CAP
                    o.sigcount = c - o.semidx * SEM_CAP
            nsem[e] = max(1, (c + SEM_CAP - 1) // SEM_CAP)
        from contextlib import ExitStack
        with ExitStack() as es:
            csem = {e: [es.enter_context(nc.semaphore(f"c_{e}_{k}")) for k in range(nsem[e])] for e in ENGS}
            dsem = {e: [es.enter_context(nc.semaphore(f"d_{e}_{s}")) for s in range(NSLOT)]
                    for e in ENGS if self.ndma[e] > 0}
            block = es.enter_context(nc.Block())
            ops = self.ops

            def replay(e, eng):
                waited = {}

                def wait_for(d):
                    t = ops[d[0]][d[1]]
                    if t.dma:
                        key = ("d", d[0], t.slot)
                        sem = dsem[d[0]][t.slot]
                        val = t.val
                    else:
                        if t.fn is None:
                            return
                        key = ("c", d[0], t.semidx)
                        sem = csem[d[0]][t.semidx]
                        val = t.sigcount
                    if waited.get(key, 0) < val:
                        eng.wait_ge(sem, val)
                        waited[key] = val

                for o in ops[e]:
                    for d in o.deps:
                        wait_for(d)
                    if o.fn is None:
                        continue
                    if o.dma:
                        if o.prev is not None:
                            wait_for(o.prev)
                        inst = o.fn(eng)
                        inst.then_inc(dsem[e][o.slot], 16)
                    else:
                        inst = o.fn(eng)
                        if o.signal:
                            inst.then_inc(csem[e][o.semidx], 1)

            @block.sync
            def _(eng):
                replay("sp", eng)

            @block.scalar
            def _(eng):
                replay("act", eng)

            @block.vector
            def _(eng):
                replay("dve", eng)

            @block.gpsimd
            def _(eng):
                replay("pool", eng)

            @block.tensor
            def _(eng):
                replay("pe", eng)
        return nc


import math
import ml_dtypes
from concourse.bass_utils import run_bass_kernel_spmd

D = 2048
KC = 16
LX = 2048
LCX = 256
T = LX + LCX
NT = T // 128
TB = [(0, 512), (512, 512), (1024, 512), (1536, 512), (2048, 256)]
PW = T + 4
IN_W = 11264
FF = 5632
FJ = FF // 128
DEPTH = 4
EPS = 1e-6
NCORE = 4
TWO_PI = 2.0 * math.pi


def pidx(t):
    return t + 1 if t < LX else t + 3


class B:
    def __init__(self, dump=()):
        self.P = Prog()
        self.nc = self.P.nc
        self.dump = set(dump)
        nc = self.nc
        self.ps = [nc.alloc_psum_tensor(f"psb{i}", [128, 512], F32) for i in range(8)]
        self.rot = 0
        self.ident = None

    def din(self, name, shape, dt=F32):
        return self.P.dram(name, shape, dt, kind="ExternalInput").ap()

    def dsc(self, name, shape, dt=F32):
        kind = "ExternalOutput" if name in self.dump else "Internal"
        return self.P.dram(name, shape, dt, kind=kind).ap()

    def gbank(self):
        b = self.ps[self.rot % 4]
        self.rot += 1
        return b

    def mm(self, out, lhsT, rhs, start, stop, r, w, skip=False):
        if skip:
            self.P.op("pe", lambda e: e.matmul(out, lhsT=lhsT, rhs=rhs, start=start, stop=stop,
                                               skip_group_check=True), r, w)
        else:
            self.P.op("pe", lambda e: e.matmul(out, lhsT=lhsT, rhs=rhs, start=start, stop=stop), r, w)

    def tr(self, out, in_, r, w, ident=None):
        idn = self.ident if ident is None else ident
        self.P.op("pe", lambda e: e.transpose(out=out, in_=in_, identity=idn), r, w)

    def act(self, out, in_, func, r, w, bias=0.0, scale=1.0, accum=None):
        if accum is None:
            self.P.op("act", lambda e: e.activation(out=out, in_=in_, func=func, bias=bias, scale=scale), r, w)
        else:
            self.P.op("act", lambda e: e.activation(out=out, in_=in_, func=func, bias=bias, scale=scale,
                                                    accum_out=accum), r, w)

    def tt(self, eng, out, in0, in1, op, r, w):
        self.P.op(eng, lambda e: e.tensor_tensor(out=out, in0=in0, in1=in1, op=op), r, w)

    def ts(self, eng, out, in0, s1, s2, op0, op1, r, w):
        if s2 is None:
            self.P.op(eng, lambda e: e.tensor_scalar(out=out, in0=in0, scalar1=s1, scalar2=None, op0=op0), r, w)
        else:
            self.P.op(eng, lambda e: e.tensor_scalar(out=out, in0=in0, scalar1=s1, scalar2=s2, op0=op0, op1=op1), r, w)

    def stt(self, out, in0, scalar, in1, op0, op1, r, w):
        self.P.op("dve", lambda e: e.scalar_tensor_tensor(out=out, in0=in0, scalar=scalar, in1=in1, op0=op0, op1=op1),
                  r, w)

    def cp(self, eng, out, in_, r, w):
        if eng == "act":
            self.P.op("act", lambda e: e.copy(out=out, in_=in_), r, w)
        else:
            self.P.op(eng, lambda e: e.tensor_copy(out=out, in_=in_), r, w)

    def ms(self, eng, ap, val, w):
        self.P.op(eng, lambda e: e.memset(ap, val), (), w)

    def recip(self, out, in_, r, w):
        self.P.op("dve", lambda e: e.reciprocal(out=out, in_=in_), r, w)

    def dma(self, q, out, in_, r, w):
        self.P.op(q, lambda e: e.dma_start(out=out, in_=in_), r, w, dma=True)


def blk(ap3, c0, c1, t0, n):
    return ap3[c0:c1, :, t0:t0 + n].rearrange("c p t -> p c t")


def wslab(w2d, k0, kc, c0, ncol):
    return w2d[k0 * 128:(k0 + kc) * 128, c0:c0 + ncol].rearrange("(kc p) n -> p kc n", p=128)


def build(nlayers=DEPTH, dump=(), stop_after=None):
    bb = B(dump)
    P = bb.P
    nc = bb.nc
    ps = bb.ps
    mm, tr, act, tt, ts, stt, cp, ms, recip, dma = bb.mm, bb.tr, bb.act, bb.tt, bb.ts, bb.stt, bb.cp, bb.ms, bb.recip, bb.dma
    din, dsc = bb.din, bb.dsc

    xT_in = din("xT", [KC, 128, T])
    ccT = din("ccT", [128, KC, 2])
    w_ada = din("w_ada", [DEPTH, D, 6 * D])
    b_ada = din("b_ada", [DEPTH, 6 * D])
    gT = din("gT", [DEPTH, 128, 4, KC])
    w_in = din("w_in", [DEPTH, D, IN_W])
    diff_lam = din("diff_lam", [DEPTH, 4 * 64])
    subln = din("attn_subln_g", [DEPTH, 128])
    hsw = din("hsw", [DEPTH, 128, 12, 3])
    hw1 = din("hy_ffn_w1", [DEPTH, 33, 64])
    hw2 = din("hy_ffn_w2", [DEPTH, 64, 64])
    hw3 = din("hy_ffn_w3", [DEPTH, 64, 2048])
    hpb = din("hpb", [DEPTH, 64, 4])
    hbias = din("hbias", [DEPTH, 128, 2, 4])
    pool_w = din("pool_w", [DEPTH, 4, 128, 128])
    pscale = din("pscale", [DEPTH, 128, 4])
    w_att_o = din("w_att_o", [DEPTH, 1024, D])
    w_hy_o = din("w_hy_o", [DEPTH, 512, D])
    w_pool_o = din("w_pool_o", [DEPTH, 512, D])
    w_out = din("w_out", [DEPTH, D, D])
    w_up = din("w_up", [DEPTH, D, 2 * FF])
    fcw = din("fcw", [DEPTH, 128, FJ, 3])
    w_down = din("w_down", [DEPTH, FF, D])
    c_ident = din("c_ident", [128, 128])
    c_cos = din("c_cos", [128, T])
    c_sin = din("c_sin", [128, T])
    c_corr = din("c_corr", [128, 4, 16])
    segs = []
    for nm, L in (("x", LX), ("c", LCX)):
        SC = L // 128
        TBI = 256
        segs.append(dict(
            nm=nm, L=L, SC=SC, t0=0 if nm == "x" else LX, TBI=TBI,
            FW=din(f"c_fw_{nm}", [2, SC, 128, SC, 128], BF16),
            GV=din(f"c_gv_{nm}", [L // TBI, 128, 2 * SC, TBI], BF16),
            zT=din(f"c_zT_{nm}", [33, L]),
            decay=din(f"c_decay_{nm}", [128, SC, 512]),
            ws=din(f"c_ws_{nm}", [128, 3, SC]),
            KTAB=dsc(f"KTAB_{nm}", [DEPTH, 2, SC, 128, 3, 512]),
        ))
    outT = P.dram("outT", [KC, 128, LX], F32, kind="ExternalOutput").ap()

    XT = dsc("XT", [KC, 128, T])
    MOD = dsc("MOD", [DEPTH, 2, 6 * D])
    QKT = dsc("QKT", [16, 128, T], BF16)
    VA = dsc("VA", [NT, 128, 8, 129], BF16)
    HYT = dsc("HYT", [12, 128, T])
    PLT = dsc("PLT", [4, 128, T])
    GT = dsc("GT", [48, 128, T], BF16)
    BRT = dsc("BRT", [16, 128, T], BF16)
    YT = dsc("YT", [KC, 128, T])
    AT = dsc("AT", [FJ, 128, T], BF16)

    identf = P.sb([128, 128], F32, "identf")
    ident = P.sb([128, 128], BF16, "ident")
    ones_b = P.sb([128, 128], BF16, "ones_b")
    ones_f = P.sb([128, 128], F32, "ones_f")
    modD = P.sb([128, DEPTH, 2, 6, KC], F32, "modD")
    bb.ident = ident[:]
    dma("sp", identf[:], c_ident, (), [identf])
    cp("dve", ident[:], identf[:], [identf], [ident])
    ms("dve", ones_b[:], 1.0, [ones_b])
    ms("dve", ones_f[:], 1.0, [ones_f])
    dma("sp", XT, xT_in, (), ["XT"])
    base_mark = P.sb_mark()

    def ada_phase():
        m0 = P.sb_mark()
        sc = P.sb([128, KC, 2], F32, "sc")
        NB_ = 5
        wa = [P.sb([128, KC, 512], F32, f"wa{i}") for i in range(NB_)]
        bt = [P.sb([2, 512], F32, f"bt{i}") for i in range(NB_)]
        mo = [P.sb([2, 512], F32, f"mo{i}") for i in range(NB_)]
        dma("sp", sc[:], ccT, (), [sc])
        act(sc[:], sc[:], AF.Silu, [sc], [sc])
        it = 0
        for l in range(nlayers):
            for ng in range(24):
                b = it % NB_
                it += 1
                dma("sp" if it % 2 == 0 else "act", wa[b][:], wslab(w_ada[l], 0, KC, ng * 512, 512), (), [wa[b]])
                dma("act", bt[b][:], b_ada[l:l + 1, ng * 512:(ng + 1) * 512].broadcast_to([2, 512]), (), [bt[b]])
                pb = ps[4 + b % 2]
                for kc in range(KC):
                    mm(pb[0:2, :], sc[:, kc, :], wa[b][:, kc, :], kc == 0, kc == KC - 1, [sc, wa[b]], [pb])
                tt("dve", mo[b][:], pb[0:2, :], bt[b][:], ALU.add, [pb, bt[b]], [mo[b]])
                dma("sp", MOD[l, :, ng * 512:(ng + 1) * 512], mo[b][:], [mo[b]], [("MOD", l)])
        P.barrier()
        P.sb_reset(m0)
        mr = [P.sb([96, 128], F32, f"mr{i}") for i in range(2)]
        mt = P.sb([128, 2, 6, KC], F32, "mt")
        g_sb = P.sb([128, 4, KC], F32, "g_sb")
        tmp = P.sb([128, KC], F32, "tmpm")
        it = 0
        for l in range(nlayers):
            dma("sp", g_sb[:], gT[l], (), [g_sb])
            for s in range(2):
                b = it % 2
                it += 1
                dma("sp", mr[b][:], MOD[l, s].rearrange("(r p) -> r p", p=128), (), [mr[b]])
                pb = ps[4 + b]
                tr(pb[:, 0:96], mr[b][:], [mr[b], identf], [pb], ident=identf[0:96, 0:96])
                cp("dve", mt[:, s].rearrange("p m j -> p (m j)"), pb[:, 0:96], [pb], [(mt, s)])
                ts("dve", tmp[:], mt[:, s, 1, :], 1.0, None, ALU.add, None, [(mt, s)], [tmp])
                tt("dve", modD[:, l, s, 0, :], tmp[:], g_sb[:, 0, :], ALU.mult, [tmp, g_sb], [modD])
                cp("dve", modD[:, l, s, 1, :], mt[:, s, 0, :], [(mt, s)], [modD])
                tt("dve", modD[:, l, s, 2, :], mt[:, s, 2, :], g_sb[:, 1, :], ALU.mult, [(mt, s), g_sb], [modD])
                ts("dve", tmp[:], mt[:, s, 4, :], 1.0, None, ALU.add, None, [(mt, s)], [tmp])
                tt("dve", modD[:, l, s, 3, :], tmp[:], g_sb[:, 2, :], ALU.mult, [tmp, g_sb], [modD])
                cp("dve", modD[:, l, s, 4, :], mt[:, s, 3, :], [(mt, s)], [modD])
                tt("dve", modD[:, l, s, 5, :], mt[:, s, 5, :], g_sb[:, 3, :], ALU.mult, [(mt, s), g_sb], [modD])
        P.barrier()
        P.sb_reset(m0)

    def stats_rstd(src, n, sq, rstd, pstat, rkeys, nchunks=KC, dim=D):
        act(sq[:, :, :n], src, AF.Square, rkeys, [sq])
        for j in range(nchunks):
            mm(pstat[:, :n], ones_b[:], sq[:, j, :n], j == 0, j == nchunks - 1, [sq, ones_b], [pstat])
        act(rstd[:, :n], pstat[:, :n], AF.Ln, [pstat], [rstd], bias=EPS, scale=1.0 / dim)
        act(rstd[:, :n], rstd[:, :n], AF.Exp, [rstd], [rstd], scale=-0.5)

    def seg_of(t0):
        return 0 if t0 < LX else 1

    def norm_pass(l, ia, ib, hxT):
        m0 = P.sb_mark()
        xb = [P.sb([128, KC, 512], F32, f"xb{i}") for i in range(2)]
        sq = P.sb([128, KC, 512], BF16, "sq")
        t1 = P.sb([128, KC, 512], F32, "t1")
        rstd = [P.sb([128, 512], F32, f"rstd{i}") for i in range(2)]
        for bi, (t0, n) in enumerate(TB):
            b = bi % 2
            s = seg_of(t0)
            dma("sp", xb[b][:, :, :n], blk(XT, 0, KC, t0, n), ["XT"], [xb[b]])
            stats_rstd(xb[b][:, :, :n], n, sq, rstd[b], ps[4 + b], [xb[b]])
            tt("dve", t1[:, :, :n], xb[b][:, :, :n], rstd[b][:, :n].unsqueeze(1).broadcast_to([128, KC, n]), ALU.mult,
               [xb[b], rstd[b]], [t1])
            tt("pool", t1[:, :, :n], t1[:, :, :n], modD[:, l, s, ia, :].unsqueeze(2).broadcast_to([128, KC, n]), ALU.mult,
               [t1, modD], [t1])
            tt("dve", hxT[:, :, t0:t0 + n], t1[:, :, :n], modD[:, l, s, ib, :].unsqueeze(2).broadcast_to([128, KC, n]),
               ALU.add, [t1, modD], [("hxT", bi)])
        P.barrier()
        P.sb_reset(m0)

    def resid_pass(l, ig):
        m0 = P.sb_mark()
        xb = [P.sb([128, KC, 512], F32, f"rxb{i}") for i in range(2)]
        yb = [P.sb([128, KC, 512], F32, f"ryb{i}") for i in range(2)]
        sq = P.sb([128, KC, 512], BF16, "rsq")
        rstd = [P.sb([128, 512], F32, f"rrstd{i}") for i in range(2)]
        for bi, (t0, n) in enumerate(TB):
            b = bi % 2
            s = seg_of(t0)
            dma("sp", xb[b][:, :, :n], blk(XT, 0, KC, t0, n), ["XT"], [xb[b]])
            dma("act", yb[b][:, :, :n], blk(YT, 0, KC, t0, n), ["YT"], [yb[b]])
            stats_rstd(yb[b][:, :, :n], n, sq, rstd[b], ps[4 + b], [yb[b]])
            tt("dve", yb[b][:, :, :n], yb[b][:, :, :n], rstd[b][:, :n].unsqueeze(1).broadcast_to([128, KC, n]), ALU.mult,
               [yb[b], rstd[b]], [yb[b]])
            tt("pool", yb[b][:, :, :n], yb[b][:, :, :n], modD[:, l, s, ig, :].unsqueeze(2).broadcast_to([128, KC, n]),
               ALU.mult, [yb[b], modD], [yb[b]])
            tt("dve", xb[b][:, :, :n], xb[b][:, :, :n], yb[b][:, :, :n], ALU.add, [xb[b], yb[b]], [xb[b]])
            dma("sp", blk(XT, 0, KC, t0, n), xb[b][:, :, :n], [xb[b]], ["XT"])
        P.barrier()
        P.sb_reset(m0)

    def resid_norm_pass(l, ig, l2, ia, ib, hxT):
        m0 = P.sb_mark()
        NB2 = 256
        NBUF = 3
        xb = [P.sb([128, KC, NB2], F32, f"fxb{i}") for i in range(NBUF)]
        yb = [P.sb([128, KC, NB2], F32, f"fyb{i}") for i in range(NBUF)]
        sq = [P.sb([128, KC, NB2], BF16, f"fsq{i}") for i in range(NBUF)]
        rstd = [P.sb([128, NB2], F32, f"frstd{i}") for i in range(2 * NBUF)]
        n = NB2
        nblk = T // NB2

        def first(k):
            t0 = k * NB2
            b = k % NBUF
            s_ = seg_of(t0)
            dma("sp", xb[b][:], blk(XT, 0, KC, t0, n), ["XT"], [xb[b]])
            dma("act", yb[b][:], blk(YT, 0, KC, t0, n), ["YT"], [yb[b]])
            stats_rstd(yb[b][:], n, sq[b], rstd[b], ps[4 + k % 2], [yb[b]])
            tt("dve", yb[b][:], yb[b][:], rstd[b][:, :n].unsqueeze(1).broadcast_to([128, KC, n]), ALU.mult,
               [yb[b], rstd[b]], [yb[b]])
            tt("pool", yb[b][:], yb[b][:], modD[:, l, s_, ig, :].unsqueeze(2).broadcast_to([128, KC, n]),
               ALU.mult, [yb[b], modD], [yb[b]])
            tt("dve", xb[b][:], xb[b][:], yb[b][:], ALU.add, [xb[b], yb[b]], [xb[b]])
            dma("sp", blk(XT, 0, KC, t0, n), xb[b][:], [xb[b]], ["XT"])

        def second(k):
            t0 = k * NB2
            bi = min(t0 // 512, 4)
            b = k % NBUF
            s_ = seg_of(t0)
            r2 = rstd[NBUF + b]
            stats_rstd(xb[b][:], n, sq[b], r2, ps[6 + k % 2], [xb[b]])
            tt("dve", yb[b][:], xb[b][:], r2[:, :n].unsqueeze(1).broadcast_to([128, KC, n]), ALU.mult,
               [xb[b], r2], [yb[b]])
            tt("pool", yb[b][:], yb[b][:], modD[:, l2, s_, ia, :].unsqueeze(2).broadcast_to([128, KC, n]),
               ALU.mult, [yb[b], modD], [yb[b]])
            tt("dve", hxT[:, :, t0:t0 + n], yb[b][:], modD[:, l2, s_, ib, :].unsqueeze(2).broadcast_to([128, KC, n]),
               ALU.add, [yb[b], modD], [("hxT", bi)])

        first(0)
        if nblk > 1:
            first(1)
        for k in range(nblk):
            second(k)
            if k + 2 < nblk:
                first(k + 2)
        P.barrier()
        P.sb_reset(m0)

    def gemm_fm(loaders, kcs, actT, akey, blocks, evac, wt, perm_wt=None, ncs=4, akc=None):
        for gi, load in enumerate(loaders):
            slab = wt[gi % 2]
            load(slab)
            pslab = None
            if perm_wt is not None and perm_wt[0](gi):
                pslab = perm_wt[1][gi % 2]
                v = slab[:].rearrange("p k (g b i) -> p k g b i", b=2, i=16)
                vp = pslab[:].rearrange("p k (g b i) -> p k g b i", b=2, i=16)
                for kh in range(2):
                    ksl = slice(kh * (kcs // 2), (kh + 1) * (kcs // 2))
                    cp("pool", vp[:, ksl, :, 0, :], v[:, ksl, :, 1, :], [slab], [(pslab, kh, 0)])
                    cp("pool", vp[:, ksl, :, 1, :], v[:, ksl, :, 0, :], [slab], [(pslab, kh, 1)])
            for bi, (t0, n) in enumerate(blocks):
                for c in range(ncs):
                    pa = bb.gbank()
                    for kc in range(kcs):
                        mm(pa[:, :n], slab[:, kc, c * 128:(c + 1) * 128], actT[:, kc, t0:t0 + n], kc == 0, kc == kcs - 1,
                           [slab, akc(kc) if akc is not None else akey(bi)], [pa])
                    pb = None
                    if pslab is not None:
                        pb = bb.gbank()
                        rk = [(pslab, kh, q) for kh in range(2) for q in range(2)]
                        for kc in range(kcs):
                            mm(pb[:, :n], pslab[:, kc, c * 128:(c + 1) * 128], actT[:, kc, t0:t0 + n], kc == 0,
                               kc == kcs - 1, rk + [akey(bi)], [pb])
                    evac(gi, c, bi, t0, n, pa, pb)

    def proj_phase(l, hxT):
        m0 = P.sb_mark()
        wt = [P.sb([128, KC, 512], BF16, f"wt{i}") for i in range(2)]
        wp = [P.sb([128, KC, 512], BF16, f"wp{i}") for i in range(2)]
        cosT = P.sb([128, T], F32, "cosT")
        sinT = P.sb([128, T], F32, "sinT")
        dma("sp", cosT[:], c_cos, (), [cosT])
        dma("sp", sinT[:], c_sin, (), [sinT])
        ev = [P.sb([128, 512], F32, f"ev{i}") for i in range(4)]
        evb = [P.sb([128, 512], BF16, f"evb{i}") for i in range(4)]
        st = {"i": 0}
        W = w_in[l]

        def ld(c0):
            def f(slab):
                dma("pool", slab[:], wslab(W, 0, KC, c0, 512), (), [slab])
            return f

        def evac(gi_abs):
            def f(gi, c, bi, t0, n, pa, pb):
                i = st["i"] % 4
                st["i"] += 1
                ch = gi_abs * 4 + c
                if gi_abs < 4:
                    tt("dve", ev[i][:, :n], pa[:, :n], cosT[:, t0:t0 + n], ALU.mult, [pa, cosT], [ev[i]])
                    j = (i + 1) % 4
                    st["i"] += 1
                    tt("dve", ev[j][:, :n], pb[:, :n], sinT[:, t0:t0 + n], ALU.mult, [pb, sinT], [ev[j]])
                    tt("pool", evb[i][:, :n], ev[i][:, :n], ev[j][:, :n], ALU.add, [ev[i], ev[j]], [evb[i]])
                    dma("sp", QKT[ch, :, t0:t0 + n], evb[i][:, :n], [evb[i]], [("QKT", ch)])
                elif gi_abs < 9:
                    cp("act", ev[i][:, :n], pa[:, :n], [pa], [ev[i]])
                    dma("sp", HYT[ch - 24, :, t0:t0 + n], ev[i][:, :n], [ev[i]], [("HYT", ch - 24)])
                elif gi_abs < 10:
                    cp("act", ev[i][:, :n], pa[:, :n], [pa], [ev[i]])
                    dma("sp", PLT[ch - 36, :, t0:t0 + n], ev[i][:, :n], [ev[i]], [("PLT", ch - 36)])
                else:
                    act(evb[i][:, :n], pa[:, :n], AF.Sigmoid, [pa], [evb[i]])
                    dma("sp", GT[ch - 40, :, t0:t0 + n], evb[i][:, :n], [evb[i]], [("GT", ch - 40)])
            return f

        akey = lambda bi: ("hxT", bi)
        for gi_abs in list(range(0, 4)) + list(range(6, 22)):
            gemm_fm([ld(gi_abs * 512)], KC, hxT, akey, TB, evac(gi_abs), wt,
                    perm_wt=((lambda g: True), wp) if gi_abs < 4 else None)
            wt.reverse()
            wp.reverse()
        va = [P.sb([128, 8, 129], BF16, f"va{i}") for i in range(2)]
        for i in range(2):
            ms("dve", va[i][:, :, 128:129], 1.0, [va[i]])
        wv = [wt[0], wt[1]]
        for g in range(2):
            dma("pool", wv[g][:], wslab(W, 0, KC, 2048 + g * 512, 512), (), [wv[g]])
        for it in range(NT):
            b = it % 2
            bi = min(it // 4, 4)
            for g in range(2):
                pa = bb.gbank()
                for kc in range(KC):
                    mm(pa[:], hxT[:, kc, it * 128:(it + 1) * 128], wv[g][:, kc, :], kc == 0, kc == KC - 1,
                       [wv[g], ("hxT", bi)], [pa])
                cp("act" if g == 0 else "dve", va[b][:, g * 4:(g + 1) * 4, 0:128],
                   pa[:].rearrange("p (h e) -> p h e", h=4), [pa], [va[b]])
            dma("sp", VA[it], va[b][:], [va[b]], ["VA"])
        P.barrier()
        P.sb_reset(m0)

    def attn_phase(l):
        m0 = P.sb_mark()
        lam_init = 0.8 - 0.6 * math.exp(-0.3 * l)
        lp = P.sb([128, 4, 64], F32, "lp")
        pr = P.sb([128, 2, 64], F32, "pr")
        s12 = P.sb([128, 2], F32, "s12")
        nlam = P.sb([128, 1], F32, "nlam")
        gcol = P.sb([128, 1], F32, "gcol")
        dma("sp", lp[:].rearrange("p a d -> p (a d)"), diff_lam[l:l + 1, :].broadcast_to([128, 256]), (), [lp])
        dma("sp", gcol[:], subln[l:l + 1, :].rearrange("o e -> e o"), (), [gcol])
        ts("dve", gcol[:], gcol[:], float(1.0 - lam_init), None, ALU.mult, None, [gcol], [gcol])
        tt("dve", pr[:, 0, :], lp[:, 0, :], lp[:, 1, :], ALU.mult, [lp], [pr])
        tt("dve", pr[:, 1, :], lp[:, 2, :], lp[:, 3, :], ALU.mult, [lp], [pr])
        P.op("dve", lambda e: e.reduce_sum(out=s12[:], in_=pr[:], axis=AX.X), [pr], [s12])
        act(s12[:], s12[:], AF.Exp, [s12], [s12])
        tt("dve", nlam[:], s12[:, 1:2], s12[:, 0:1], ALU.subtract, [s12], [nlam])
        ts("dve", nlam[:], nlam[:], -lam_init, None, ALU.add, None, [nlam], [nlam])
        qT = [P.sb([128, T], BF16, f"qT{i}") for i in range(2)]
        kT = [P.sb([128, T], BF16, f"kT{i}") for i in range(2)]
        vh = [P.sb([128, NT, 128], BF16, f"vh{i}") for i in range(2)]
        qz = [[P.sb([128, T], BF16, f"qz{i}_{m}") for m in range(2)] for i in range(2)]
        for i in range(2):
            ms("pool", qz[i][0][64:128, :], 0.0, [qz[i][0]])
            ms("pool", qz[i][1][0:64, :], 0.0, [qz[i][1]])
        E = [P.sb([128, 512], BF16, f"E{i}") for i in range(4)]
        rd = [P.sb([128, 512], F32, f"rd{i}") for i in range(2)]
        SBK = [ps[0], ps[1]]
        NJUNK = 0
        om = [P.sb([128, 512], F32, f"om{i}") for i in range(2)]
        attf_ = [P.sb([128, 512], F32, f"attf{i}") for i in range(2)]
        sqb_ = [P.sb([128, 512], BF16, f"sqb{i}") for i in range(2)]
        rs__ = [P.sb([128, 512], F32, f"ars{i}") for i in range(2)]
        pending = []
        atT = [P.sb([128, 512], BF16, f"atT{i}") for i in range(2)]
        state = {"ob": 0}
        oT = [ps[2], ps[3]]
        den = [ps[4], ps[5]]
        pstat = ps[6]

        def emit_load(h):
            hb = h % 2
            dma("sp", qT[hb][:], QKT[h], [("QKT", h)], [qT[hb]])
            dma("sp", kT[hb][:], QKT[8 + h], [("QKT", 8 + h)], [kT[hb]])
            dma("act", vh[hb][:], VA[:, :, h, 0:128].rearrange("i p e -> p i e"), ["VA"], [vh[hb]])
            cp("pool", qz[hb][0][0:64, :], qT[hb][0:64, :], [qT[hb]], [qz[hb][0]])
            cp("pool", qz[hb][1][64:128, :], qT[hb][64:128, :], [qT[hb]], [qz[hb][1]])

        steps = []
        for h in range(8):
            for bi, (q0, n) in enumerate(TB):
                kcs = list(range(NT)) if q0 < LX else [16, 17]
                for m in range(2):
                    for kc in kcs:
                        steps.append(dict(h=h, bi=bi, q0=q0, n=n, m=m, kc=kc, first=kc == kcs[0], last=kc == kcs[-1]))

        def emit_S(i):
            st_ = steps[i]
            hb, m, kc, q0, n = st_["h"] % 2, st_["m"], st_["kc"], st_["q0"], st_["n"]
            sb_ = SBK[i % 2]
            mm(sb_[:, :n], kT[hb][:, kc * 128:(kc + 1) * 128],
               qz[hb][m][:, q0:q0 + n], True, True, [kT[hb], qz[hb][m]], [sb_])

        def emit_PV(i):
            st_ = steps[i]
            h, m, kc, q0, n = st_["h"], st_["m"], st_["kc"], st_["q0"], st_["n"]
            hb = h % 2
            sb_ = SBK[i % 2]
            Et = E[i % 4]
            act(Et[:, :n], sb_[:, :n], AF.Exp, [sb_], [Et], scale=0.125)
            mm(oT[m][:, :n], vh[hb][:, kc, :], Et[:, :n], st_["first"], st_["last"], [Et, vh[hb]], [oT[m]])
            mm(den[m][:, :n], ones_b[:], Et[:, :n], st_["first"], st_["last"], [Et, ones_b], [den[m]])
            for _ in range(NJUNK):
                mm(ps[7][:, :n], ones_b[:], Et[:, :n], True, True, [], [])
            if not st_["last"]:
                return
            recip(rd[m][:, :n], den[m][:, :n], [den[m]], [rd[m]])
            tt("dve", om[m][:, :n], oT[m][:, :n], rd[m][:, :n], ALU.mult, [oT[m], rd[m]], [om[m]])
            if m == 0:
                return
            ob = state["ob"]
            state["ob"] += 1
            attf, sqb, rs_, o_t = attf_[ob % 2], sqb_[ob % 2], rs__[ob % 2], atT[ob % 2]
            stt(attf[:, :n], om[1][:, :n], nlam[:, 0:1], om[0][:, :n], ALU.mult, ALU.add, [om[0], om[1], nlam], [attf])
            tt("pool", sqb[:, :n], attf[:, :n], attf[:, :n], ALU.mult, [attf], [sqb])

            def stC():
                mm(pstat[:, :n], ones_b[:], sqb[:, :n], True, True, [sqb, ones_b], [pstat])

            def stD():
                act(rs_[:, :n], pstat[:, :n], AF.Ln, [pstat], [rs_], bias=EPS, scale=1.0 / 128)
                act(rs_[:, :n], rs_[:, :n], AF.Exp, [rs_], [rs_], scale=-0.5)

            def stE():
                tt("dve", attf[:, :n], attf[:, :n], rs_[:, :n], ALU.mult, [attf, rs_], [attf])
                ts("dve", o_t[:, :n], attf[:, :n], gcol[:, 0:1], None, ALU.mult, None, [attf, gcol], [o_t])
                dma("sp", BRT[h, :, q0:q0 + n], o_t[:, :n], [o_t], [("BRT", h)])
            pending.append((i + 5, stC))
            pending.append((i + 7, stD))
            pending.append((i + 9, stE))

        emit_load(0)
        emit_S(0)
        loaded = {0}
        for i in range(len(steps)):
            hn = steps[i]["h"] + 1
            if hn < 8 and hn not in loaded and steps[i]["bi"] >= 1:
                emit_load(hn)
                loaded.add(hn)
            if i + 1 < len(steps):
                emit_S(i + 1)
            emit_PV(i)
            while pending and pending[0][0] <= i:
                pending.pop(0)[1]()
        while pending:
            pending.pop(0)[1]()
        P.barrier()
        P.sb_reset(m0)

    def dwconv3(eng, out, src, w3, r, w):
        n = PW - 2
        ts(eng, out[:, 1:1 + n], src[:, 0:n], w3[:, 0:1], None, ALU.mult, None, r, w)
        if eng == "dve":
            stt(out[:, 1:1 + n], src[:, 1:1 + n], w3[:, 1:2], out[:, 1:1 + n], ALU.mult, ALU.add, r + w, w)
            stt(out[:, 1:1 + n], src[:, 2:2 + n], w3[:, 2:3], out[:, 1:1 + n], ALU.mult, ALU.add, r + w, w)
        else:
            raise NotImplementedError

    def load_padded(q, dst, src3, ch, w, rkeys):
        dma(q, dst[:, 1:1 + LX], src3[ch, :, 0:LX], rkeys, w)
        dma(q, dst[:, LX + 3:LX + 3 + LCX], src3[ch, :, LX:T], rkeys, w)

    def zero_pads(eng, t, w):
        ms(eng, t[:, 0:1], 0.0, w)
        ms(eng, t[:, LX + 1:LX + 3], 0.0, w)
        ms(eng, t[:, PW - 1:PW], 0.0, w)

    def filt_phase(l, sg):
        m0 = P.sb_mark()
        L, SC = sg["L"], sg["SC"]
        NB = 512 if L >= 512 else L
        w1 = P.sb([33, 64], F32, "w1")
        w2 = P.sb([64, 64], F32, "w2")
        w3 = P.sb([64, 2048], F32, "w3")
        pbf = P.sb([64, 4], F32, "pbf")
        zT = P.sb([33, L], F32, "zT")
        h1 = P.sb([64, L], F32, "h1")
        h2 = P.sb([64, L], F32, "h2")
        tq = P.sb([64, 512], F32, "tq")
        rq = P.sb([64, 512], F32, "rq")
        MAGIC = 12582912.0
        wsc = P.sb([128, 3, SC], F32, "wsc")
        dma("sp", w1[:], hw1[l], (), [w1])
        dma("sp", w2[:], hw2[l], (), [w2])
        dma("sp", w3[:], hw3[l], (), [w3])
        dma("sp", pbf[:], hpb[l], (), [pbf])
        dma("sp", zT[:], sg["zT"], (), [zT])
        dma("sp", wsc[:], sg["ws"], (), [wsc])
        OFF = math.pi + 16 * TWO_PI
        for (wm, kk, src, dst, ib, ifr) in ((w1, 33, zT, h1, 0, 1), (w2, 64, h1, h2, 2, 3)):
            for b0 in range(0, L, NB):
                pa = bb.gbank()
                mm(pa[0:64, :NB], wm[0:kk, :], src[0:kk, b0:b0 + NB], True, True, [wm, src], [pa])
                ts("dve", tq[:, :NB], pa[0:64, :NB], pbf[:, ib:ib + 1], pbf[:, ifr:ifr + 1], ALU.add, ALU.mult,
                   [pa, pbf], [tq])
                ts("dve", rq[:, :NB], tq[:, :NB], 1.0 / TWO_PI, MAGIC, ALU.mult, ALU.add, [tq], [rq])
                ts("dve", rq[:, :NB], rq[:, :NB], -MAGIC, -TWO_PI, ALU.add, ALU.mult, [rq], [rq])
                tt("dve", tq[:, :NB], tq[:, :NB], rq[:, :NB], ALU.add, [tq, rq], [tq])
                act(dst[:, b0:b0 + NB], tq[:, :NB], AF.Sin, [tq], [dst])
        dec = [P.sb([128, 512], F32, f"dec{i}") for i in range(2)]
        kr = [P.sb([128, 512], F32, f"kr{i}") for i in range(4)]
        ab = [P.sb([128, 512], F32, f"ab{i}") for i in range(4)]
        ksum = P.sb([128, SC, 1024], BF16, "ksum")
        kdif = P.sb([128, SC, 1024], BF16, "kdif")
        rn = [P.sb([128, 512], F32, f"rn{i}") for i in range(2)]
        psN = [ps[4], ps[5]]
        for lc in range(SC):
            d_ = dec[lc % 2]
            dma("sp", d_[:], sg["decay"][:, lc, :], (), [d_])
            for cb in range(4):
                pa = bb.gbank()
                mm(pa[:], h2[:, lc * 128:(lc + 1) * 128], w3[:, cb * 512:(cb + 1) * 512], True, True, [h2, w3], [pa])
                tt("dve", kr[cb][:], pa[:], d_[:], ALU.mult, [pa, d_], [kr[cb]])
                if lc == 0 and cb >= 2:
                    ms("dve", kr[cb][0:1, :], 0.0, [kr[cb]])
                act(ab[cb][:], kr[cb][:], AF.Abs, [kr[cb]], [ab[cb]])
            for o in range(2):
                for dr in range(2):
                    mm(psN[o][:], ones_f[:], ab[dr * 2 + o][:], lc == 0 and dr == 0, lc == SC - 1 and dr == 1,
                       [ab[dr * 2 + o], ones_f], [psN[o]])
                tt("pool", ksum[:, lc, o * 512:(o + 1) * 512], kr[o][:], kr[2 + o][:], ALU.add, [kr[o], kr[2 + o]],
                   [(ksum, lc, o)])
                tt("pool", kdif[:, lc, o * 512:(o + 1) * 512], kr[o][:], kr[2 + o][:], ALU.subtract,
                   [kr[o], kr[2 + o]], [(kdif, lc, o)])
        for o in range(2):
            ts("dve", rn[o][:], psN[o][:], EPS, None, ALU.add, None, [psN[o]], [rn[o]])
            recip(rn[o][:], rn[o][:], [rn[o]], [rn[o]])
        fcs = [P.sb([128, SC, 128], BF16, f"fcs{i}") for i in range(2)]
        fss = [P.sb([128, SC, 128], BF16, f"fss{i}") for i in range(2)]
        tab = [P.sb([128, 3, 512], F32, f"tab{i}") for i in range(2)]
        N = 2 * L
        it = 0
        for fc in range(SC):
            fb = fc % 2
            dma("sp", fcs[fb][:], sg["FW"][0, fc], (), [fcs[fb]])
            dma("act", fss[fb][:], sg["FW"][1, fc], (), [fss[fb]])
            for o in range(2):
                tb_ = tab[it % 2]
                it += 1
                pA = bb.gbank()
                pB = bb.gbank()
                ksk = [(ksum, lc, o) for lc in range(SC)]
                kdk = [(kdif, lc, o) for lc in range(SC)]
                for lc in range(SC):
                    mm(pA[:], fcs[fb][:, lc, :], ksum[:, lc, o * 512:(o + 1) * 512], lc == 0, lc == SC - 1,
                       [fcs[fb]] + ksk, [pA])
                for lc in range(SC):
                    mm(pB[:], fss[fb][:, lc, :], kdif[:, lc, o * 512:(o + 1) * 512], lc == 0, lc == SC - 1,
                       [fss[fb]] + kdk, [pB])
                stt(tb_[:, 0, :], pA[:], wsc[:, 0, fc:fc + 1], rn[o][:], ALU.mult, ALU.mult, [pA, wsc, rn[o]], [tb_])
                stt(tb_[:, 1, :], pB[:], wsc[:, 1, fc:fc + 1], rn[o][:], ALU.mult, ALU.mult, [pB, wsc, rn[o]], [tb_])
                stt(tb_[:, 2, :], pA[:], wsc[:, 2, fc:fc + 1], rn[o][:], ALU.mult, ALU.mult, [pA, wsc, rn[o]], [tb_])
                if fc == 0:
                    pC = bb.gbank()
                    for lc in range(SC):
                        mm(pC[0:1, :], fss[fb][:, lc, 0:1], ksum[:, lc, o * 512:(o + 1) * 512], lc == 0, lc == SC - 1,
                           [fss[fb]] + ksk, [pC])
                    stt(tb_[0:1, 2, :], pC[0:1, :], 1.0 / N, rn[o][0:1, :], ALU.mult, ALU.mult, [pC, rn[o], tb_], [tb_])
                dma("sp", sg["KTAB"][l, o, fc], tb_[:], [tb_], [("KTAB", sg["nm"])])
        P.barrier()
        P.sb_reset(m0)

    def hyena_phase(l):
        m0 = P.sb_mark()
        sw = P.sb([128, 12, 3], F32, "sw")
        hb = P.sb([128, 2, 4], F32, "hb")
        dma("sp", sw[:], hsw[l], (), [sw])
        dma("sp", hb[:], hbias[l], (), [hb])
        uT = P.sb([128, 4, PW], F32, "uT")
        mT = P.sb([128, 4, PW], F32, "mT")
        utok = P.sb([128, NT, 512], BF16, "utok")
        mk_a = P.sb_mark()
        raw = [P.sb([128, PW], F32, f"raw{i}") for i in range(2)]
        ubf = P.sb([128, 4, T], BF16, "ubf")
        P.sb_reset(mk_a)
        Y = P.sb([128, 32, 512], BF16, "Y")
        fcs = [P.sb([128, 16, 128], BF16, f"hfc{i}") for i in range(2)]
        fss = [P.sb([128, 16, 128], BF16, f"hfs{i}") for i in range(2)]
        tab = [P.sb([128, 3, 512], F32, f"htab{i}") for i in range(2)]
        gv = [P.sb([128, 32, 256], BF16, f"gv{i}") for i in range(2)]
        tm = [P.sb([128, 512], F32, f"htm{i}") for i in range(4)]
        ob = [P.sb([128, 256], BF16, f"hob{i}") for i in range(2)]
        ri = 0

        def conv_chunks(c0, dst):
            nonlocal ri
            for cc in range(4):
                r_ = raw[ri % 2]
                ri += 1
                load_padded("sp", r_, HYT, c0 + cc, [r_], [("HYT", c0 + cc)])
                dwconv3("dve", dst[:, cc, :], r_, sw[:, c0 + cc, :], [r_, sw], [(dst, cc)])

        ti = 0
        gi = 0
        for o in range(2):
            P.barrier()
            for i in range(2):
                zero_pads("dve", raw[i], [raw[i]])
            if o == 0:
                conv_chunks(0, uT)
            conv_chunks(4 + 4 * o, mT)
            for cc in range(4):
                cp("pool", ubf[:, cc, 0:LX], uT[:, cc, 1:1 + LX], [(uT, cc)], [(ubf, cc)])
                cp("pool", ubf[:, cc, LX:T], uT[:, cc, LX + 3:LX + 3 + LCX], [(uT, cc)], [(ubf, cc)])
            tpi = 0
            for cc in range(4):
                for s0 in range(0, NT, 8):
                    ns = min(8, NT - s0)
                    pt = ps[6 + tpi % 2]
                    tpi += 1
                    ptb = pt[:].bitcast(BF16)
                    for k in range(ns):
                        tr(ptb[:, k * 128:(k + 1) * 128], ubf[:, cc, (s0 + k) * 128:(s0 + k + 1) * 128],
                           [(ubf, cc), ident], [pt])
                    cp("act", utok[:, s0:s0 + ns, cc * 128:(cc + 1) * 128],
                       ptb[:, :ns * 128].rearrange("p (k c) -> p k c", c=128), [pt], [(utok, cc, s0)])
            ukeys = [(utok, cc, s0) for cc in range(4) for s0 in range(0, NT, 8)]
            P.barrier()
            for sg in segs:
                SC, L, tk0 = sg["SC"], sg["L"], sg["t0"] // 128
                for fc in range(SC):
                    fb = gi % 2
                    gi += 1
                    dma("sp", fcs[fb][:, :SC, :], sg["FW"][0, fc], (), [fcs[fb]])
                    dma("act", fss[fb][:, :SC, :], sg["FW"][1, fc], (), [fss[fb]])
                    dma("sp", tab[fb][:], sg["KTAB"][l, o, fc], [("KTAB", sg["nm"])], [tab[fb]])
                    pc = ps[4] if fc % 2 == 0 else ps[2]
                    pS = ps[5] if fc % 2 == 0 else ps[3]
                    for sc in range(SC):
                        mm(pc[:], fcs[fb][:, sc, :], utok[:, tk0 + sc, :], sc == 0, sc == SC - 1, [fcs[fb]] + ukeys, [pc])
                    for sc in range(SC):
                        mm(pS[:], fss[fb][:, sc, :], utok[:, tk0 + sc, :], sc == 0, sc == SC - 1, [fss[fb]] + ukeys, [pS])
                    a_, b_, c_, d_ = tm[0], tm[1], tm[2], tm[3]
                    tt("dve", a_[:], pc[:], tab[fb][:, 0, :], ALU.mult, [pc, tab[fb]], [a_])
                    tt("dve", b_[:], pS[:], tab[fb][:, 1, :], ALU.mult, [pS, tab[fb]], [b_])
                    tt("pool", Y[:, fc, :], a_[:], b_[:], ALU.subtract, [a_, b_], [(Y, fc)])
                    tt("dve", c_[:], pc[:], tab[fb][:, 1, :], ALU.mult, [pc, tab[fb]], [c_])
                    tt("dve", d_[:], pS[:], tab[fb][:, 2, :], ALU.mult, [pS, tab[fb]], [d_])
                    tt("pool", Y[:, SC + fc, :], c_[:], d_[:], ALU.add, [c_, d_], [(Y, SC + fc)])
                ykeys = [(Y, k) for k in range(2 * SC)]
                TBI = sg["TBI"]
                for tbk in range(L // TBI):
                    g_ = gv[ti % 2]
                    ti += 1
                    dma("sp", g_[:, :2 * SC, :], sg["GV"][tbk], (), [g_])
                    tt0 = sg["t0"] + tbk * TBI
                    p0 = pidx(tt0)
                    for cc in range(4):
                        pa = bb.gbank()
                        for k in range(2 * SC):
                            mm(pa[:, :TBI], Y[:, k, cc * 128:(cc + 1) * 128], g_[:, k, :], k == 0, k == 2 * SC - 1,
                               ykeys + [g_], [pa])
                        t_ = tm[(cc) % 4]
                        stt(t_[:, :TBI], uT[:, cc, p0:p0 + TBI], hb[:, o, cc:cc + 1], pa[:, :TBI], ALU.mult, ALU.add,
                            [(uT, cc), hb, pa], [t_])
                        if o == 0:
                            tt("pool", uT[:, cc, p0:p0 + TBI], t_[:, :TBI], mT[:, cc, p0:p0 + TBI], ALU.mult,
                               [t_, (mT, cc)], [(uT, cc)])
                        else:
                            o_ = ob[cc % 2]
                            tt("pool", o_[:, :TBI], t_[:, :TBI], mT[:, cc, p0:p0 + TBI], ALU.mult, [t_, (mT, cc)], [o_])
                            dma("sp", BRT[8 + cc, :, tt0:tt0 + TBI], o_[:, :TBI], [o_], [("BRT", 8 + cc)])
        P.barrier()
        P.sb_reset(m0)

    def pool_phase(l):
        m0 = P.sb_mark()
        PP = T + 32
        xo, co = 8, LX + 24
        pw = P.sb([128, 4, 128], BF16, "pw")
        psc = P.sb([128, 4], F32, "psc")
        corr = P.sb([128, 4, 16], F32, "corr")
        dma("pool", pw[:], pool_w[l].rearrange("g i o -> i g o"), (), [pw])
        dma("sp", psc[:], pscale[l], (), [psc])
        dma("sp", corr[:], c_corr, (), [corr])
        for g, win in enumerate((2, 4, 8, 16)):
            pin = P.sb([128, PP], F32, f"pin{g}")
            A_ = P.sb([128, PP], F32, f"pA{g}")
            B_ = P.sb([128, PP], F32, f"pB{g}")
            pm = P.sb([128, T], BF16, f"pm{g}")
            ms("pool", pin[:], 0.0, [pin])
            dma("sp", pin[:, xo:xo + LX], PLT[g, :, 0:LX], [("PLT", g)], [pin])
            dma("sp", pin[:, co:co + LCX], PLT[g, :, LX:T], [("PLT", g)], [pin])
            lo, hi = 8, PP - 8
            nn = hi - lo
            if win == 2:
                tt("dve", A_[:, lo:hi], pin[:, lo - 1:hi - 1], pin[:, lo:hi], ALU.add, [pin], [A_])
                S = A_
            else:
                tt("dve", A_[:, 0:PP - 1], pin[:, 0:PP - 1], pin[:, 1:PP], ALU.add, [pin], [A_])
                if win == 4:
                    tt("dve", B_[:, lo:hi], A_[:, lo - 2:hi - 2], A_[:, lo:hi], ALU.add, [A_], [B_])
                    S = B_
                else:
                    tt("dve", B_[:, 0:PP - 3], A_[:, 0:PP - 3], A_[:, 2:PP - 1], ALU.add, [A_], [B_])
                    if win == 8:
                        tt("dve", A_[:, lo:hi], B_[:, lo - 4:hi - 4], B_[:, lo:hi], ALU.add, [B_, A_], [A_])
                        S = A_
                    else:
                        tt("dve", A_[:, 0:PP - 7], B_[:, 0:PP - 7], B_[:, 4:PP - 3], ALU.add, [B_, A_], [A_])
                        tt("dve", B_[:, lo:hi], A_[:, lo - 8:hi - 8], A_[:, lo:hi], ALU.add, [A_, B_], [B_])
                        S = B_
            ts("dve", S[:, lo:hi], S[:, lo:hi], 1.0 / win, None, ALU.mult, None, [S], [S])
            for (o_, Ls) in ((xo, LX), (co, LCX)):
                tt("dve", S[:, o_:o_ + 8], S[:, o_:o_ + 8], corr[:, g, 0:8], ALU.mult, [S, corr], [S])
                tt("dve", S[:, o_ + Ls - 8:o_ + Ls], S[:, o_ + Ls - 8:o_ + Ls], corr[:, g, 8:16], ALU.mult, [S, corr], [S])
            tt("dve", pm[:, 0:LX], S[:, xo:xo + LX], pin[:, xo:xo + LX], ALU.subtract, [S, pin], [pm])
            tt("dve", pm[:, LX:T], S[:, co:co + LCX], pin[:, co:co + LCX], ALU.subtract, [S, pin], [pm])
            for bi, (t0, n) in enumerate(TB):
                pa = bb.gbank()
                mm(pa[:, :n], pw[:, g, :], pm[:, t0:t0 + n], True, True, [pw, pm], [pa])
                o_t = P.sb([128, 512], BF16, f"po{g}_{bi}")
                ts("dve", o_t[:, :n], pa[:, :n], psc[:, g:g + 1], None, ALU.mult, None, [pa, psc], [o_t])
                dma("sp", BRT[12 + g, :, t0:t0 + n], o_t[:, :n], [o_t], [("BRT", 12 + g)])
        P.barrier()
        P.sb_reset(m0)

    def merge_phase(l):
        m0 = P.sb_mark()
        brt = P.sb([128, 16, T], BF16, "brt")
        mgT = P.sb([128, 16, T], BF16, "mgT")
        wt = [P.sb([128, KC, 512], BF16, f"mwt{i}") for i in range(2)]
        g3 = [P.sb([128, 3, 512], BF16, f"g3{i}") for i in range(2)]
        t3 = [P.sb([128, 3, 512], F32, f"t3{i}") for i in range(2)]
        ev = [P.sb([128, 512], F32, f"mev{i}") for i in range(2)]
        for q4 in range(4):
            dma("sp", brt[:, q4 * 4:(q4 + 1) * 4, :], BRT[q4 * 4:(q4 + 1) * 4].rearrange("c p t -> p c t"),
                [("BRT", c) for c in range(q4 * 4, q4 * 4 + 4)], [(brt, q4)])
        GT4 = GT.rearrange("(b c) p t -> b c p t", b=3)
        it = 0
        for ng in range(4):
            slab = wt[ng % 2]
            dma("pool", slab[:, 0:8, :], wslab(w_att_o[l], 0, 8, ng * 512, 512), (), [(slab, 0)])
            dma("pool", slab[:, 8:12, :], wslab(w_hy_o[l], 0, 4, ng * 512, 512), (), [(slab, 1)])
            dma("pool", slab[:, 12:16, :], wslab(w_pool_o[l], 0, 4, ng * 512, 512), (), [(slab, 2)])
            for bi, (t0, n) in enumerate(TB):
                for c in range(4):
                    nch = ng * 4 + c
                    b = it % 2
                    it += 1
                    dma("sp", g3[b][:, :, :n], GT4[:, nch, :, t0:t0 + n].rearrange("b p t -> p b t"),
                        [("GT", br * 16 + nch) for br in range(3)], [g3[b]])
                    for br, (k0, k1) in enumerate(((0, 8), (8, 12), (12, 16))):
                        pa = bb.gbank()
                        q4s = [(brt, q) for q in ((0, 1) if br == 0 else (2,) if br == 1 else (3,))]
                        for kc in range(k0, k1):
                            mm(pa[:, :n], slab[:, kc, c * 128:(c + 1) * 128], brt[:, kc, t0:t0 + n], kc == k0, kc == k1 - 1,
                               [(slab, br)] + q4s, [pa])
                        tt("dve", t3[b][:, br, :n], pa[:, :n], g3[b][:, br, :n], ALU.mult, [pa, g3[b]], [(t3[b], br)])
                    tt("pool", t3[b][:, 0, :n], t3[b][:, 0, :n], t3[b][:, 1, :n], ALU.add, [(t3[b], 0), (t3[b], 1)],
                       [(t3[b], 0)])
                    tt("pool", mgT[:, nch, t0:t0 + n], t3[b][:, 0, :n], t3[b][:, 2, :n], ALU.add,
                       [(t3[b], 0), (t3[b], 2)], [(mgT, bi, nch)])
        P.barrier()

        def ld(c0):
            def f(slab):
                dma("pool", slab[:], wslab(w_out[l], 0, KC, c0, 512), (), [slab])
            return f
        st = {"i": 0}

        def evac(gi, c, bi, t0, n, pa, pb):
            i = st["i"] % 2
            st["i"] += 1
            cp("act" if i == 0 else "dve", ev[i][:, :n], pa[:, :n], [pa], [ev[i]])
            dma("sp", YT[gi * 4 + c, :, t0:t0 + n], ev[i][:, :n], [ev[i]], ["YT"])
        gemm_fm([ld(g * 512) for g in range(4)], KC, mgT, lambda bi: "nokey", TB, evac, wt)
        P.barrier()
        P.sb_reset(m0)

    def ffn_phase(l, h2T):
        m0 = P.sb_mark()
        cw = P.sb([128, FJ, 3], F32, "cw")
        dma("sp", cw[:], fcw[l], (), [cw])
        wg = [P.sb([128, KC, 256], BF16, f"wg{i}") for i in range(2)]
        wv = [P.sb([128, KC, 256], BF16, f"wv{i}") for i in range(2)]
        gp = [P.sb([128, PW], F32, f"gp{i}") for i in range(2)]
        vv = [P.sb([128, T], F32, f"vv{i}") for i in range(2)]
        cv = P.sb([128, PW], F32, "cv")
        x2 = P.sb([128, PW], F32, "x2")
        uu = P.sb([128, PW], F32, "uu")
        sg_ = P.sb([128, PW], F32, "sg")
        abf = [P.sb([128, T], BF16, f"abf{i}") for i in range(2)]
        for i in range(2):
            zero_pads("dve", gp[i], [gp[i]])
        n_ = PW - 2
        GC = 2.0 * math.sqrt(2.0 / math.pi)
        W = w_up[l]
        pending = []

        def chain(j, b):
            def c1():
                dwconv3("dve", cv, gp[b], cw[:, j, :], [gp[b], cw], [cv])

            def c2():
                act(x2[:, 1:1 + n_], cv[:, 1:1 + n_], AF.Square, [cv], [x2])
                ts("pool", x2[:, 1:1 + n_], x2[:, 1:1 + n_], 0.044715, 1.0, ALU.mult, ALU.add, [x2], [x2])
                tt("pool", uu[:, 1:1 + n_], x2[:, 1:1 + n_], cv[:, 1:1 + n_], ALU.mult, [x2, cv], [uu])

            def c3():
                act(sg_[:, 1:1 + n_], uu[:, 1:1 + n_], AF.Sigmoid, [uu], [sg_], scale=GC)
                tt("pool", uu[:, 1:1 + n_], cv[:, 1:1 + n_], sg_[:, 1:1 + n_], ALU.mult, [cv, sg_, uu], [uu])

            def c4():
                tt("dve", abf[b][:, 0:LX], uu[:, 1:1 + LX], vv[b][:, 0:LX], ALU.mult, [uu, vv[b]], [(abf[b], 0)])
                tt("dve", abf[b][:, LX:T], uu[:, LX + 3:LX + 3 + LCX], vv[b][:, LX:T], ALU.mult, [uu, vv[b]],
                   [(abf[b], 1)])
                dma("sp", AT[j], abf[b][:], [(abf[b], 0), (abf[b], 1)], ["AT"])
            return [c1, c2, c3, c4]

        for g2 in range(FJ // 2):
            sb_ = g2 % 2
            dma("pool", wg[sb_][:], wslab(W, 0, KC, g2 * 256, 256), (), [wg[sb_]])
            dma("pool", wv[sb_][:], wslab(W, 0, KC, FF + g2 * 256, 256), (), [wv[sb_]])
            for c in range(2):
                j = g2 * 2 + c
                b = j % 2
                for bi, (t0, n) in enumerate(TB):
                    pa = bb.gbank()
                    for kc in range(KC):
                        mm(pa[:, :n], wg[sb_][:, kc, c * 128:(c + 1) * 128], h2T[:, kc, t0:t0 + n], kc == 0, kc == KC - 1,
                           [wg[sb_]], [pa])
                    cp("act", gp[b][:, pidx(t0):pidx(t0) + n], pa[:, :n], [pa], [gp[b]])
                    pb = bb.gbank()
                    for kc in range(KC):
                        mm(pb[:, :n], wv[sb_][:, kc, c * 128:(c + 1) * 128], h2T[:, kc, t0:t0 + n], kc == 0, kc == KC - 1,
                           [wv[sb_]], [pb])
                    cp("dve", vv[b][:, t0:t0 + n], pb[:, :n], [pb], [vv[b]])
                    if pending:
                        pending.pop(0)()
                pending.extend(chain(j, b))
        for f_ in pending:
            f_()
        P.barrier()
        P.sb_reset(m0)

    def down_phase(l):
        m0 = P.sb_mark()
        HT = T // 2
        aT = P.sb([128, FJ, HT], BF16, "aT")
        wd = [P.sb([128, FJ, 256], BF16, f"wd{i}") for i in range(2)]
        ev = [P.sb([128, 512], F32, f"dev{i}") for i in range(2)]
        st = {"i": 0}
        for half in range(2):
            h0 = half * HT
            for q4 in range(4):
                dma("sp" if q4 % 2 == 0 else "act", aT[:, q4 * 11:(q4 + 1) * 11, :],
                    blk(AT, q4 * 11, (q4 + 1) * 11, h0, HT), ["AT"], [(aT, q4)])

            def ld(c0):
                def f(slab):
                    dma("pool", slab[:], w_down[l][:, c0:c0 + 256].rearrange("(j p) n -> p j n", p=128), (), [slab])
                return f

            def evac(gi, c, bi, t0, n, pa, pb, h0=h0):
                i = st["i"] % 2
                st["i"] += 1
                cp("act" if i == 0 else "dve", ev[i][:, :n], pa[:, :n], [pa], [ev[i]])
                dma("sp", YT[gi * 2 + c, :, h0 + t0:h0 + t0 + n], ev[i][:, :n], [ev[i]], ["YT"])
            gemm_fm([ld(g * 256) for g in range(8)], FJ, aT, None, [(0, 512), (512, 512), (1024, 128)], evac, wd,
                    ncs=2, akc=lambda kc: (aT, kc // 11))
            P.barrier()
        P.sb_reset(m0)

    ada_phase()
    for l in range(nlayers):
        for sg in segs:
            filt_phase(l, sg)
    mk = P.sb_mark()
    hxT = P.sb([128, KC, T], BF16, "hxT")
    norm_pass(0, 0, 1, hxT)
    for l in range(nlayers):
        proj_phase(l, hxT)
        P.sb_reset(mk)
        if stop_after == "proj":
            break
        attn_phase(l)
        hyena_phase(l)
        pool_phase(l)
        if stop_after == "branches":
            break
        merge_phase(l)
        mk = P.sb_mark()
        hxT = P.sb([128, KC, T], BF16, "h2T")
        resid_norm_pass(l, 2, l, 3, 4, hxT)
        if stop_after == "mixer":
            break
        ffn_phase(l, hxT)
        P.sb_reset(mk)
        down_phase(l)
        if l + 1 < nlayers:
            mk = P.sb_mark()
            hxT = P.sb([128, KC, T], BF16, "hxT")
            resid_norm_pass(l, 5, l + 1, 0, 1, hxT)
        else:
            resid_pass(l, 5)
    dma("sp", outT, XT[:, :, 0:LX], ["XT"], ["outT"])
    P.barrier()
    return P.emit(), P


def _bf16(a):
    return np.asarray(a, dtype=np.float32).astype(ml_dtypes.bfloat16)


def _dft_consts(L):
    N = 2 * L
    SC = L // 128
    ct = np.cos(2.0 * np.pi * np.arange(N) / N)
    st = np.sin(2.0 * np.pi * np.arange(N) / N)
    s = np.arange(L)
    idx = (s[:, None] * s[None, :]) % N
    Mc = ct[idx]
    Ms = st[idx]
    Fs = Ms.copy()
    Fs[:, 0] = (-1.0) ** s
    Gs = Ms.copy()
    Gs[0, :] = (-1.0) ** s
    def fw(M):
        return M.reshape(SC, 128, SC, 128).transpose(2, 1, 0, 3)
    FW = np.stack([fw(Mc), fw(Fs)], axis=0)
    TBI = 256
    def gv(M):
        return M.reshape(SC, 128, L // TBI, TBI).transpose(2, 1, 0, 3)
    GV = np.concatenate([gv(Mc), gv(Gs)], axis=2)
    ws = np.full((128, 3, SC), 2.0 / N, dtype=np.float32)
    ws[0, 0, 0] = 1.0 / N
    ws[0, 1, 0] = 0.0
    f32 = np.float32
    t = np.linspace(0.0, 1.0, L, dtype=f32)[:, None]
    w_ang = (f32(2.0 * math.pi) * np.arange(L, dtype=f32)[:, None] / f32(L)).astype(f32)
    fb = np.linspace(1e-4, 15, 16, dtype=f32)[None, :]
    ang = (fb * w_ang).astype(f32)
    z = np.concatenate([t, np.cos(ang), -np.sin(ang)], axis=-1).astype(f32)
    zT = np.ascontiguousarray(z.T)
    dmin = math.log(1e-2) / 1.5
    dmax = math.log(1e-2) / 0.3
    deltas = np.abs(np.linspace(dmin, dmax, 512, dtype=f32))
    decay = np.exp(-t * deltas[None, :]).astype(f32)
    decay = np.ascontiguousarray(decay.reshape(SC, 128, 512).transpose(1, 0, 2))
    return dict(FW=_bf16(np.ascontiguousarray(FW)), GV=_bf16(np.ascontiguousarray(GV)), ws=ws, zT=zT, decay=decay)


def _rope_consts():
    f32 = np.float32
    inv = (f32(10000.0) ** (-np.arange(16, dtype=f32) / f32(16))).astype(f32)
    tt_ = np.arange(LX)
    row = (tt_ // 64).astype(f32)
    col = (tt_ % 64).astype(f32)
    cosT = np.ones((128, T), dtype=f32)
    sinT = np.zeros((128, T), dtype=f32)
    for m in range(2):
        for a in range(2):
            pos = row if a == 0 else col
            ang = (pos[None, :] * inv[:, None]).astype(f32)
            for b in range(2):
                p0 = m * 64 + a * 32 + b * 16
                cosT[p0:p0 + 16, :LX] = np.cos(ang)
                sinT[p0:p0 + 16, :LX] = (-1.0 if b == 0 else 1.0) * np.sin(ang)
    return cosT, sinT


def _pool_corr():
    corr = np.ones((128, 4, 16), dtype=np.float32)
    for g, win in enumerate((2, 4, 8, 16)):
        h = win // 2
        for pos in range(8):
            cnt = min(pos, h) + h
            corr[:, g, pos] = win / cnt
            rem = 8 - pos
            cnt2 = h + min(h, rem)
            corr[:, g, 8 + pos] = win / cnt2
    return corr


_CACHE = {}


def _consts():
    if "c" not in _CACHE:
        cx = _dft_consts(LX)
        cc = _dft_consts(LCX)
        cosT, sinT = _rope_consts()
        m = {"c_ident": np.eye(128, dtype=np.float32), "c_cos": cosT, "c_sin": sinT, "c_corr": _pool_corr()}
        for nm, c in (("x", cx), ("c", cc)):
            m[f"c_fw_{nm}"] = c["FW"]
            m[f"c_gv_{nm}"] = c["GV"]
            m[f"c_zT_{nm}"] = c["zT"]
            m[f"c_decay_{nm}"] = c["decay"]
            m[f"c_ws_{nm}"] = c["ws"]
        _CACHE["c"] = m
    return _CACHE["c"]


def make_in_maps(inputs, ncore=NCORE):
    f = lambda a: np.ascontiguousarray(np.asarray(a, dtype=np.float32))
    I = {k: np.asarray(v) for k, v in inputs.items()}
    shared = dict(_consts())
    shared["w_ada"] = f(I["w_ada"])
    shared["b_ada"] = f(I["b_ada"])
    shared["gT"] = f(I["norm_g"].reshape(DEPTH, 4, KC, 128).transpose(0, 3, 1, 2))
    shared["w_in"] = f(I["w_in"])
    shared["diff_lam"] = f(I["diff_lam"].reshape(DEPTH, 256))
    shared["attn_subln_g"] = f(I["attn_subln_g"])
    shared["hsw"] = f(I["hy_short_w"].reshape(DEPTH, 3, 12, 128).transpose(0, 3, 2, 1))
    shared["hy_ffn_w1"] = f(I["hy_ffn_w1"])
    shared["hy_ffn_w2"] = f(I["hy_ffn_w2"])
    shared["hy_ffn_w3"] = f(I["hy_ffn_w3"])
    shared["hpb"] = f(np.stack([I["hy_ffn_b1"], I["hy_freq"][:, 0], I["hy_ffn_b2"], I["hy_freq"][:, 1]], axis=-1))
    shared["hbias"] = f(I["hy_bias"].reshape(DEPTH, 2, 4, 128).transpose(0, 3, 1, 2))
    shared["pool_w"] = f(I["pool_w"])
    shared["pscale"] = f(I["pool_scale"].reshape(DEPTH, 4, 128).transpose(0, 2, 1))
    shared["w_att_o"] = f(I["w_att_o"])
    shared["w_hy_o"] = f(I["w_hy_o"])
    shared["w_pool_o"] = f(I["w_pool_o"])
    shared["w_out"] = f(I["w_out"])
    shared["w_up"] = f(I["w_up"])
    shared["fcw"] = f(I["ff_conv_w"].reshape(DEPTH, 3, FJ, 128).transpose(0, 3, 2, 1))
    shared["w_down"] = f(I["w_down"])
    maps = []
    for b in range(ncore):
        X = np.concatenate([I["x"][b], I["ctx"][b]], axis=0)
        m = dict(shared)
        m["xT"] = f(X.T.reshape(KC, 128, T))
        cc = np.stack([I["c"][b], I["c_ctx"]], axis=0)
        m["ccT"] = f(cc.reshape(2, KC, 128).transpose(2, 1, 0))
        maps.append(m)
    return maps


def kernel(**inputs):
    if "nc" not in _CACHE:
        _CACHE["nc"] = build()[0]
    nc = _CACHE["nc"]
    maps = make_in_maps(inputs)
    res = run_bass_kernel_spmd(nc, maps, core_ids=list(range(NCORE)))
    outs = []
    for b in range(NCORE):
        oT = np.asarray(res.results[b]["outT"], dtype=np.float32)
        outs.append(np.ascontiguousarray(oT.reshape(D, LX).T))
    return np.stack(outs, axis=0)
```

```python
import numpy as np
import concourse.bass as bass
import concourse.mybir as mybir

F32 = mybir.dt.float32
BF16 = mybir.dt.bfloat16
AF = mybir.ActivationFunctionType
ALU = mybir.AluOpType
AX = mybir.AxisListType

ENGS = ("sp", "act", "dve", "pool", "pe")
NSLOT = 12
SEM_CAP = 30000


class _Op:
    __slots__ = ("fn", "deps", "dma", "signal", "sigcount", "slot", "val", "prev", "semidx")

    def __init__(self, fn, deps, dma):
        self.fn = fn
        self.deps = deps
        self.dma = dma
        self.signal = False
        self.sigcount = 0
        self.slot = 0
        self.val = 0
        self.prev = None
        self.semidx = 0


def _key(x):
    if isinstance(x, tuple):
        return tuple(_key(y) for y in x)
    if isinstance(x, (str, int)):
        return x
    return ("id", id(x))


class Prog:
    def __init__(self, same_engine_sync=True):
        self.nc = bass.Bass("TRN2", target_bir_lowering=False)
        self.ops = {e: [] for e in ENGS}
        self.lastw = {}
        self.readers = {}
        self.same = same_engine_sync
        self.sb_off = 16384
        self.sb_hw = 0
        self.ndma = {e: 0 for e in ENGS}
        self.slot_last = {}
        self._n = 0

    def sb(self, shape, dt, name=None):
        self._n += 1
        name = name or f"sb{self._n}"
        esz = 4 if dt == F32 else 2
        if dt in (mybir.dt.int32, mybir.dt.uint32):
            esz = 4
        per_part = int(np.prod(shape[1:])) * esz
        per_part = (per_part + 63) // 64 * 64
        t = self.nc.alloc_sbuf_tensor_at(f"{name}_{self._n}", list(shape), dt, offset=self.sb_off)
        self.sb_off += per_part
        self.sb_hw = max(self.sb_hw, self.sb_off)
        assert self.sb_off <= 16384 + 212000, f"SBUF overflow {self.sb_off}"
        return t

    def sb_mark(self):
        return self.sb_off

    def sb_reset(self, mark):
        self.sb_off = mark

    def dram(self, name, shape, dt, kind="Internal"):
        return self.nc.dram_tensor(name, list(shape), dt, kind=kind)

    def op(self, eng, fn, reads=(), writes=(), dma=False):
        reads = [_key(r) for r in reads]
        writes = [_key(w) for w in writes]
        deps = set()
        for r in reads:
            lw = self.lastw.get(r)
            if lw is not None:
                deps.add(lw)
        for w in writes:
            lw = self.lastw.get(w)
            if lw is not None:
                deps.add(lw)
            for rd in self.readers.get(w, ()):
                deps.add(rd)
        idx = len(self.ops[eng])
        me = (eng, idx)
        o = _Op(fn, None, dma)
        fdeps = []
        for d in deps:
            if d == me:
                continue
            tgt = self.ops[d[0]][d[1]]
            if d[0] == eng and not tgt.dma:
                if eng == "pe" or not self.same:
                    continue
            fdeps.append(d)
        best = {}
        pruned = []
        for d in fdeps:
            if self.ops[d[0]][d[1]].dma:
                pruned.append(d)
            elif d[0] not in best or d[1] > best[d[0]]:
                best[d[0]] = d[1]
        pruned.extend(best.items())
        o.deps = pruned
        if dma:
            j = self.ndma[eng]
            self.ndma[eng] += 1
            o.slot = j % NSLOT
            o.val = 16 * (j // NSLOT + 1)
            o.prev = self.slot_last.get((eng, o.slot))
            self.slot_last[(eng, o.slot)] = me
        self.ops[eng].append(o)
        for w in writes:
            self.lastw[w] = me
            self.readers[w] = []
        for r in reads:
            self.readers.setdefault(r, []).append(me)
        return me

    def barrier(self):
        lasts = []
        for e in ENGS:
            if self.ops[e]:
                for i in range(len(self.ops[e]) - 1, -1, -1):
                    if not self.ops[e][i].dma and self.ops[e][i].fn is not None:
                        lasts.append((e, i))
                        break
            for s in range(NSLOT):
                l = self.slot_last.get((e, s))
                if l is not None:
                    lasts.append(l)
        self._barrier_deps = lasts
        for e in ENGS:
            o = _Op(None, [d for d in lasts if not (d[0] == e and not self.ops[d[0]][d[1]].dma and e == "pe")], False)
            self.ops[e].append(o)
        self.lastw = {}
        self.readers = {}

    def emit(self):
        nc = self.nc
        for e in ENGS:
            for o in self.ops[e]:
                for (e2, i2) in o.deps:
                    t = self.ops[e2][i2]
                    if not t.dma:
                        t.signal = True
        nsem = {}
        for e in ENGS:
            c = 0
            for o in self.ops[e]:
                if o.dma or o.fn is None:
                    continue
                if o.signal:
                    c += 1
                    o.semidx = (c - 1) // SEM_CAP
                    o.sigcount = c - o.semidx * SEM_CAP
            nsem[e] = max(1, (c + SEM_CAP - 1) // SEM_CAP)
        from contextlib import ExitStack
        with ExitStack() as es:
            csem = {e: [es.enter_context(nc.semaphore(f"c_{e}_{k}")) for k in range(nsem[e])] for e in ENGS}
            dsem = {e: [es.enter_context(nc.semaphore(f"d_{e}_{s}")) for s in range(NSLOT)]
                    for e in ENGS if self.ndma[e] > 0}
            block = es.enter_context(nc.Block())
            ops = self.ops

            def replay(e, eng):
                waited = {}

                def wait_for(d):
                    t = ops[d[0]][d[1]]
                    if t.dma:
                        key = ("d", d[0], t.slot)
                        sem = dsem[d[0]][t.slot]
                        val = t.val
                    else:
                        if t.fn is None:
                            return
                        key = ("c", d[0], t.semidx)
                        sem = csem[d[0]][t.semidx]
                        val = t.sigcount
                    if waited.get(key, 0) < val:
                        eng.wait_ge(sem, val)
                        waited[key] = val

                for o in ops[e]:
                    for d in o.deps:
                        wait_for(d)
                    if o.fn is None:
                        continue
                    if o.dma:
                        if o.prev is not None:
                            wait_for(o.prev)
                        inst = o.fn(eng)
                        inst.then_inc(dsem[e][o.slot], 16)
                    else:
                        inst = o.fn(eng)
                        if o.signal:
                            inst.then_inc(csem[e][o.semidx], 1)

            @block.sync
            def _(eng):
                replay("sp", eng)

            @block.scalar
            def _(eng):
                replay("act", eng)

            @block.vector
            def _(eng):
                replay("dve", eng)

            @block.gpsimd
            def _(eng):
                replay("pool", eng)

            @block.tensor
            def _(eng):
                replay("pe", eng)
        return nc


import math
import ml_dtypes
from concourse.bass_utils import run_bass_kernel_spmd

D = 2048
KC = 16
LX = 2048
LCX = 256
T = LX + LCX
NT = T // 128
TB = [(0, 512), (512, 512), (1024, 512), (1536, 512), (2048, 256)]
PW = T + 4
IN_W = 11264
FF = 5632
FJ = FF // 128
DEPTH = 4
EPS = 1e-6
NCORE = 4
TWO_PI = 2.0 * math.pi


def pidx(t):
    return t + 1 if t < LX else t + 3


class B:
    def __init__(self, dump=()):
        self.P = Prog()
        self.nc = self.P.nc
        self.dump = set(dump)
        nc = self.nc
        self.ps = [nc.alloc_psum_tensor(f"psb{i}", [128, 512], F32) for i in range(8)]
        self.rot = 0
        self.ident = None

    def din(self, name, shape, dt=F32):
        return self.P.dram(name, shape, dt, kind="ExternalInput").ap()

    def dsc(self, name, shape, dt=F32):
        kind = "ExternalOutput" if name in self.dump else "Internal"
        return self.P.dram(name, shape, dt, kind=kind).ap()

    def gbank(self):
        b = self.ps[self.rot % 4]
        self.rot += 1
        return b

    def mm(self, out, lhsT, rhs, start, stop, r, w, skip=False):
        if skip:
            self.P.op("pe", lambda e: e.matmul(out, lhsT=lhsT, rhs=rhs, start=start, stop=stop,
                                               skip_group_check=True), r, w)
        else:
            self.P.op("pe", lambda e: e.matmul(out, lhsT=lhsT, rhs=rhs, start=start, stop=stop), r, w)

    def tr(self, out, in_, r, w, ident=None):
        idn = self.ident if ident is None else ident
        self.P.op("pe", lambda e: e.transpose(out=out, in_=in_, identity=idn), r, w)

    def act(self, out, in_, func, r, w, bias=0.0, scale=1.0, accum=None):
        if accum is None:
            self.P.op("act", lambda e: e.activation(out=out, in_=in_, func=func, bias=bias, scale=scale), r, w)
        else:
            self.P.op("act", lambda e: e.activation(out=out, in_=in_, func=func, bias=bias, scale=scale,
                                                    accum_out=accum), r, w)

    def tt(self, eng, out, in0, in1, op, r, w):
        self.P.op(eng, lambda e: e.tensor_tensor(out=out, in0=in0, in1=in1, op=op), r, w)

    def ts(self, eng, out, in0, s1, s2, op0, op1, r, w):
        if s2 is None:
            self.P.op(eng, lambda e: e.tensor_scalar(out=out, in0=in0, scalar1=s1, scalar2=None, op0=op0), r, w)
        else:
            self.P.op(eng, lambda e: e.tensor_scalar(out=out, in0=in0, scalar1=s1, scalar2=s2, op0=op0, op1=op1), r, w)

    def stt(self, out, in0, scalar, in1, op0, op1, r, w):
        self.P.op("dve", lambda e: e.scalar_tensor_tensor(out=out, in0=in0, scalar=scalar, in1=in1, op0=op0, op1=op1),
                  r, w)

    def cp(self, eng, out, in_, r, w):
        if eng == "act":
            self.P.op("act", lambda e: e.copy(out=out, in_=in_), r, w)
        else:
            self.P.op(eng, lambda e: e.tensor_copy(out=out, in_=in_), r, w)

    def ms(self, eng, ap, val, w):
        self.P.op(eng, lambda e: e.memset(ap, val), (), w)

    def recip(self, out, in_, r, w):
        self.P.op("dve", lambda e: e.reciprocal(out=out, in_=in_), r, w)

    def dma(self, q, out, in_, r, w):
        self.P.op(q, lambda e: e.dma_start(out=out, in_=in_), r, w, dma=True)


def blk(ap3, c0, c1, t0, n):
    return ap3[c0:c1, :, t0:t0 + n].rearrange("c p t -> p c t")


def wslab(w2d, k0, kc, c0, ncol):
    return w2d[k0 * 128:(k0 + kc) * 128, c0:c0 + ncol].rearrange("(kc p) n -> p kc n", p=128)


def build(nlayers=DEPTH, dump=(), stop_after=None):
    bb = B(dump)
    P = bb.P
    nc = bb.nc
    ps = bb.ps
    mm, tr, act, tt, ts, stt, cp, ms, recip, dma = bb.mm, bb.tr, bb.act, bb.tt, bb.ts, bb.stt, bb.cp, bb.ms, bb.recip, bb.dma
    din, dsc = bb.din, bb.dsc

    xT_in = din("xT", [KC, 128, T])
    ccT = din("ccT", [128, KC, 2])
    w_ada = din("w_ada", [DEPTH, D, 6 * D])
    b_ada = din("b_ada", [DEPTH, 6 * D])
    gT = din("gT", [DEPTH, 128, 4, KC])
    w_in = din("w_in", [DEPTH, D, IN_W])
    diff_lam = din("diff_lam", [DEPTH, 4 * 64])
    subln = din("attn_subln_g", [DEPTH, 128])
    hsw = din("hsw", [DEPTH, 128, 12, 3])
    hw1 = din("hy_ffn_w1", [DEPTH, 33, 64])
    hw2 = din("hy_ffn_w2", [DEPTH, 64, 64])
    hw3 = din("hy_ffn_w3", [DEPTH, 64, 2048])
    hpb = din("hpb", [DEPTH, 64, 4])
    hbias = din("hbias", [DEPTH, 128, 2, 4])
    pool_w = din("pool_w", [DEPTH, 4, 128, 128])
    pscale = din("pscale", [DEPTH, 128, 4])
    w_att_o = din("w_att_o", [DEPTH, 1024, D])
    w_hy_o = din("w_hy_o", [DEPTH, 512, D])
    w_pool_o = din("w_pool_o", [DEPTH, 512, D])
    w_out = din("w_out", [DEPTH, D, D])
    w_up = din("w_up", [DEPTH, D, 2 * FF])
    fcw = din("fcw", [DEPTH, 128, FJ, 3])
    w_down = din("w_down", [DEPTH, FF, D])
    c_ident = din("c_ident", [128, 128])
    c_cos = din("c_cos", [128, T])
    c_sin = din("c_sin", [128, T])
    c_corr = din("c_corr", [128, 4, 16])
    segs = []
    for nm, L in (("x", LX), ("c", LCX)):
        SC = L // 128
        TBI = 256
        segs.append(dict(
            nm=nm, L=L, SC=SC, t0=0 if nm == "x" else LX, TBI=TBI,
            FW=din(f"c_fw_{nm}", [2, SC, 128, SC, 128], BF16),
            GV=din(f"c_gv_{nm}", [L // TBI, 128, 2 * SC, TBI], BF16),
            zT=din(f"c_zT_{nm}", [33, L]),
            decay=din(f"c_decay_{nm}", [128, SC, 512]),
            ws=din(f"c_ws_{nm}", [128, 3, SC]),
            KTAB=dsc(f"KTAB_{nm}", [DEPTH, 2, SC, 128, 3, 512]),
        ))
    outT = P.dram("outT", [KC, 128, LX], F32, kind="ExternalOutput").ap()

    XT = dsc("XT", [KC, 128, T])
    MOD = dsc("MOD", [DEPTH, 2, 6 * D])
    QKT = dsc("QKT", [16, 128, T], BF16)
    VA = dsc("VA", [NT, 128, 8, 129], BF16)
    HYT = dsc("HYT", [12, 128, T])
    PLT = dsc("PLT", [4, 128, T])
    GT = dsc("GT", [48, 128, T], BF16)
    BRT = dsc("BRT", [16, 128, T], BF16)
    YT = dsc("YT", [KC, 128, T])
    AT = dsc("AT", [FJ, 128, T], BF16)

    identf = P.sb([128, 128], F32, "identf")
    ident = P.sb([128, 128], BF16, "ident")
    ones_b = P.sb([128, 128], BF16, "ones_b")
    ones_f = P.sb([128, 128], F32, "ones_f")
    modD = P.sb([128, DEPTH, 2, 6, KC], F32, "modD")
    bb.ident = ident[:]
    dma("sp", identf[:], c_ident, (), [identf])
    cp("dve", ident[:], identf[:], [identf], [ident])
    ms("dve", ones_b[:], 1.0, [ones_b])
    ms("dve", ones_f[:], 1.0, [ones_f])
    dma("sp", XT, xT_in, (), ["XT"])
    base_mark = P.sb_mark()

    def ada_phase():
        m0 = P.sb_mark()
        sc = P.sb([128, KC, 2], F32, "sc")
        NB_ = 5
        wa = [P.sb([128, KC, 512], F32, f"wa{i}") for i in range(NB_)]
        bt = [P.sb([2, 512], F32, f"bt{i}") for i in range(NB_)]
        mo = [P.sb([2, 512], F32, f"mo{i}") for i in range(NB_)]
        dma("sp", sc[:], ccT, (), [sc])
        act(sc[:], sc[:], AF.Silu, [sc], [sc])
        it = 0
        for l in range(nlayers):
            for ng in range(24):
                b = it % NB_
                it += 1
                dma("sp" if it % 2 == 0 else "act", wa[b][:], wslab(w_ada[l], 0, KC, ng * 512, 512), (), [wa[b]])
                dma("act", bt[b][:], b_ada[l:l + 1, ng * 512:(ng + 1) * 512].broadcast_to([2, 512]), (), [bt[b]])
                pb = ps[4 + b % 2]
                for kc in range(KC):
                    mm(pb[0:2, :], sc[:, kc, :], wa[b][:, kc, :], kc == 0, kc == KC - 1, [sc, wa[b]], [pb])
                tt("dve", mo[b][:], pb[0:2, :], bt[b][:], ALU.add, [pb, bt[b]], [mo[b]])
                dma("sp", MOD[l, :, ng * 512:(ng + 1) * 512], mo[b][:], [mo[b]], [("MOD", l)])
        P.barrier()
        P.sb_reset(m0)
        mr = [P.sb([96, 128], F32, f"mr{i}") for i in range(2)]
        mt = P.sb([128, 2, 6, KC], F32, "mt")
        g_sb = P.sb([128, 4, KC], F32, "g_sb")
        tmp = P.sb([128, KC], F32, "tmpm")
        it = 0
        for l in range(nlayers):
            dma("sp", g_sb[:], gT[l], (), [g_sb])
            for s in range(2):
                b = it % 2
                it += 1
                dma("sp", mr[b][:], MOD[l, s].rearrange("(r p) -> r p", p=128), (), [mr[b]])
                pb = ps[4 + b]
                tr(pb[:, 0:96], mr[b][:], [mr[b], identf], [pb], ident=identf[0:96, 0:96])
                cp("dve", mt[:, s].rearrange("p m j -> p (m j)"), pb[:, 0:96], [pb], [(mt, s)])
                ts("dve", tmp[:], mt[:, s, 1, :], 1.0, None, ALU.add, None, [(mt, s)], [tmp])
                tt("dve", modD[:, l, s, 0, :], tmp[:], g_sb[:, 0, :], ALU.mult, [tmp, g_sb], [modD])
                cp("dve", modD[:, l, s, 1, :], mt[:, s, 0, :], [(mt, s)], [modD])
                tt("dve", modD[:, l, s, 2, :], mt[:, s, 2, :], g_sb[:, 1, :], ALU.mult, [(mt, s), g_sb], [modD])
                ts("dve", tmp[:], mt[:, s, 4, :], 1.0, None, ALU.add, None, [(mt, s)], [tmp])
                tt("dve", modD[:, l, s, 3, :], tmp[:], g_sb[:, 2, :], ALU.mult, [tmp, g_sb], [modD])
                cp("dve", modD[:, l, s, 4, :], mt[:, s, 3, :], [(mt, s)], [modD])
                tt("dve", modD[:, l, s, 5, :], mt[:, s, 5, :], g_sb[:, 3, :], ALU.mult, [(mt, s), g_sb], [modD])
        P.barrier()
        P.sb_reset(m0)

    def stats_rstd(src, n, sq, rstd, pstat, rkeys, nchunks=KC, dim=D):
        act(sq[:, :, :n], src, AF.Square, rkeys, [sq])
        for j in range(nchunks):
            mm(pstat[:, :n], ones_b[:], sq[:, j, :n], j == 0, j == nchunks - 1, [sq, ones_b], [pstat])
        act(rstd[:, :n], pstat[:, :n], AF.Ln, [pstat], [rstd], bias=EPS, scale=1.0 / dim)
        act(rstd[:, :n], rstd[:, :n], AF.Exp, [rstd], [rstd], scale=-0.5)

    def seg_of(t0):
        return 0 if t0 < LX else 1

    def norm_pass(l, ia, ib, hxT):
        m0 = P.sb_mark()
        xb = [P.sb([128, KC, 512], F32, f"xb{i}") for i in range(2)]
        sq = P.sb([128, KC, 512], BF16, "sq")
        t1 = P.sb([128, KC, 512], F32, "t1")
        rstd = [P.sb([128, 512], F32, f"rstd{i}") for i in range(2)]
        for bi, (t0, n) in enumerate(TB):
            b = bi % 2
            s = seg_of(t0)
            dma("sp", xb[b][:, :, :n], blk(XT, 0, KC, t0, n), ["XT"], [xb[b]])
            stats_rstd(xb[b][:, :, :n], n, sq, rstd[b], ps[4 + b], [xb[b]])
            tt("dve", t1[:, :, :n], xb[b][:, :, :n], rstd[b][:, :n].unsqueeze(1).broadcast_to([128, KC, n]), ALU.mult,
               [xb[b], rstd[b]], [t1])
            tt("pool", t1[:, :, :n], t1[:, :, :n], modD[:, l, s, ia, :].unsqueeze(2).broadcast_to([128, KC, n]), ALU.mult,
               [t1, modD], [t1])
            tt("dve", hxT[:, :, t0:t0 + n], t1[:, :, :n], modD[:, l, s, ib, :].unsqueeze(2).broadcast_to([128, KC, n]),
               ALU.add, [t1, modD], [("hxT", bi)])
        P.barrier()
        P.sb_reset(m0)

    def resid_pass(l, ig):
        m0 = P.sb_mark()
        xb = [P.sb([128, KC, 512], F32, f"rxb{i}") for i in range(2)]
        yb = [P.sb([128, KC, 512], F32, f"ryb{i}") for i in range(2)]
        sq = P.sb([128, KC, 512], BF16, "rsq")
        rstd = [P.sb([128, 512], F32, f"rrstd{i}") for i in range(2)]
        for bi, (t0, n) in enumerate(TB):
            b = bi % 2
            s = seg_of(t0)
            dma("sp", xb[b][:, :, :n], blk(XT, 0, KC, t0, n), ["XT"], [xb[b]])
            dma("act", yb[b][:, :, :n], blk(YT, 0, KC, t0, n), ["YT"], [yb[b]])
            stats_rstd(yb[b][:, :, :n], n, sq, rstd[b], ps[4 + b], [yb[b]])
            tt("dve", yb[b][:, :, :n], yb[b][:, :, :n], rstd[b][:, :n].unsqueeze(1).broadcast_to([128, KC, n]), ALU.mult,
               [yb[b], rstd[b]], [yb[b]])
            tt("pool", yb[b][:, :, :n], yb[b][:, :, :n], modD[:, l, s, ig, :].unsqueeze(2).broadcast_to([128, KC, n]),
               ALU.mult, [yb[b], modD], [yb[b]])
            tt("dve", xb[b][:, :, :n], xb[b][:, :, :n], yb[b][:, :, :n], ALU.add, [xb[b], yb[b]], [xb[b]])
            dma("sp", blk(XT, 0, KC, t0, n), xb[b][:, :, :n], [xb[b]], ["XT"])
        P.barrier()
        P.sb_reset(m0)

    def resid_norm_pass(l, ig, l2, ia, ib, hxT):
        m0 = P.sb_mark()
        NB2 = 256
        xb = [P.sb([128, KC, NB2], F32, f"fxb{i}") for i in range(2)]
        yb = [P.sb([128, KC, NB2], F32, f"fyb{i}") for i in range(2)]
        sq = [P.sb([128, KC, NB2], BF16, f"fsq{i}") for i in range(2)]
        rstd = [P.sb([128, NB2], F32, f"frstd{i}") for i in range(4)]
        n = NB2
        for k in range(T // NB2):
            t0 = k * NB2
            bi = min(t0 // 512, 4)
            b = k % 2
            s_ = seg_of(t0)
            dma("sp", xb[b][:], blk(XT, 0, KC, t0, n), ["XT"], [xb[b]])
            dma("act", yb[b][:], blk(YT, 0, KC, t0, n), ["YT"], [yb[b]])
            stats_rstd(yb[b][:], n, sq[b], rstd[b], ps[4 + b], [yb[b]])
            tt("dve", yb[b][:], yb[b][:], rstd[b][:, :n].unsqueeze(1).broadcast_to([128, KC, n]), ALU.mult,
               [yb[b], rstd[b]], [yb[b]])
            tt("pool", yb[b][:], yb[b][:], modD[:, l, s_, ig, :].unsqueeze(2).broadcast_to([128, KC, n]),
               ALU.mult, [yb[b], modD], [yb[b]])
            tt("dve", xb[b][:], xb[b][:], yb[b][:], ALU.add, [xb[b], yb[b]], [xb[b]])
            dma("sp", blk(XT, 0, KC, t0, n), xb[b][:], [xb[b]], ["XT"])
            r2 = rstd[2 + b]
            stats_rstd(xb[b][:], n, sq[b], r2, ps[6 + b], [xb[b]])
            tt("dve", yb[b][:], xb[b][:], r2[:, :n].unsqueeze(1).broadcast_to([128, KC, n]), ALU.mult,
               [xb[b], r2], [yb[b]])
            tt("pool", yb[b][:], yb[b][:], modD[:, l2, s_, ia, :].unsqueeze(2).broadcast_to([128, KC, n]),
               ALU.mult, [yb[b], modD], [yb[b]])
            tt("dve", hxT[:, :, t0:t0 + n], yb[b][:], modD[:, l2, s_, ib, :].unsqueeze(2).broadcast_to([128, KC, n]),
               ALU.add, [yb[b], modD], [("hxT", bi)])
        P.barrier()
        P.sb_reset(m0)

    def gemm_fm(loaders, kcs, actT, akey, blocks, evac, wt, perm_wt=None, ncs=4, akc=None):
        for gi, load in enumerate(loaders):
            slab = wt[gi % 2]
            load(slab)
            pslab = None
            if perm_wt is not None and perm_wt[0](gi):
                pslab = perm_wt[1][gi % 2]
                v = slab[:].rearrange("p k (g b i) -> p k g b i", b=2, i=16)
                vp = pslab[:].rearrange("p k (g b i) -> p k g b i", b=2, i=16)
                for kh in range(2):
                    ksl = slice(kh * (kcs // 2), (kh + 1) * (kcs // 2))
                    cp("pool", vp[:, ksl, :, 0, :], v[:, ksl, :, 1, :], [slab], [(pslab, kh, 0)])
                    cp("pool", vp[:, ksl, :, 1, :], v[:, ksl, :, 0, :], [slab], [(pslab, kh, 1)])
            for bi, (t0, n) in enumerate(blocks):
                for c in range(ncs):
                    pa = bb.gbank()
                    for kc in range(kcs):
                        mm(pa[:, :n], slab[:, kc, c * 128:(c + 1) * 128], actT[:, kc, t0:t0 + n], kc == 0, kc == kcs - 1,
                           [slab, akc(kc) if akc is not None else akey(bi)], [pa])
                    pb = None
                    if pslab is not None:
                        pb = bb.gbank()
                        rk = [(pslab, kh, q) for kh in range(2) for q in range(2)]
                        for kc in range(kcs):
                            mm(pb[:, :n], pslab[:, kc, c * 128:(c + 1) * 128], actT[:, kc, t0:t0 + n], kc == 0,
                               kc == kcs - 1, rk + [akey(bi)], [pb])
                    evac(gi, c, bi, t0, n, pa, pb)

    def proj_phase(l, hxT):
        m0 = P.sb_mark()
        wt = [P.sb([128, KC, 512], BF16, f"wt{i}") for i in range(2)]
        wp = [P.sb([128, KC, 512], BF16, f"wp{i}") for i in range(2)]
        cosT = P.sb([128, T], F32, "cosT")
        sinT = P.sb([128, T], F32, "sinT")
        dma("sp", cosT[:], c_cos, (), [cosT])
        dma("sp", sinT[:], c_sin, (), [sinT])
        ev = [P.sb([128, 512], F32, f"ev{i}") for i in range(4)]
        evb = [P.sb([128, 512], BF16, f"evb{i}") for i in range(4)]
        st = {"i": 0}
        W = w_in[l]

        def ld(c0):
            def f(slab):
                dma("pool", slab[:], wslab(W, 0, KC, c0, 512), (), [slab])
            return f

        def evac(gi_abs):
            def f(gi, c, bi, t0, n, pa, pb):
                i = st["i"] % 4
                st["i"] += 1
                ch = gi_abs * 4 + c
                if gi_abs < 4:
                    tt("dve", ev[i][:, :n], pa[:, :n], cosT[:, t0:t0 + n], ALU.mult, [pa, cosT], [ev[i]])
                    j = (i + 1) % 4
                    st["i"] += 1
                    tt("dve", ev[j][:, :n], pb[:, :n], sinT[:, t0:t0 + n], ALU.mult, [pb, sinT], [ev[j]])
                    tt("pool", evb[i][:, :n], ev[i][:, :n], ev[j][:, :n], ALU.add, [ev[i], ev[j]], [evb[i]])
                    dma("sp", QKT[ch, :, t0:t0 + n], evb[i][:, :n], [evb[i]], [("QKT", ch)])
                elif gi_abs < 9:
                    cp("act", ev[i][:, :n], pa[:, :n], [pa], [ev[i]])
                    dma("sp", HYT[ch - 24, :, t0:t0 + n], ev[i][:, :n], [ev[i]], [("HYT", ch - 24)])
                elif gi_abs < 10:
                    cp("act", ev[i][:, :n], pa[:, :n], [pa], [ev[i]])
                    dma("sp", PLT[ch - 36, :, t0:t0 + n], ev[i][:, :n], [ev[i]], [("PLT", ch - 36)])
                else:
                    act(evb[i][:, :n], pa[:, :n], AF.Sigmoid, [pa], [evb[i]])
                    dma("sp", GT[ch - 40, :, t0:t0 + n], evb[i][:, :n], [evb[i]], [("GT", ch - 40)])
            return f

        akey = lambda bi: ("hxT", bi)
        for gi_abs in list(range(0, 4)) + list(range(6, 22)):
            gemm_fm([ld(gi_abs * 512)], KC, hxT, akey, TB, evac(gi_abs), wt,
                    perm_wt=((lambda g: True), wp) if gi_abs < 4 else None)
            wt.reverse()
            wp.reverse()
        va = [P.sb([128, 8, 129], BF16, f"va{i}") for i in range(2)]
        for i in range(2):
            ms("dve", va[i][:, :, 128:129], 1.0, [va[i]])
        wv = [wt[0], wt[1]]
        for g in range(2):
            dma("pool", wv[g][:], wslab(W, 0, KC, 2048 + g * 512, 512), (), [wv[g]])
        for it in range(NT):
            b = it % 2
            bi = min(it // 4, 4)
            for g in range(2):
                pa = bb.gbank()
                for kc in range(KC):
                    mm(pa[:], hxT[:, kc, it * 128:(it + 1) * 128], wv[g][:, kc, :], kc == 0, kc == KC - 1,
                       [wv[g], ("hxT", bi)], [pa])
                cp("act" if g == 0 else "dve", va[b][:, g * 4:(g + 1) * 4, 0:128],
                   pa[:].rearrange("p (h e) -> p h e", h=4), [pa], [va[b]])
            dma("sp", VA[it], va[b][:], [va[b]], ["VA"])
        P.barrier()
        P.sb_reset(m0)

    def attn_phase(l):
        m0 = P.sb_mark()
        lam_init = 0.8 - 0.6 * math.exp(-0.3 * l)
        lp = P.sb([128, 4, 64], F32, "lp")
        pr = P.sb([128, 2, 64], F32, "pr")
        s12 = P.sb([128, 2], F32, "s12")
        nlam = P.sb([128, 1], F32, "nlam")
        gcol = P.sb([128, 1], F32, "gcol")
        dma("sp", lp[:].rearrange("p a d -> p (a d)"), diff_lam[l:l + 1, :].broadcast_to([128, 256]), (), [lp])
        dma("sp", gcol[:], subln[l:l + 1, :].rearrange("o e -> e o"), (), [gcol])
        ts("dve", gcol[:], gcol[:], float(1.0 - lam_init), None, ALU.mult, None, [gcol], [gcol])
        tt("dve", pr[:, 0, :], lp[:, 0, :], lp[:, 1, :], ALU.mult, [lp], [pr])
        tt("dve", pr[:, 1, :], lp[:, 2, :], lp[:, 3, :], ALU.mult, [lp], [pr])
        P.op("dve", lambda e: e.reduce_sum(out=s12[:], in_=pr[:], axis=AX.X), [pr], [s12])
        act(s12[:], s12[:], AF.Exp, [s12], [s12])
        tt("dve", nlam[:], s12[:, 1:2], s12[:, 0:1], ALU.subtract, [s12], [nlam])
        ts("dve", nlam[:], nlam[:], -lam_init, None, ALU.add, None, [nlam], [nlam])
        qT = [P.sb([128, T], BF16, f"qT{i}") for i in range(2)]
        kT = [P.sb([128, T], BF16, f"kT{i}") for i in range(2)]
        vh = [P.sb([128, NT, 128], BF16, f"vh{i}") for i in range(2)]
        qz = [[P.sb([128, T], BF16, f"qz{i}_{m}") for m in range(2)] for i in range(2)]
        for i in range(2):
            ms("pool", qz[i][0][64:128, :], 0.0, [qz[i][0]])
            ms("pool", qz[i][1][0:64, :], 0.0, [qz[i][1]])
        E = [P.sb([128, 512], BF16, f"E{i}") for i in range(4)]
        rd = [P.sb([128, 512], F32, f"rd{i}") for i in range(2)]
        SBK = [ps[0], ps[1]]
        NJUNK = 0
        om = [P.sb([128, 512], F32, f"om{i}") for i in range(2)]
        attf_ = [P.sb([128, 512], F32, f"attf{i}") for i in range(2)]
        sqb_ = [P.sb([128, 512], BF16, f"sqb{i}") for i in range(2)]
        rs__ = [P.sb([128, 512], F32, f"ars{i}") for i in range(2)]
        pending = []
        atT = [P.sb([128, 512], BF16, f"atT{i}") for i in range(2)]
        state = {"ob": 0}
        oT = [ps[2], ps[3]]
        den = [ps[4], ps[5]]
        pstat = ps[6]

        def emit_load(h):
            hb = h % 2
            dma("sp", qT[hb][:], QKT[h], [("QKT", h)], [qT[hb]])
            dma("sp", kT[hb][:], QKT[8 + h], [("QKT", 8 + h)], [kT[hb]])
            dma("act", vh[hb][:], VA[:, :, h, 0:128].rearrange("i p e -> p i e"), ["VA"], [vh[hb]])
            cp("pool", qz[hb][0][0:64, :], qT[hb][0:64, :], [qT[hb]], [qz[hb][0]])
            cp("pool", qz[hb][1][64:128, :], qT[hb][64:128, :], [qT[hb]], [qz[hb][1]])

        steps = []
        for h in range(8):
            for bi, (q0, n) in enumerate(TB):
                kcs = list(range(NT)) if q0 < LX else [16, 17]
                for m in range(2):
                    for kc in kcs:
                        steps.append(dict(h=h, bi=bi, q0=q0, n=n, m=m, kc=kc, first=kc == kcs[0], last=kc == kcs[-1]))

        def emit_S(i):
            st_ = steps[i]
            hb, m, kc, q0, n = st_["h"] % 2, st_["m"], st_["kc"], st_["q0"], st_["n"]
            sb_ = SBK[i % 2]
            mm(sb_[:, :n], kT[hb][:, kc * 128:(kc + 1) * 128],
               qz[hb][m][:, q0:q0 + n], True, True, [kT[hb], qz[hb][m]], [sb_])

        def emit_PV(i):
            st_ = steps[i]
            h, m, kc, q0, n = st_["h"], st_["m"], st_["kc"], st_["q0"], st_["n"]
            hb = h % 2
            sb_ = SBK[i % 2]
            Et = E[i % 4]
            act(Et[:, :n], sb_[:, :n], AF.Exp, [sb_], [Et], scale=0.125)
            mm(oT[m][:, :n], vh[hb][:, kc, :], Et[:, :n], st_["first"], st_["last"], [Et, vh[hb]], [oT[m]])
            mm(den[m][:, :n], ones_b[:], Et[:, :n], st_["first"], st_["last"], [Et, ones_b], [den[m]])
            for _ in range(NJUNK):
                mm(ps[7][:, :n], ones_b[:], Et[:, :n], True, True, [], [])
            if not st_["last"]:
                return
            recip(rd[m][:, :n], den[m][:, :n], [den[m]], [rd[m]])
            tt("dve", om[m][:, :n], oT[m][:, :n], rd[m][:, :n], ALU.mult, [oT[m], rd[m]], [om[m]])
            if m == 0:
                return
            ob = state["ob"]
            state["ob"] += 1
            attf, sqb, rs_, o_t = attf_[ob % 2], sqb_[ob % 2], rs__[ob % 2], atT[ob % 2]
            stt(attf[:, :n], om[1][:, :n], nlam[:, 0:1], om[0][:, :n], ALU.mult, ALU.add, [om[0], om[1], nlam], [attf])
            tt("pool", sqb[:, :n], attf[:, :n], attf[:, :n], ALU.mult, [attf], [sqb])

            def stC():
                mm(pstat[:, :n], ones_b[:], sqb[:, :n], True, True, [sqb, ones_b], [pstat])

            def stD():
                act(rs_[:, :n], pstat[:, :n], AF.Ln, [pstat], [rs_], bias=EPS, scale=1.0 / 128)
                act(rs_[:, :n], rs_[:, :n], AF.Exp, [rs_], [rs_], scale=-0.5)

            def stE():
                tt("dve", attf[:, :n], attf[:, :n], rs_[:, :n], ALU.mult, [attf, rs_], [attf])
                ts("dve", o_t[:, :n], attf[:, :n], gcol[:, 0:1], None, ALU.mult, None, [attf, gcol], [o_t])
                dma("sp", BRT[h, :, q0:q0 + n], o_t[:, :n], [o_t], [("BRT", h)])
            pending.append((i + 5, stC))
            pending.append((i + 7, stD))
            pending.append((i + 9, stE))

        emit_load(0)
        emit_S(0)
        loaded = {0}
        for i in range(len(steps)):
            hn = steps[i]["h"] + 1
            if hn < 8 and hn not in loaded and steps[i]["bi"] >= 1:
                emit_load(hn)
                loaded.add(hn)
            if i + 1 < len(steps):
                emit_S(i + 1)
            emit_PV(i)
            while pending and pending[0][0] <= i:
                pending.pop(0)[1]()
        while pending:
            pending.pop(0)[1]()
        P.barrier()
        P.sb_reset(m0)

    def dwconv3(eng, out, src, w3, r, w):
        n = PW - 2
        ts(eng, out[:, 1:1 + n], src[:, 0:n], w3[:, 0:1], None, ALU.mult, None, r, w)
        if eng == "dve":
            stt(out[:, 1:1 + n], src[:, 1:1 + n], w3[:, 1:2], out[:, 1:1 + n], ALU.mult, ALU.add, r + w, w)
            stt(out[:, 1:1 + n], src[:, 2:2 + n], w3[:, 2:3], out[:, 1:1 + n], ALU.mult, ALU.add, r + w, w)
        else:
            raise NotImplementedError

    def load_padded(q, dst, src3, ch, w, rkeys):
        dma(q, dst[:, 1:1 + LX], src3[ch, :, 0:LX], rkeys, w)
        dma(q, dst[:, LX + 3:LX + 3 + LCX], src3[ch, :, LX:T], rkeys, w)

    def zero_pads(eng, t, w):
        ms(eng, t[:, 0:1], 0.0, w)
        ms(eng, t[:, LX + 1:LX + 3], 0.0, w)
        ms(eng, t[:, PW - 1:PW], 0.0, w)

    def filt_phase(l, sg):
        m0 = P.sb_mark()
        L, SC = sg["L"], sg["SC"]
        NB = 512 if L >= 512 else L
        w1 = P.sb([33, 64], F32, "w1")
        w2 = P.sb([64, 64], F32, "w2")
        w3 = P.sb([64, 2048], F32, "w3")
        pbf = P.sb([64, 4], F32, "pbf")
        zT = P.sb([33, L], F32, "zT")
        h1 = P.sb([64, L], F32, "h1")
        h2 = P.sb([64, L], F32, "h2")
        tq = P.sb([64, 512], F32, "tq")
        rq = P.sb([64, 512], F32, "rq")
        MAGIC = 12582912.0
        wsc = P.sb([128, 3, SC], F32, "wsc")
        dma("sp", w1[:], hw1[l], (), [w1])
        dma("sp", w2[:], hw2[l], (), [w2])
        dma("sp", w3[:], hw3[l], (), [w3])
        dma("sp", pbf[:], hpb[l], (), [pbf])
        dma("sp", zT[:], sg["zT"], (), [zT])
        dma("sp", wsc[:], sg["ws"], (), [wsc])
        OFF = math.pi + 16 * TWO_PI
        for (wm, kk, src, dst, ib, ifr) in ((w1, 33, zT, h1, 0, 1), (w2, 64, h1, h2, 2, 3)):
            for b0 in range(0, L, NB):
                pa = bb.gbank()
                mm(pa[0:64, :NB], wm[0:kk, :], src[0:kk, b0:b0 + NB], True, True, [wm, src], [pa])
                ts("dve", tq[:, :NB], pa[0:64, :NB], pbf[:, ib:ib + 1], pbf[:, ifr:ifr + 1], ALU.add, ALU.mult,
                   [pa, pbf], [tq])
                ts("dve", rq[:, :NB], tq[:, :NB], 1.0 / TWO_PI, MAGIC, ALU.mult, ALU.add, [tq], [rq])
                ts("dve", rq[:, :NB], rq[:, :NB], -MAGIC, -TWO_PI, ALU.add, ALU.mult, [rq], [rq])
                tt("dve", tq[:, :NB], tq[:, :NB], rq[:, :NB], ALU.add, [tq, rq], [tq])
                act(dst[:, b0:b0 + NB], tq[:, :NB], AF.Sin, [tq], [dst])
        dec = [P.sb([128, 512], F32, f"dec{i}") for i in range(2)]
        kr = [P.sb([128, 512], F32, f"kr{i}") for i in range(4)]
        ab = [P.sb([128, 512], F32, f"ab{i}") for i in range(4)]
        ksum = P.sb([128, SC, 1024], BF16, "ksum")
        kdif = P.sb([128, SC, 1024], BF16, "kdif")
        rn = [P.sb([128, 512], F32, f"rn{i}") for i in range(2)]
        psN = [ps[4], ps[5]]
        for lc in range(SC):
            d_ = dec[lc % 2]
            dma("sp", d_[:], sg["decay"][:, lc, :], (), [d_])
            for cb in range(4):
                pa = bb.gbank()
                mm(pa[:], h2[:, lc * 128:(lc + 1) * 128], w3[:, cb * 512:(cb + 1) * 512], True, True, [h2, w3], [pa])
                tt("dve", kr[cb][:], pa[:], d_[:], ALU.mult, [pa, d_], [kr[cb]])
                if lc == 0 and cb >= 2:
                    ms("dve", kr[cb][0:1, :], 0.0, [kr[cb]])
                act(ab[cb][:], kr[cb][:], AF.Abs, [kr[cb]], [ab[cb]])
            for o in range(2):
                for dr in range(2):
                    mm(psN[o][:], ones_f[:], ab[dr * 2 + o][:], lc == 0 and dr == 0, lc == SC - 1 and dr == 1,
                       [ab[dr * 2 + o], ones_f], [psN[o]])
                tt("pool", ksum[:, lc, o * 512:(o + 1) * 512], kr[o][:], kr[2 + o][:], ALU.add, [kr[o], kr[2 + o]],
                   [(ksum, lc, o)])
                tt("pool", kdif[:, lc, o * 512:(o + 1) * 512], kr[o][:], kr[2 + o][:], ALU.subtract,
                   [kr[o], kr[2 + o]], [(kdif, lc, o)])
        for o in range(2):
            ts("dve", rn[o][:], psN[o][:], EPS, None, ALU.add, None, [psN[o]], [rn[o]])
            recip(rn[o][:], rn[o][:], [rn[o]], [rn[o]])
        fcs = [P.sb([128, SC, 128], BF16, f"fcs{i}") for i in range(2)]
        fss = [P.sb([128, SC, 128], BF16, f"fss{i}") for i in range(2)]
        tab = [P.sb([128, 3, 512], F32, f"tab{i}") for i in range(2)]
        N = 2 * L
        it = 0
        for fc in range(SC):
            fb = fc % 2
            dma("sp", fcs[fb][:], sg["FW"][0, fc], (), [fcs[fb]])
            dma("act", fss[fb][:], sg["FW"][1, fc], (), [fss[fb]])
            for o in range(2):
                tb_ = tab[it % 2]
                it += 1
                pA = bb.gbank()
                pB = bb.gbank()
                ksk = [(ksum, lc, o) for lc in range(SC)]
                kdk = [(kdif, lc, o) for lc in range(SC)]
                for lc in range(SC):
                    mm(pA[:], fcs[fb][:, lc, :], ksum[:, lc, o * 512:(o + 1) * 512], lc == 0, lc == SC - 1,
                       [fcs[fb]] + ksk, [pA])
                for lc in range(SC):
                    mm(pB[:], fss[fb][:, lc, :], kdif[:, lc, o * 512:(o + 1) * 512], lc == 0, lc == SC - 1,
                       [fss[fb]] + kdk, [pB])
                stt(tb_[:, 0, :], pA[:], wsc[:, 0, fc:fc + 1], rn[o][:], ALU.mult, ALU.mult, [pA, wsc, rn[o]], [tb_])
                stt(tb_[:, 1, :], pB[:], wsc[:, 1, fc:fc + 1], rn[o][:], ALU.mult, ALU.mult, [pB, wsc, rn[o]], [tb_])
                stt(tb_[:, 2, :], pA[:], wsc[:, 2, fc:fc + 1], rn[o][:], ALU.mult, ALU.mult, [pA, wsc, rn[o]], [tb_])
                if fc == 0:
                    pC = bb.gbank()
                    for lc in range(SC):
                        mm(pC[0:1, :], fss[fb][:, lc, 0:1], ksum[:, lc, o * 512:(o + 1) * 512], lc == 0, lc == SC - 1,
                           [fss[fb]] + ksk, [pC])
                    stt(tb_[0:1, 2, :], pC[0:1, :], 1.0 / N, rn[o][0:1, :], ALU.mult, ALU.mult, [pC, rn[o], tb_], [tb_])
                dma("sp", sg["KTAB"][l, o, fc], tb_[:], [tb_], [("KTAB", sg["nm"])])
        P.barrier()
        P.sb_reset(m0)

    def hyena_phase(l):
        m0 = P.sb_mark()
        sw = P.sb([128, 12, 3], F32, "sw")
        hb = P.sb([128, 2, 4], F32, "hb")
        dma("sp", sw[:], hsw[l], (), [sw])
        dma("sp", hb[:], hbias[l], (), [hb])
        uT = P.sb([128, 4, PW], F32, "uT")
        mT = P.sb([128, 4, PW], F32, "mT")
        utok = P.sb([128, NT, 512], BF16, "utok")
        mk_a = P.sb_mark()
        raw = [P.sb([128, PW], F32, f"raw{i}") for i in range(2)]
        ubf = P.sb([128, 4, T], BF16, "ubf")
        P.sb_reset(mk_a)
        Y = P.sb([128, 32, 512], BF16, "Y")
        fcs = [P.sb([128, 16, 128], BF16, f"hfc{i}") for i in range(2)]
        fss = [P.sb([128, 16, 128], BF16, f"hfs{i}") for i in range(2)]
        tab = [P.sb([128, 3, 512], F32, f"htab{i}") for i in range(2)]
        gv = [P.sb([128, 32, 256], BF16, f"gv{i}") for i in range(2)]
        tm = [P.sb([128, 512], F32, f"htm{i}") for i in range(4)]
        ob = [P.sb([128, 256], BF16, f"hob{i}") for i in range(2)]
        ri = 0

        def conv_chunks(c0, dst):
            nonlocal ri
            for cc in range(4):
                r_ = raw[ri % 2]
                ri += 1
                load_padded("sp", r_, HYT, c0 + cc, [r_], [("HYT", c0 + cc)])
                dwconv3("dve", dst[:, cc, :], r_, sw[:, c0 + cc, :], [r_, sw], [(dst, cc)])

        ti = 0
        gi = 0
        for o in range(2):
            P.barrier()
            for i in range(2):
                zero_pads("dve", raw[i], [raw[i]])
            if o == 0:
                conv_chunks(0, uT)
            conv_chunks(4 + 4 * o, mT)
            for cc in range(4):
                cp("pool", ubf[:, cc, 0:LX], uT[:, cc, 1:1 + LX], [(uT, cc)], [(ubf, cc)])
                cp("pool", ubf[:, cc, LX:T], uT[:, cc, LX + 3:LX + 3 + LCX], [(uT, cc)], [(ubf, cc)])
            tpi = 0
            for cc in range(4):
                for s0 in range(0, NT, 8):
                    ns = min(8, NT - s0)
                    pt = ps[6 + tpi % 2]
                    tpi += 1
                    ptb = pt[:].bitcast(BF16)
                    for k in range(ns):
                        tr(ptb[:, k * 128:(k + 1) * 128], ubf[:, cc, (s0 + k) * 128:(s0 + k + 1) * 128],
                           [(ubf, cc), ident], [pt])
                    cp("act", utok[:, s0:s0 + ns, cc * 128:(cc + 1) * 128],
                       ptb[:, :ns * 128].rearrange("p (k c) -> p k c", c=128), [pt], [(utok, cc, s0)])
            ukeys = [(utok, cc, s0) for cc in range(4) for s0 in range(0, NT, 8)]
            P.barrier()
            for sg in segs:
                SC, L, tk0 = sg["SC"], sg["L"], sg["t0"] // 128
                for fc in range(SC):
                    fb = gi % 2
                    gi += 1
                    dma("sp", fcs[fb][:, :SC, :], sg["FW"][0, fc], (), [fcs[fb]])
                    dma("act", fss[fb][:, :SC, :], sg["FW"][1, fc], (), [fss[fb]])
                    dma("sp", tab[fb][:], sg["KTAB"][l, o, fc], [("KTAB", sg["nm"])], [tab[fb]])
                    pc = ps[4] if fc % 2 == 0 else ps[2]
                    pS = ps[5] if fc % 2 == 0 else ps[3]
                    for sc in range(SC):
                        mm(pc[:], fcs[fb][:, sc, :], utok[:, tk0 + sc, :], sc == 0, sc == SC - 1, [fcs[fb]] + ukeys, [pc])
                    for sc in range(SC):
                        mm(pS[:], fss[fb][:, sc, :], utok[:, tk0 + sc, :], sc == 0, sc == SC - 1, [fss[fb]] + ukeys, [pS])
                    a_, b_, c_, d_ = tm[0], tm[1], tm[2], tm[3]
                    tt("dve", a_[:], pc[:], tab[fb][:, 0, :], ALU.mult, [pc, tab[fb]], [a_])
                    tt("dve", b_[:], pS[:], tab[fb][:, 1, :], ALU.mult, [pS, tab[fb]], [b_])
                    tt("pool", Y[:, fc, :], a_[:], b_[:], ALU.subtract, [a_, b_], [(Y, fc)])
                    tt("dve", c_[:], pc[:], tab[fb][:, 1, :], ALU.mult, [pc, tab[fb]], [c_])
                    tt("dve", d_[:], pS[:], tab[fb][:, 2, :], ALU.mult, [pS, tab[fb]], [d_])
                    tt("pool", Y[:, SC + fc, :], c_[:], d_[:], ALU.add, [c_, d_], [(Y, SC + fc)])
                ykeys = [(Y, k) for k in range(2 * SC)]
                TBI = sg["TBI"]
                for tbk in range(L // TBI):
                    g_ = gv[ti % 2]
                    ti += 1
                    dma("sp", g_[:, :2 * SC, :], sg["GV"][tbk], (), [g_])
                    tt0 = sg["t0"] + tbk * TBI
                    p0 = pidx(tt0)
                    for cc in range(4):
                        pa = bb.gbank()
                        for k in range(2 * SC):
                            mm(pa[:, :TBI], Y[:, k, cc * 128:(cc + 1) * 128], g_[:, k, :], k == 0, k == 2 * SC - 1,
                               ykeys + [g_], [pa])
                        t_ = tm[(cc) % 4]
                        stt(t_[:, :TBI], uT[:, cc, p0:p0 + TBI], hb[:, o, cc:cc + 1], pa[:, :TBI], ALU.mult, ALU.add,
                            [(uT, cc), hb, pa], [t_])
                        if o == 0:
                            tt("pool", uT[:, cc, p0:p0 + TBI], t_[:, :TBI], mT[:, cc, p0:p0 + TBI], ALU.mult,
                               [t_, (mT, cc)], [(uT, cc)])
                        else:
                            o_ = ob[cc % 2]
                            tt("pool", o_[:, :TBI], t_[:, :TBI], mT[:, cc, p0:p0 + TBI], ALU.mult, [t_, (mT, cc)], [o_])
                            dma("sp", BRT[8 + cc, :, tt0:tt0 + TBI], o_[:, :TBI], [o_], [("BRT", 8 + cc)])
        P.barrier()
        P.sb_reset(m0)

    def pool_phase(l):
        m0 = P.sb_mark()
        PP = T + 32
        xo, co = 8, LX + 24
        pw = P.sb([128, 4, 128], BF16, "pw")
        psc = P.sb([128, 4], F32, "psc")
        corr = P.sb([128, 4, 16], F32, "corr")
        dma("pool", pw[:], pool_w[l].rearrange("g i o -> i g o"), (), [pw])
        dma("sp", psc[:], pscale[l], (), [psc])
        dma("sp", corr[:], c_corr, (), [corr])
        for g, win in enumerate((2, 4, 8, 16)):
            pin = P.sb([128, PP], F32, f"pin{g}")
            A_ = P.sb([128, PP], F32, f"pA{g}")
            B_ = P.sb([128, PP], F32, f"pB{g}")
            pm = P.sb([128, T], BF16, f"pm{g}")
            ms("pool", pin[:], 0.0, [pin])
            dma("sp", pin[:, xo:xo + LX], PLT[g, :, 0:LX], [("PLT", g)], [pin])
            dma("sp", pin[:, co:co + LCX], PLT[g, :, LX:T], [("PLT", g)], [pin])
            lo, hi = 8, PP - 8
            nn = hi - lo
            if win == 2:
                tt("dve", A_[:, lo:hi], pin[:, lo - 1:hi - 1], pin[:, lo:hi], ALU.add, [pin], [A_])
                S = A_
            else:
                tt("dve", A_[:, 0:PP - 1], pin[:, 0:PP - 1], pin[:, 1:PP], ALU.add, [pin], [A_])
                if win == 4:
                    tt("dve", B_[:, lo:hi], A_[:, lo - 2:hi - 2], A_[:, lo:hi], ALU.add, [A_], [B_])
                    S = B_
                else:
                    tt("dve", B_[:, 0:PP - 3], A_[:, 0:PP - 3], A_[:, 2:PP - 1], ALU.add, [A_], [B_])
                    if win == 8:
                        tt("dve", A_[:, lo:hi], B_[:, lo - 4:hi - 4], B_[:, lo:hi], ALU.add, [B_, A_], [A_])
                        S = A_
                    else:
                        tt("dve", A_[:, 0:PP - 7], B_[:, 0:PP - 7], B_[:, 4:PP - 3], ALU.add, [B_, A_], [A_])
                        tt("dve", B_[:, lo:hi], A_[:, lo - 8:hi - 8], A_[:, lo:hi], ALU.add, [A_, B_], [B_])
                        S = B_
            ts("dve", S[:, lo:hi], S[:, lo:hi], 1.0 / win, None, ALU.mult, None, [S], [S])
            for (o_, Ls) in ((xo, LX), (co, LCX)):
                tt("dve", S[:, o_:o_ + 8], S[:, o_:o_ + 8], corr[:, g, 0:8], ALU.mult, [S, corr], [S])
                tt("dve", S[:, o_ + Ls - 8:o_ + Ls], S[:, o_ + Ls - 8:o_ + Ls], corr[:, g, 8:16], ALU.mult, [S, corr], [S])
            tt("dve", pm[:, 0:LX], S[:, xo:xo + LX], pin[:, xo:xo + LX], ALU.subtract, [S, pin], [pm])
            tt("dve", pm[:, LX:T], S[:, co:co + LCX], pin[:, co:co + LCX], ALU.subtract, [S, pin], [pm])
            for bi, (t0, n) in enumerate(TB):
                pa = bb.gbank()
                mm(pa[:, :n], pw[:, g, :], pm[:, t0:t0 + n], True, True, [pw, pm], [pa])
                o_t = P.sb([128, 512], BF16, f"po{g}_{bi}")
                ts("dve", o_t[:, :n], pa[:, :n], psc[:, g:g + 1], None, ALU.mult, None, [pa, psc], [o_t])
                dma("sp", BRT[12 + g, :, t0:t0 + n], o_t[:, :n], [o_t], [("BRT", 12 + g)])
        P.barrier()
        P.sb_reset(m0)

    def merge_phase(l):
        m0 = P.sb_mark()
        brt = P.sb([128, 16, T], BF16, "brt")
        mgT = P.sb([128, 16, T], BF16, "mgT")
        wt = [P.sb([128, KC, 512], BF16, f"mwt{i}") for i in range(2)]
        g3 = [P.sb([128, 3, 512], BF16, f"g3{i}") for i in range(2)]
        t3 = [P.sb([128, 3, 512], F32, f"t3{i}") for i in range(2)]
        ev = [P.sb([128, 512], F32, f"mev{i}") for i in range(2)]
        for q4 in range(4):
            dma("sp", brt[:, q4 * 4:(q4 + 1) * 4, :], BRT[q4 * 4:(q4 + 1) * 4].rearrange("c p t -> p c t"),
                [("BRT", c) for c in range(q4 * 4, q4 * 4 + 4)], [(brt, q4)])
        GT4 = GT.rearrange("(b c) p t -> b c p t", b=3)
        it = 0
        for ng in range(4):
            slab = wt[ng % 2]
            dma("pool", slab[:, 0:8, :], wslab(w_att_o[l], 0, 8, ng * 512, 512), (), [(slab, 0)])
            dma("pool", slab[:, 8:12, :], wslab(w_hy_o[l], 0, 4, ng * 512, 512), (), [(slab, 1)])
            dma("pool", slab[:, 12:16, :], wslab(w_pool_o[l], 0, 4, ng * 512, 512), (), [(slab, 2)])
            for bi, (t0, n) in enumerate(TB):
                for c in range(4):
                    nch = ng * 4 + c
                    b = it % 2
                    it += 1
                    dma("sp", g3[b][:, :, :n], GT4[:, nch, :, t0:t0 + n].rearrange("b p t -> p b t"),
                        [("GT", br * 16 + nch) for br in range(3)], [g3[b]])
                    for br, (k0, k1) in enumerate(((0, 8), (8, 12), (12, 16))):
                        pa = bb.gbank()
                        q4s = [(brt, q) for q in ((0, 1) if br == 0 else (2,) if br == 1 else (3,))]
                        for kc in range(k0, k1):
                            mm(pa[:, :n], slab[:, kc, c * 128:(c + 1) * 128], brt[:, kc, t0:t0 + n], kc == k0, kc == k1 - 1,
                               [(slab, br)] + q4s, [pa])
                        tt("dve", t3[b][:, br, :n], pa[:, :n], g3[b][:, br, :n], ALU.mult, [pa, g3[b]], [(t3[b], br)])
                    tt("pool", t3[b][:, 0, :n], t3[b][:, 0, :n], t3[b][:, 1, :n], ALU.add, [(t3[b], 0), (t3[b], 1)],
                       [(t3[b], 0)])
                    tt("pool", mgT[:, nch, t0:t0 + n], t3[b][:, 0, :n], t3[b][:, 2, :n], ALU.add,
                       [(t3[b], 0), (t3[b], 2)], [(mgT, bi, nch)])
        P.barrier()

        def ld(c0):
            def f(slab):
                dma("pool", slab[:], wslab(w_out[l], 0, KC, c0, 512), (), [slab])
            return f
        st = {"i": 0}

        def evac(gi, c, bi, t0, n, pa, pb):
            i = st["i"] % 2
            st["i"] += 1
            cp("act" if i == 0 else "dve", ev[i][:, :n], pa[:, :n], [pa], [ev[i]])
            dma("sp", YT[gi * 4 + c, :, t0:t0 + n], ev[i][:, :n], [ev[i]], ["YT"])
        gemm_fm([ld(g * 512) for g in range(4)], KC, mgT, lambda bi: "nokey", TB, evac, wt)
        P.barrier()
        P.sb_reset(m0)

    def ffn_phase(l, h2T):
        m0 = P.sb_mark()
        cw = P.sb([128, FJ, 3], F32, "cw")
        dma("sp", cw[:], fcw[l], (), [cw])
        wg = [P.sb([128, KC, 256], BF16, f"wg{i}") for i in range(2)]
        wv = [P.sb([128, KC, 256], BF16, f"wv{i}") for i in range(2)]
        gp = [P.sb([128, PW], F32, f"gp{i}") for i in range(2)]
        vv = [P.sb([128, T], F32, f"vv{i}") for i in range(2)]
        cv = P.sb([128, PW], F32, "cv")
        x2 = P.sb([128, PW], F32, "x2")
        uu = P.sb([128, PW], F32, "uu")
        sg_ = P.sb([128, PW], F32, "sg")
        abf = [P.sb([128, T], BF16, f"abf{i}") for i in range(2)]
        for i in range(2):
            zero_pads("dve", gp[i], [gp[i]])
        n_ = PW - 2
        GC = 2.0 * math.sqrt(2.0 / math.pi)
        W = w_up[l]
        pending = []

        def chain(j, b):
            def c1():
                dwconv3("dve", cv, gp[b], cw[:, j, :], [gp[b], cw], [cv])

            def c2():
                act(x2[:, 1:1 + n_], cv[:, 1:1 + n_], AF.Square, [cv], [x2])
                ts("pool", x2[:, 1:1 + n_], x2[:, 1:1 + n_], 0.044715, 1.0, ALU.mult, ALU.add, [x2], [x2])
                tt("pool", uu[:, 1:1 + n_], x2[:, 1:1 + n_], cv[:, 1:1 + n_], ALU.mult, [x2, cv], [uu])

            def c3():
                act(sg_[:, 1:1 + n_], uu[:, 1:1 + n_], AF.Sigmoid, [uu], [sg_], scale=GC)
                tt("pool", uu[:, 1:1 + n_], cv[:, 1:1 + n_], sg_[:, 1:1 + n_], ALU.mult, [cv, sg_, uu], [uu])

            def c4():
                tt("dve", abf[b][:, 0:LX], uu[:, 1:1 + LX], vv[b][:, 0:LX], ALU.mult, [uu, vv[b]], [(abf[b], 0)])
                tt("dve", abf[b][:, LX:T], uu[:, LX + 3:LX + 3 + LCX], vv[b][:, LX:T], ALU.mult, [uu, vv[b]],
                   [(abf[b], 1)])
                dma("sp", AT[j], abf[b][:], [(abf[b], 0), (abf[b], 1)], ["AT"])
            return [c1, c2, c3, c4]

        def load_w(g2):
            sb_ = g2 % 2
            dma("pool", wg[sb_][:], wslab(W, 0, KC, g2 * 256, 256), (), [wg[sb_]])
            dma("pool", wv[sb_][:], wslab(W, 0, KC, FF + g2 * 256, 256), (), [wv[sb_]])

        load_w(0)
        for g2 in range(FJ // 2):
            sb_ = g2 % 2
            if g2 + 1 < FJ // 2:
                load_w(g2 + 1)
            for c in range(2):
                j = g2 * 2 + c
                b = j % 2
                for bi, (t0, n) in enumerate(TB):
                    pa = bb.gbank()
                    for kc in range(KC):
                        mm(pa[:, :n], wg[sb_][:, kc, c * 128:(c + 1) * 128], h2T[:, kc, t0:t0 + n], kc == 0, kc == KC - 1,
                           [wg[sb_]], [pa])
                    cp("act", gp[b][:, pidx(t0):pidx(t0) + n], pa[:, :n], [pa], [gp[b]])
                    pb = bb.gbank()
                    for kc in range(KC):
                        mm(pb[:, :n], wv[sb_][:, kc, c * 128:(c + 1) * 128], h2T[:, kc, t0:t0 + n], kc == 0, kc == KC - 1,
                           [wv[sb_]], [pb])
                    cp("dve", vv[b][:, t0:t0 + n], pb[:, :n], [pb], [vv[b]])
                    if pending:
                        pending.pop(0)()
                pending.extend(chain(j, b))
        for f_ in pending:
            f_()
        P.barrier()
        P.sb_reset(m0)

    def down_phase(l):
        m0 = P.sb_mark()
        HT = T // 2
        aT = P.sb([128, FJ, HT], BF16, "aT")
        wd = [P.sb([128, FJ, 256], BF16, f"wd{i}") for i in range(2)]
        ev = [P.sb([128, 512], F32, f"dev{i}") for i in range(2)]
        st = {"i": 0}
        for half in range(2):
            h0 = half * HT
            for q4 in range(4):
                dma("sp" if q4 % 2 == 0 else "act", aT[:, q4 * 11:(q4 + 1) * 11, :],
                    blk(AT, q4 * 11, (q4 + 1) * 11, h0, HT), ["AT"], [(aT, q4)])

            def ld(c0):
                def f(slab):
                    dma("pool", slab[:], w_down[l][:, c0:c0 + 256].rearrange("(j p) n -> p j n", p=128), (), [slab])
                return f

            def evac(gi, c, bi, t0, n, pa, pb, h0=h0):
                i = st["i"] % 2
                st["i"] += 1
                cp("act" if i == 0 else "dve", ev[i][:, :n], pa[:, :n], [pa], [ev[i]])
                dma("sp", YT[gi * 2 + c, :, h0 + t0:h0 + t0 + n], ev[i][:, :n], [ev[i]], ["YT"])
            gemm_fm([ld(g * 256) for g in range(8)], FJ, aT, None, [(0, 512), (512, 512), (1024, 128)], evac, wd,
                    ncs=2, akc=lambda kc: (aT, kc // 11))
            P.barrier()
        P.sb_reset(m0)

    ada_phase()
    for l in range(nlayers):
        for sg in segs:
            filt_phase(l, sg)
    mk = P.sb_mark()
    hxT = P.sb([128, KC, T], BF16, "hxT")
    norm_pass(0, 0, 1, hxT)
    for l in range(nlayers):
        proj_phase(l, hxT)
        P.sb_reset(mk)
        if stop_after == "proj":
            break
        attn_phase(l)
        hyena_phase(l)
        pool_phase(l)
        if stop_after == "branches":
            break
        merge_phase(l)
        mk = P.sb_mark()
        hxT = P.sb([128, KC, T], BF16, "h2T")
        resid_norm_pass(l, 2, l, 3, 4, hxT)
        if stop_after == "mixer":
            break
        ffn_phase(l, hxT)
        P.sb_reset(mk)
        down_phase(l)
        if l + 1 < nlayers:
            mk = P.sb_mark()
            hxT = P.sb([128, KC, T], BF16, "hxT")
            resid_norm_pass(l, 5, l + 1, 0, 1, hxT)
        else:
            resid_pass(l, 5)
    dma("sp", outT, XT[:, :, 0:LX], ["XT"], ["outT"])
    P.barrier()
    return P.emit(), P


def _bf16(a):
    return np.asarray(a, dtype=np.float32).astype(ml_dtypes.bfloat16)


def _dft_consts(L):
    N = 2 * L
    SC = L // 128
    ct = np.cos(2.0 * np.pi * np.arange(N) / N)
    st = np.sin(2.0 * np.pi * np.arange(N) / N)
    s = np.arange(L)
    idx = (s[:, None] * s[None, :]) % N
    Mc = ct[idx]
    Ms = st[idx]
    Fs = Ms.copy()
    Fs[:, 0] = (-1.0) ** s
    Gs = Ms.copy()
    Gs[0, :] = (-1.0) ** s
    def fw(M):
        return M.reshape(SC, 128, SC, 128).transpose(2, 1, 0, 3)
    FW = np.stack([fw(Mc), fw(Fs)], axis=0)
    TBI = 256
    def gv(M):
        return M.reshape(SC, 128, L // TBI, TBI).transpose(2, 1, 0, 3)
    GV = np.concatenate([gv(Mc), gv(Gs)], axis=2)
    ws = np.full((128, 3, SC), 2.0 / N, dtype=np.float32)
    ws[0, 0, 0] = 1.0 / N
    ws[0, 1, 0] = 0.0
    f32 = np.float32
    t = np.linspace(0.0, 1.0, L, dtype=f32)[:, None]
    w_ang = (f32(2.0 * math.pi) * np.arange(L, dtype=f32)[:, None] / f32(L)).astype(f32)
    fb = np.linspace(1e-4, 15, 16, dtype=f32)[None, :]
    ang = (fb * w_ang).astype(f32)
    z = np.concatenate([t, np.cos(ang), -np.sin(ang)], axis=-1).astype(f32)
    zT = np.ascontiguousarray(z.T)
    dmin = math.log(1e-2) / 1.5
    dmax = math.log(1e-2) / 0.3
    deltas = np.abs(np.linspace(dmin, dmax, 512, dtype=f32))
    decay = np.exp(-t * deltas[None, :]).astype(f32)
    decay = np.ascontiguousarray(decay.reshape(SC, 128, 512).transpose(1, 0, 2))
    return dict(FW=_bf16(np.ascontiguousarray(FW)), GV=_bf16(np.ascontiguousarray(GV)), ws=ws, zT=zT, decay=decay)


def _rope_consts():
    f32 = np.float32
    inv = (f32(10000.0) ** (-np.arange(16, dtype=f32) / f32(16))).astype(f32)
    tt_ = np.arange(LX)
    row = (tt_ // 64).astype(f32)
    col = (tt_ % 64).astype(f32)
    cosT = np.ones((128, T), dtype=f32)
    sinT = np.zeros((128, T), dtype=f32)
    for m in range(2):
        for a in range(2):
            pos = row if a == 0 else col
            ang = (pos[None, :] * inv[:, None]).astype(f32)
            for b in range(2):
                p0 = m * 64 + a * 32 + b * 16
                cosT[p0:p0 + 16, :LX] = np.cos(ang)
                sinT[p0:p0 + 16, :LX] = (-1.0 if b == 0 else 1.0) * np.sin(ang)
    return cosT, sinT


def _pool_corr():
    corr = np.ones((128, 4, 16), dtype=np.float32)
    for g, win in enumerate((2, 4, 8, 16)):
        h = win // 2
        for pos in range(8):
            cnt = min(pos, h) + h
            corr[:, g, pos] = win / cnt
            rem = 8 - pos
            cnt2 = h + min(h, rem)
            corr[:, g, 8 + pos] = win / cnt2
    return corr


_CACHE = {}


def _consts():
    if "c" not in _CACHE:
        cx = _dft_consts(LX)
        cc = _dft_consts(LCX)
        cosT, sinT = _rope_consts()
        m = {"c_ident": np.eye(128, dtype=np.float32), "c_cos": cosT, "c_sin": sinT, "c_corr": _pool_corr()}
        for nm, c in (("x", cx), ("c", cc)):
            m[f"c_fw_{nm}"] = c["FW"]
            m[f"c_gv_{nm}"] = c["GV"]
            m[f"c_zT_{nm}"] = c["zT"]
            m[f"c_decay_{nm}"] = c["decay"]
            m[f"c_ws_{nm}"] = c["ws"]
        _CACHE["c"] = m
    return _CACHE["c"]


def make_in_maps(inputs, ncore=NCORE):
    f = lambda a: np.ascontiguousarray(np.asarray(a, dtype=np.float32))
    I = {k: np.asarray(v) for k, v in inputs.items()}
    shared = dict(_consts())
    shared["w_ada"] = f(I["w_ada"])
    shared["b_ada"] = f(I["b_ada"])
    shared["gT"] = f(I["norm_g"].reshape(DEPTH, 4, KC, 128).transpose(0, 3, 1, 2))
    shared["w_in"] = f(I["w_in"])
    shared["diff_lam"] = f(I["diff_lam"].reshape(DEPTH, 256))
    shared["attn_subln_g"] = f(I["attn_subln_g"])
    shared["hsw"] = f(I["hy_short_w"].reshape(DEPTH, 3, 12, 128).transpose(0, 3, 2, 1))
    shared["hy_ffn_w1"] = f(I["hy_ffn_w1"])
    shared["hy_ffn_w2"] = f(I["hy_ffn_w2"])
    shared["hy_ffn_w3"] = f(I["hy_ffn_w3"])
    shared["hpb"] = f(np.stack([I["hy_ffn_b1"], I["hy_freq"][:, 0], I["hy_ffn_b2"], I["hy_freq"][:, 1]], axis=-1))
    shared["hbias"] = f(I["hy_bias"].reshape(DEPTH, 2, 4, 128).transpose(0, 3, 1, 2))
    shared["pool_w"] = f(I["pool_w"])
    shared["pscale"] = f(I["pool_scale"].reshape(DEPTH, 4, 128).transpose(0, 2, 1))
    shared["w_att_o"] = f(I["w_att_o"])
    shared["w_hy_o"] = f(I["w_hy_o"])
    shared["w_pool_o"] = f(I["w_pool_o"])
    shared["w_out"] = f(I["w_out"])
    shared["w_up"] = f(I["w_up"])
    shared["fcw"] = f(I["ff_conv_w"].reshape(DEPTH, 3, FJ, 128).transpose(0, 3, 2, 1))
    shared["w_down"] = f(I["w_down"])
    maps = []
    for b in range(ncore):
        X = np.concatenate([I["x"][b], I["ctx"][b]], axis=0)
        m = dict(shared)
        m["xT"] = f(X.T.reshape(KC, 128, T))
        cc = np.stack([I["c"][b], I["c_ctx"]], axis=0)
        m["ccT"] = f(cc.reshape(2, KC, 128).transpose(2, 1, 0))
        maps.append(m)
    return maps


def kernel(**inputs):
    if "nc" not in _CACHE:
        _CACHE["nc"] = build()[0]
    nc = _CACHE["nc"]
    maps = make_in_maps(inputs)
    res = run_bass_kernel_spmd(nc, maps, core_ids=list(range(NCORE)))
    outs = []
    for b in range(NCORE):
        oT = np.asarray(res.results[b]["outT"], dtype=np.float32)
        outs.append(np.ascontiguousarray(oT.reshape(D, LX).T))
    return np.stack(outs, axis=0)
```
